# Optimizing a Trainium2 kernel written in Bass

```python
import jax, jax.numpy as jnp
from jax import lax
import numpy as np

D_MODEL = 1024
BATCH = 8
SEQ = 2048
DEPTH = 4

GRID_W = 64
CTX_LEN = 256
N_MIXERS = 4
HEAD_DIM = 64
ROPE_THETA = 10000.0
NA_HEADS = 16
NA_WIN_H = 8
NA_WIN_W = 16
GQA_HEADS = 16
GQA_KV_HEADS = 4
SWA_HEADS = 16
SWA_KV_HEADS = 2
SWA_WINDOW = 128
Q_BLOCK = 128
LRU_WIDTH = D_MODEL
LRU_BLOCKS = 16
LRU_BLOCK_DIM = LRU_WIDTH // LRU_BLOCKS
CONV_WIDTH = 4
LRU_C = 8.0
FFN_HIDDEN = ((8 * D_MODEL + 2) // 3 + 255) // 256 * 256
N_MOD = 6
EPS = 1e-6
MASK_VALUE = -1e30
N_NA_LAYERS = (DEPTH + 3) // N_MIXERS
N_GQA_LAYERS = (DEPTH + 2) // N_MIXERS
N_SWA_LAYERS = (DEPTH + 1) // N_MIXERS
N_LRU_LAYERS = DEPTH // N_MIXERS

kernel_name = 'hybrid_dit_interleaved_na_gqa_swa_rglru'


def rms_norm(x, g):
    xf = x.astype(jnp.float32)
    y = xf * lax.rsqrt(jnp.mean(xf * xf, axis=-1, keepdims=True) + EPS)
    return (y * g.astype(jnp.float32)).astype(x.dtype)


def ada_norm(x, g, shift, scale):
    return rms_norm(x, g) * (1.0 + scale) + shift


def axial_angles(n_tokens):
    t = jnp.arange(n_tokens, dtype=jnp.int32)
    row = (t // GRID_W).astype(jnp.float32)
    col = (t % GRID_W).astype(jnp.float32)
    half = HEAD_DIM // 2
    inv_freq = 1.0 / (ROPE_THETA ** (jnp.arange(0, half, 2, dtype=jnp.float32) / half))
    return row[:, None] * inv_freq, col[:, None] * inv_freq


def _rotate(x, ang):
    cos = jnp.cos(ang)[None, :, None, :].astype(x.dtype)
    sin = jnp.sin(ang)[None, :, None, :].astype(x.dtype)
    x1, x2 = jnp.split(x, 2, axis=-1)
    return jnp.concatenate([x1 * cos - x2 * sin, x1 * sin + x2 * cos], axis=-1)


def axial_rope(x, ang_row, ang_col):
    xr, xc = jnp.split(x, 2, axis=-1)
    return jnp.concatenate([_rotate(xr, ang_row), _rotate(xc, ang_col)], axis=-1)


def project_qkv(h, w_qkv, n_q, n_kv):
    b, t, _ = h.shape
    qkv = h @ w_qkv
    q, k, v = jnp.split(qkv, [n_q * HEAD_DIM, (n_q + n_kv) * HEAD_DIM], axis=-1)
    return (q.reshape(b, t, n_q, HEAD_DIM), k.reshape(b, t, n_kv, HEAD_DIM), v.reshape(b, t, n_kv, HEAD_DIM))


def attn_probs(s, sink=None):
    if sink is None:
        return jax.nn.softmax(s, axis=-1)
    sk = sink.astype(jnp.float32)[None, :, :, None, None]
    m = jnp.maximum(jnp.max(s, axis=-1, keepdims=True), sk)
    e = jnp.exp(s - m)
    return e / (jnp.sum(e, axis=-1, keepdims=True) + jnp.exp(sk - m))


def context_attention(q, k, v, sink=None):
    b, t, n_q, dh = q.shape
    n_kv = k.shape[2]
    qg = q.reshape(b, t, n_kv, n_q // n_kv, dh)
    s = jnp.einsum('bqhgd,bshd->bhgqs', qg, k).astype(jnp.float32) * dh ** -0.5
    p = attn_probs(s, sink).astype(v.dtype)
    o = jnp.einsum('bhgqs,bshd->bqhgd', p, v)
    return o.reshape(b, t, n_q * dh)


def neighbourhood_attention_mixer(h_lat, h_ctx, w_qkv, rpb, w_o, with_ctx_out):
    b, s_len, _ = h_lat.shape
    rows = s_len // GRID_W
    kh = min(NA_WIN_H, rows)
    scale = HEAD_DIM ** -0.5
    q, k, v = project_qkv(h_lat, w_qkv, NA_HEADS, NA_HEADS)
    qc, kc, vc = project_qkv(h_ctx, w_qkv, NA_HEADS, NA_HEADS)
    k_grid = k.reshape(b, rows, GRID_W, NA_HEADS, HEAD_DIM)
    v_grid = v.reshape(b, rows, GRID_W, NA_HEADS, HEAD_DIM)
    q_rows = jnp.moveaxis(q.reshape(b, rows, GRID_W, NA_HEADS, HEAD_DIM), 1, 0)
    qcol = jnp.arange(GRID_W, dtype=jnp.int32)
    kcol = jnp.arange(GRID_W, dtype=jnp.int32)
    col_start = jnp.clip(qcol - NA_WIN_W // 2, 0, GRID_W - NA_WIN_W)
    col_in = (kcol[None, :] >= col_start[:, None]) & (kcol[None, :] < col_start[:, None] + NA_WIN_W)
    dcol = jnp.clip(kcol[None, :] - qcol[:, None] + NA_WIN_W - 1, 0, 2 * NA_WIN_W - 2)
    n_nb = kh * GRID_W

    def row_block(args):
        r, q_r = args
        r0 = jnp.clip(r - kh // 2, 0, rows - kh)
        k_b = lax.dynamic_slice_in_dim(k_grid, r0, kh, axis=1)
        v_b = lax.dynamic_slice_in_dim(v_grid, r0, kh, axis=1).reshape(b, n_nb, NA_HEADS, HEAD_DIM)
        drow = r0 + jnp.arange(kh, dtype=jnp.int32) - r + NA_WIN_H - 1
        bias = rpb[:, drow[None, :, None], dcol[:, None, :]]
        s_nb = jnp.einsum('bqhd,biwhd->bhqiw', q_r, k_b).astype(jnp.float32) * scale + bias.astype(jnp.float32)[None]
        s_nb = jnp.where(col_in[None, None, :, None, :], s_nb, MASK_VALUE).reshape(b, NA_HEADS, GRID_W, n_nb)
        s_cx = jnp.einsum('bqhd,bshd->bhqs', q_r, kc).astype(jnp.float32) * scale
        p = jax.nn.softmax(jnp.concatenate([s_nb, s_cx], axis=-1), axis=-1).astype(v.dtype)
        return (jnp.einsum('bhqn,bnhd->bqhd', p[..., :n_nb], v_b)
                + jnp.einsum('bhqs,bshd->bqhd', p[..., n_nb:], vc))

    o = lax.map(row_block, (jnp.arange(rows, dtype=jnp.int32), q_rows))
    y_lat = jnp.moveaxis(o, 0, 1).reshape(b, s_len, NA_HEADS * HEAD_DIM) @ w_o
    y_ctx = context_attention(qc, kc, vc) @ w_o if with_ctx_out else None
    return y_lat, y_ctx


def qknorm_gqa_mixer(h_lat, h_ctx, w_qkv, q_gain, k_gain, w_o, ang_row, ang_col, with_ctx_out):
    b, s_len, _ = h_lat.shape
    g = GQA_HEADS // GQA_KV_HEADS
    scale = HEAD_DIM ** -0.5
    q, k, v = project_qkv(h_lat, w_qkv, GQA_HEADS, GQA_KV_HEADS)
    qc, kc, vc = project_qkv(h_ctx, w_qkv, GQA_HEADS, GQA_KV_HEADS)
    q = axial_rope(rms_norm(q, q_gain), ang_row, ang_col)
    k = axial_rope(rms_norm(k, k_gain), ang_row, ang_col)
    qc = rms_norm(qc, q_gain)
    kc = rms_norm(kc, k_gain)
    k_all = jnp.concatenate([kc, k], axis=1)
    v_all = jnp.concatenate([vc, v], axis=1)
    nb = s_len // Q_BLOCK
    q_blocks = jnp.moveaxis(q.reshape(b, nb, Q_BLOCK, GQA_KV_HEADS, g, HEAD_DIM), 1, 0)

    def block(q_b):
        s = jnp.einsum('bqhgd,bshd->bhgqs', q_b, k_all).astype(jnp.float32) * scale
        p = jax.nn.softmax(s, axis=-1).astype(v_all.dtype)
        return jnp.einsum('bhgqs,bshd->bqhgd', p, v_all)

    o = lax.map(block, q_blocks)
    y_lat = jnp.moveaxis(o, 0, 1).reshape(b, s_len, GQA_HEADS * HEAD_DIM) @ w_o
    y_ctx = context_attention(qc, kc, vc) @ w_o if with_ctx_out else None
    return y_lat, y_ctx


def sliding_window_mixer(h_lat, h_ctx, w_qkv, sinks, w_o, ang_row, ang_col, with_ctx_out):
    b, s_len, _ = h_lat.shape
    g = SWA_HEADS // SWA_KV_HEADS
    scale = HEAD_DIM ** -0.5
    q, k, v = project_qkv(h_lat, w_qkv, SWA_HEADS, SWA_KV_HEADS)
    qc, kc, vc = project_qkv(h_ctx, w_qkv, SWA_HEADS, SWA_KV_HEADS)
    q = axial_rope(q, ang_row, ang_col)
    k = axial_rope(k, ang_row, ang_col)
    sink = sinks.reshape(SWA_KV_HEADS, g)
    band = Q_BLOCK + 2 * SWA_WINDOW
    pad = ((0, 0), (SWA_WINDOW, SWA_WINDOW), (0, 0), (0, 0))
    k_pad = jnp.pad(k, pad)
    v_pad = jnp.pad(v, pad)
    nb = s_len // Q_BLOCK
    q_blocks = jnp.moveaxis(q.reshape(b, nb, Q_BLOCK, SWA_KV_HEADS, g, HEAD_DIM), 1, 0)

    def block(args):
        j, q_b = args
        start = j * Q_BLOCK
        k_b = lax.dynamic_slice_in_dim(k_pad, start, band, axis=1)
        v_b = lax.dynamic_slice_in_dim(v_pad, start, band, axis=1)
        qpos = start + jnp.arange(Q_BLOCK, dtype=jnp.int32)
        kpos = start - SWA_WINDOW + jnp.arange(band, dtype=jnp.int32)
        valid = ((jnp.abs(qpos[:, None] - kpos[None, :]) <= SWA_WINDOW)
                 & (kpos[None, :] >= 0) & (kpos[None, :] < s_len))
        s_loc = jnp.einsum('bqhgd,bshd->bhgqs', q_b, k_b).astype(jnp.float32) * scale
        s_loc = jnp.where(valid, s_loc, MASK_VALUE)
        s_cx = jnp.einsum('bqhgd,bshd->bhgqs', q_b, kc).astype(jnp.float32) * scale
        p = attn_probs(jnp.concatenate([s_loc, s_cx], axis=-1), sink).astype(v.dtype)
        return (jnp.einsum('bhgqs,bshd->bqhgd', p[..., :band], v_b)
                + jnp.einsum('bhgqs,bshd->bqhgd', p[..., band:], vc))

    o = lax.map(block, (jnp.arange(nb, dtype=jnp.int32), q_blocks))
    y_lat = jnp.moveaxis(o, 0, 1).reshape(b, s_len, SWA_HEADS * HEAD_DIM) @ w_o
    y_ctx = context_attention(qc, kc, vc, sink) @ w_o if with_ctx_out else None
    return y_lat, y_ctx


def centred_depthwise_conv(x, w, bias):
    t = x.shape[1]
    left = CONV_WIDTH // 2
    xp = jnp.pad(x, ((0, 0), (left, CONV_WIDTH - 1 - left), (0, 0)))
    return sum(xp[:, kk:kk + t] * w[kk] for kk in range(CONV_WIDTH)) + bias


def rglru_coeffs(x, w_a, b_a, w_x, b_x, lam):
    b, t, _ = x.shape
    xf = x.astype(jnp.float32)
    xb = xf.reshape(b, t, LRU_BLOCKS, LRU_BLOCK_DIM)
    r = jax.nn.sigmoid(jnp.einsum('btnd,nde->btne', xb, w_a.astype(jnp.float32)).reshape(b, t, LRU_WIDTH) + b_a.astype(jnp.float32))
    ig = jax.nn.sigmoid(jnp.einsum('btnd,nde->btne', xb, w_x.astype(jnp.float32)).reshape(b, t, LRU_WIDTH) + b_x.astype(jnp.float32))
    log_a = -LRU_C * r * jax.nn.softplus(-lam.astype(jnp.float32))
    return jnp.exp(log_a), jnp.sqrt(-jnp.expm1(2.0 * log_a)) * (ig * xf)


def linear_scan(a, bx, h0):
    def combine(left, right):
        a_l, b_l = left
        a_r, b_r = right
        return a_r * a_l, a_r * b_l + b_r
    a_cum, b_cum = lax.associative_scan(combine, (a, bx), axis=1)
    return a_cum * h0[:, None, :] + b_cum


def rglru_mixer(h_lat, h_ctx, w_in, conv_w, conv_b, w_a, b_a, w_x, b_x, lam, w_out, with_ctx_out):
    def branches(h):
        xr, gate = jnp.split(h @ w_in, 2, axis=-1)
        return centred_depthwise_conv(xr, conv_w, conv_b), jax.nn.gelu(gate, approximate=True)

    xc, gc = branches(h_ctx)
    xl, gl = branches(h_lat)
    hc_sum = jnp.zeros(xc.shape, jnp.float32)
    hl_sum = jnp.zeros(xl.shape, jnp.float32)
    for d in range(2):
        ac, bc = rglru_coeffs(xc, w_a[d], b_a[d], w_x[d], b_x[d], lam[d])
        al, bl = rglru_coeffs(xl, w_a[d], b_a[d], w_x[d], b_x[d], lam[d])
        if d == 1:
            ac, bc, al, bl = jnp.flip(ac, 1), jnp.flip(bc, 1), jnp.flip(al, 1), jnp.flip(bl, 1)
        hc = linear_scan(ac, bc, jnp.zeros_like(bc[:, 0]))
        hl = linear_scan(al, bl, hc[:, -1])
        if d == 1:
            hc, hl = jnp.flip(hc, 1), jnp.flip(hl, 1)
        hc_sum = hc_sum + hc
        hl_sum = hl_sum + hl
    y_lat = (hl_sum.astype(h_lat.dtype) * gl) @ w_out
    y_ctx = (hc_sum.astype(h_ctx.dtype) * gc) @ w_out if with_ctx_out else None
    return y_lat, y_ctx


def swiglu_ffn(h, w_in, w_out):
    a, g = jnp.split(h @ w_in, 2, axis=-1)
    return (jax.nn.silu(a) * g) @ w_out


def setup_inputs(seed: int = 0) -> dict:
    key = jax.random.key(seed)
    ks = jax.random.split(key, 32)

    def nrm(k, shape, scale):
        return jax.random.normal(k, shape, jnp.float32) * scale

    def gain(k, shape):
        return 1.0 + 0.02 * jax.random.normal(k, shape, jnp.float32)

    d = D_MODEL
    a0 = jnp.sqrt(jax.random.uniform(ks[26], (N_LRU_LAYERS, 2, LRU_WIDTH), jnp.float32, 0.81, 0.998))
    return {
        'x': nrm(ks[0], (BATCH, SEQ, d), 1.0),
        'c': nrm(ks[1], (BATCH, d), 1.0),
        'ctx': nrm(ks[2], (BATCH, CTX_LEN, d), 1.0),
        'c_ctx': nrm(ks[3], (d,), 1.0),
        'ada_w': nrm(ks[4], (DEPTH, d, N_MOD * d), 0.5 * d ** -0.5),
        'ada_b': nrm(ks[5], (DEPTH, N_MOD * d), 0.02),
        'norm_mix': gain(ks[6], (DEPTH, d)),
        'norm_ffn': gain(ks[7], (DEPTH, d)),
        'norm_final': gain(ks[8], (d,)),
        'ffn_w_in': nrm(ks[9], (DEPTH, d, 2 * FFN_HIDDEN), d ** -0.5),
        'ffn_w_out': nrm(ks[10], (DEPTH, FFN_HIDDEN, d), FFN_HIDDEN ** -0.5),
        'na_w_qkv': nrm(ks[11], (N_NA_LAYERS, d, 3 * NA_HEADS * HEAD_DIM), d ** -0.5),
        'na_rpb': nrm(ks[12], (N_NA_LAYERS, NA_HEADS, 2 * NA_WIN_H - 1, 2 * NA_WIN_W - 1), 0.1),
        'na_w_o': nrm(ks[13], (N_NA_LAYERS, NA_HEADS * HEAD_DIM, d), (NA_HEADS * HEAD_DIM) ** -0.5),
        'gqa_w_qkv': nrm(ks[14], (N_GQA_LAYERS, d, (GQA_HEADS + 2 * GQA_KV_HEADS) * HEAD_DIM), d ** -0.5),
        'gqa_q_gain': gain(ks[15], (N_GQA_LAYERS, HEAD_DIM)),
        'gqa_k_gain': gain(ks[16], (N_GQA_LAYERS, HEAD_DIM)),
        'gqa_w_o': nrm(ks[17], (N_GQA_LAYERS, GQA_HEADS * HEAD_DIM, d), (GQA_HEADS * HEAD_DIM) ** -0.5),
        'swa_w_qkv': nrm(ks[18], (N_SWA_LAYERS, d, (SWA_HEADS + 2 * SWA_KV_HEADS) * HEAD_DIM), d ** -0.5),
        'swa_sinks': nrm(ks[19], (N_SWA_LAYERS, SWA_HEADS), 0.5),
        'swa_w_o': nrm(ks[20], (N_SWA_LAYERS, SWA_HEADS * HEAD_DIM, d), (SWA_HEADS * HEAD_DIM) ** -0.5),
        'lru_w_in': nrm(ks[21], (N_LRU_LAYERS, d, 2 * LRU_WIDTH), d ** -0.5),
        'lru_conv_w': nrm(ks[22], (N_LRU_LAYERS, CONV_WIDTH, LRU_WIDTH), CONV_WIDTH ** -0.5),
        'lru_conv_b': nrm(ks[23], (N_LRU_LAYERS, LRU_WIDTH), 0.02),
        'lru_w_a': nrm(ks[24], (N_LRU_LAYERS, 2, LRU_BLOCKS, LRU_BLOCK_DIM, LRU_BLOCK_DIM), LRU_BLOCK_DIM ** -0.5),
        'lru_b_a': nrm(ks[25], (N_LRU_LAYERS, 2, LRU_WIDTH), 0.02),
        'lru_w_x': nrm(ks[27], (N_LRU_LAYERS, 2, LRU_BLOCKS, LRU_BLOCK_DIM, LRU_BLOCK_DIM), LRU_BLOCK_DIM ** -0.5),
        'lru_b_x': nrm(ks[28], (N_LRU_LAYERS, 2, LRU_WIDTH), 0.02),
        'lru_lam': jnp.log(a0) - jnp.log1p(-a0),
        'lru_w_out': nrm(ks[29], (N_LRU_LAYERS, LRU_WIDTH, d), LRU_WIDTH ** -0.5),
    }


def reference(x, c, ctx, c_ctx, ada_w, ada_b, norm_mix, norm_ffn, norm_final, ffn_w_in, ffn_w_out,
              na_w_qkv, na_rpb, na_w_o, gqa_w_qkv, gqa_q_gain, gqa_k_gain, gqa_w_o,
              swa_w_qkv, swa_sinks, swa_w_o, lru_w_in, lru_conv_w, lru_conv_b, lru_w_a, lru_b_a,
              lru_w_x, lru_b_x, lru_lam, lru_w_out):
    s_len = x.shape[1]
    ang_row, ang_col = axial_angles(s_len)
    silu_c = jax.nn.silu(c)
    silu_cc = jax.nn.silu(c_ctx)
    h = x
    hc = ctx
    for i in range(DEPTH):
        kind = i % N_MIXERS
        slot = i // N_MIXERS
        ctx_out = i < DEPTH - 1
        mod_l = (silu_c @ ada_w[i] + ada_b[i])[:, None, :]
        mod_c = (silu_cc @ ada_w[i] + ada_b[i])[None, None, :]
        sh1_l, sc1_l, g1_l, sh2_l, sc2_l, g2_l = jnp.split(mod_l, N_MOD, axis=-1)
        sh1_c, sc1_c, g1_c, sh2_c, sc2_c, g2_c = jnp.split(mod_c, N_MOD, axis=-1)
        a_l = ada_norm(h, norm_mix[i], sh1_l, sc1_l)
        a_c = ada_norm(hc, norm_mix[i], sh1_c, sc1_c)
        if kind == 0:
            y_l, y_c = neighbourhood_attention_mixer(a_l, a_c, na_w_qkv[slot], na_rpb[slot], na_w_o[slot], ctx_out)
        elif kind == 1:
            y_l, y_c = qknorm_gqa_mixer(a_l, a_c, gqa_w_qkv[slot], gqa_q_gain[slot], gqa_k_gain[slot],
                                        gqa_w_o[slot], ang_row, ang_col, ctx_out)
        elif kind == 2:
            y_l, y_c = sliding_window_mixer(a_l, a_c, swa_w_qkv[slot], swa_sinks[slot], swa_w_o[slot],
                                            ang_row, ang_col, ctx_out)
        else:
            y_l, y_c = rglru_mixer(a_l, a_c, lru_w_in[slot], lru_conv_w[slot], lru_conv_b[slot],
                                   lru_w_a[slot], lru_b_a[slot], lru_w_x[slot], lru_b_x[slot],
                                   lru_lam[slot], lru_w_out[slot], ctx_out)
        h = h + g1_l * y_l
        h = h + g2_l * swiglu_ffn(ada_norm(h, norm_ffn[i], sh2_l, sc2_l), ffn_w_in[i], ffn_w_out[i])
        if ctx_out:
            hc = hc + g1_c * y_c
            hc = hc + g2_c * swiglu_ffn(ada_norm(hc, norm_ffn[i], sh2_c, sc2_c), ffn_w_in[i], ffn_w_out[i])
    return rms_norm(h, norm_final)
```

```python
from contextlib import ExitStack
import numpy as np
import concourse.bass as bass
import concourse.mybir as mybir
from concourse.bass_utils import run_bass_kernel_spmd

F32 = mybir.dt.float32
BF16 = mybir.dt.bfloat16
AF = mybir.ActivationFunctionType
ALU = mybir.AluOpType

D = 1024
S_LAT = 2048
L_CTX = 256
T = S_LAT + L_CTX
DEPTH = 4
HID = 2816
NCH = D // 128
TT = [(0, 512), (512, 512), (1024, 512), (1536, 512), (2048, 256)]
EPS = 1e-6
GRID_W = 64
MASKV = -30000.0


class EngQ:
    def __init__(self, name, sem):
        self.name = name
        self.sem = sem
        self.cnt = 0
        self.ops = []
        self.seen = {}


class Sched:
    ENGS = ("pe", "act", "dve", "pool", "sp")

    def __init__(self, nc, stack, n_dma_slots=8):
        self.nc = nc
        self.sems = {}
        self.q = {}
        for e in self.ENGS:
            self.sems[("c", e)] = stack.enter_context(nc.semaphore("c_" + e))
            self.q[e] = EngQ(e, ("c", e))
        self.dma_n = {}
        self.n_dma_slots = n_dma_slots
        for e in ("sp", "pool"):
            self.dma_n[e] = 0
            for i in range(n_dma_slots):
                self.sems[("d", e, i)] = stack.enter_context(nc.semaphore(f"d_{e}{i}"))
        self.w = {}
        self.r = {}
        self.out_dma = []

    def _deps(self, reads, writes):
        d = {}

        def add(k, v):
            if v > d.get(k, 0):
                d[k] = v
        for t in reads:
            if t in self.w:
                add(*self.w[t])
        for t in writes:
            if t in self.w:
                add(*self.w[t])
            for k, v in self.r.get(t, {}).items():
                add(k, v)
        return d

    def _commit(self, reads, writes, cid):
        k, v = cid
        for t in writes:
            self.w[t] = cid
            self.r[t] = {}
        for t in reads:
            rr = self.r.setdefault(t, {})
            if v > rr.get(k, 0):
                rr[k] = v

    def _waits(self, q, deps):
        ws = []
        for k, v in deps.items():
            if q.seen.get(k, 0) < v:
                q.seen[k] = v
                ws.append((k, v))
        return ws

    @staticmethod
    def _is_psum(t):
        n = t[0] if isinstance(t, tuple) else t
        return isinstance(n, str) and n.startswith("ps")

    def op(self, eng, fn, reads=(), writes=()):
        writes = list(writes) + [t for t in reads if self._is_psum(t)]
        reads = [t for t in reads if not self._is_psum(t)]
        q = self.q[eng]
        deps = self._deps(reads, writes)
        if eng == "pe":
            deps.pop(q.sem, None)
        ws = self._waits(q, deps)
        q.cnt += 1
        cid = (q.sem, q.cnt)
        q.ops.append((ws, fn, q.sem, 1))
        self._commit(reads, writes, cid)
        return cid

    def dma(self, eng, out, in_, reads=(), writes=(), is_output=False):
        q = self.q[eng]
        n = self.dma_n[eng]
        self.dma_n[eng] = n + 1
        slot = n % self.n_dma_slots
        semk = ("d", eng, slot)
        prev = 16 * (n // self.n_dma_slots)
        deps = self._deps(reads, writes)
        if prev > 0 and deps.get(semk, 0) < prev:
            deps[semk] = prev
        ws = self._waits(q, deps)
        cid = (semk, prev + 16)

        def fn(e, out=out, in_=in_):
            return e.dma_start(out=out, in_=in_)
        q.ops.append((ws, fn, semk, 16))
        self._commit(reads, writes, cid)
        if is_output:
            self.out_dma.append(cid)
        return cid

    def finish(self, eng="sp"):
        q = self.q[eng]
        d = {}
        for k, v in self.out_dma:
            d[k] = max(d.get(k, 0), v)
        ws = self._waits(q, d)
        q.ops.append((ws, None, None, 0))

    def flush(self):
        sems = self.sems
        with self.nc.Block() as block:
            def run(q, e):
                for ws, fn, isem, ival in q.ops:
                    for k, v in ws:
                        e.wait_ge(sems[k], v)
                    if fn is None:
                        continue
                    ins = fn(e)
                    if isem is not None:
                        ins.then_inc(sems[isem], ival)
                q.ops = []

            @block.tensor
            def _(e):
                run(self.q["pe"], e)

            @block.scalar
            def _(e):
                run(self.q["act"], e)

            @block.vector
            def _(e):
                run(self.q["dve"], e)

            @block.gpsimd
            def _(e):
                run(self.q["pool"], e)

            @block.sync
            def _(e):
                run(self.q["sp"], e)


class Builder:
    def __init__(self, layers, final, skip_mixer=False):
        self.layers = layers
        self.final = final
        self.skip_mixer = skip_mixer
        self.debug = False
        self.nc = bass.Bass("TRN2", target_bir_lowering=False)
        self.dr = {}

    def I(self, name):
        if name not in self.dr:
            self.dr[name] = self.nc.dram_tensor(name, list(self.shapes[name]), F32, kind="ExternalInput").ap()
        return self.dr[name]

    def dout(self, name, shape, dt=F32):
        t = self.nc.dram_tensor(name, list(shape), dt, kind="ExternalOutput").ap()
        self.dr[name] = t
        return t

    def build(self):
        nc = self.nc
        dr = self.dr
        self.shapes = {
            "hin": [D, T], "cT": [128, NCH, 2], "ada_w": [DEPTH, D, 6 * D], "ada_bT": [128, DEPTH, 48],
            "nmT": [128, DEPTH, NCH], "nfT": [128, DEPTH, NCH], "nfinT": [128, NCH],
            "ffn_w_in": [DEPTH, D, 2 * HID], "ffn_w_out": [DEPTH, HID, D],
            "na_w_qkv": [D, 3 * D], "na_w_o": [D, D], "nab": [16, 64, 15, 64],
            "gqa_w_qkv": [D, 1536], "gqa_w_qkv_p": [D, 1536], "gqa_gains": [64, 4], "gqa_w_o": [D, D],
            "swa_w_qkv": [D, 1280], "swa_w_qkv_p": [D, 1280], "swa_sinks": [64, 16], "swa_w_o": [D, D],
            "swa_masks": [128, 2, 128], "rope_cs": [64, 2, T],
            "lru_w_in": [D, 2 * D], "lru_vecs": [128, 12, NCH], "lru_w_a": [2, 16, 64, 64],
            "lru_w_x": [2, 16, 64, 64], "lru_w_out": [D, D],
        }
        if self.final:
            self.dout("y", [D, S_LAT])
        else:
            self.dout("hout", [D, T])

        with ExitStack() as st:
            self.S = S = Sched(nc, st)
            sb = lambda name, shape, dt: st.enter_context(nc.sbuf_tensor(name, list(shape), dt))
            self.h = sb("h", [128, NCH, T], F32)
            self.a = sb("a", [128, NCH, T], BF16)
            self.ones = sb("ones", [128, 128], BF16)
            self.cT = sb("cTs", [128, NCH, 2], F32)
            self.scT = sb("scT", [128, NCH, 2], BF16)
            self.adab = sb("adab", [128, DEPTH, 48], F32)
            self.nm = sb("nm", [128, DEPTH, NCH], F32)
            self.nf = sb("nf", [128, DEPTH, NCH], F32)
            self.nfin = sb("nfin", [128, NCH], F32)
            self.modT = sb("modT", [128, 48, 2], F32)
            self.gs1 = sb("gs1", [128, NCH, 2], F32)
            self.gs2 = sb("gs2", [128, NCH, 2], F32)
            self.epsc = sb("epsc", [128, 1], F32)

            for c in range(NCH):
                S.dma("sp", self.h[:, c, :], self.I("hin")[c * 128:(c + 1) * 128, :],
                      writes=[("h", c, t) for t in range(5)])
            S.dma("sp", self.cT[:], self.I("cT")[:, :, :], writes=["cT"])
            S.dma("sp", self.adab[:], self.I("ada_bT")[:, :, :], writes=["adab"])
            S.dma("sp", self.nm[:], self.I("nmT")[:, :, :], writes=["nm"])
            S.dma("sp", self.nf[:], self.I("nfT")[:, :, :], writes=["nf"])
            S.dma("sp", self.nfin[:], self.I("nfinT")[:, :], writes=["nfin"])
            S.op("dve", lambda e: e.memset(self.ones[:], 1.0), writes=["ones"])
            S.op("dve", lambda e: e.memset(self.epsc[:], EPS), writes=["epsc"])
            S.op("act", lambda e: e.activation(out=self.scT[:], in_=self.cT[:], func=AF.Silu),
                 reads=["cT"], writes=["scT"])
            S.flush()

            for i in self.layers:
                last = (i == DEPTH - 1)
                self.mods_phase(i)
                kind = i % 4
                ntile = 4 if last else 5
                if self.skip_mixer:
                    pass
                elif kind == 0:
                    self.attn_phase(i, "na")
                elif kind == 1:
                    self.attn_phase(i, "gqa")
                elif kind == 2:
                    self.attn_phase(i, "swa")
                else:
                    self.lru_phase(i)
                self.ffn_phase(i, ntile)
            if self.final:
                self.final_phase()
            else:
                for c in range(NCH):
                    S.dma("sp", dr["hout"][c * 128:(c + 1) * 128, :], self.h[:, c, :],
                          reads=[("h", c, t) for t in range(5)], is_output=True)
            S.finish()
            S.flush()
        return nc

    def mods_phase(self, i):
        nc, S, dr = self.nc, self.S, self.dr
        with ExitStack() as st:
            ring = [st.enter_context(nc.sbuf_tensor(f"adaw{s}_{i}", [128, NCH, 256], BF16)) for s in range(2)]
            psM = st.enter_context(nc.psum_tensor(f"psM_{i}", [128, 96], F32))
            for jb in range(24):
                slot = jb % 2
                S.dma("pool", ring[slot][:],
                      self.I("ada_w")[i, :, jb * 256:(jb + 1) * 256].rearrange("(k p) c -> p k c", p=128),
                      writes=[("adaw", slot)])

                def mm(e, jb=jb, slot=slot):
                    last = None
                    for jj in range(2):
                        j = jb * 2 + jj
                        for k in range(NCH):
                            last = e.matmul(psM[:, j * 2:(j + 1) * 2], ring[slot][:, k, jj * 128:(jj + 1) * 128],
                                            self.scT[:, k, :], start=(k == 0), stop=(k == NCH - 1))
                    return last
                S.op("pe", mm, reads=[("adaw", slot), "scT"], writes=["psM"])
            psv = psM[:].rearrange("p (j g) -> p j g", g=2)
            for g in range(2):
                S.op("dve", lambda e, g=g: e.tensor_tensor(self.modT[:, :, g], psv[:, :, g], self.adab[:, i, :], ALU.add),
                     reads=["psM", "adab"], writes=["modT"])
            for g in range(2):
                S.op("dve", lambda e, g=g: e.scalar_tensor_tensor(self.gs1[:, :, g], self.modT[:, 8:16, g], 1.0,
                                                                  self.nm[:, i, :], ALU.add, ALU.mult),
                     reads=["modT", "nm"], writes=["gs1"])
                S.op("dve", lambda e, g=g: e.scalar_tensor_tensor(self.gs2[:, :, g], self.modT[:, 32:40, g], 1.0,
                                                                  self.nf[:, i, :], ALU.add, ALU.mult),
                     reads=["modT", "nf"], writes=["gs2"])
            S.flush()

    def norm(self, st, gs, sh_base, ntile, tag):
        nc, S = self.nc, self.S
        sq = [st.enter_context(nc.sbuf_tensor(f"sq{s}_{tag}", [128, NCH, 512], BF16)) for s in range(2)]
        tmp = [st.enter_context(nc.sbuf_tensor(f"ntmp{s}_{tag}", [128, NCH, 512], F32)) for s in range(1)]
        rstd = [st.enter_context(nc.sbuf_tensor(f"rstd{s}_{tag}", [128, 512], F32)) for s in range(2)]
        psN = [st.enter_context(nc.psum_tensor(f"psN{s}_{tag}", [128, 512], F32)) for s in range(2)]
        for t in range(ntile):
            t0, n = TT[t]
            g = 0 if t < 4 else 1
            s2 = t % 2
            hs = [("h", c, t) for c in range(NCH)]
            S.op("act", lambda e, t0=t0, n=n, s2=s2: e.activation(out=sq[s2][:, :, 0:n], in_=self.h[:, :, t0:t0 + n], func=AF.Square),
                 reads=hs, writes=[("sq", s2)])

            def mm(e, n=n, s2=s2):
                last = None
                for k in range(NCH):
                    last = e.matmul(psN[s2][:, 0:n], self.ones[:], sq[s2][:, k, 0:n], start=(k == 0), stop=(k == NCH - 1))
                return last
            S.op("pe", mm, reads=[("sq", s2), "ones"], writes=[("psN", s2)])
            S.op("act", lambda e, n=n, s2=s2: e.activation(out=rstd[s2][:, 0:n], in_=psN[s2][:, 0:n], func=AF.Sqrt,
                                                           bias=self.epsc[:, 0:1], scale=1.0 / D),
                 reads=[("psN", s2), "epsc"], writes=[("rstd", s2)])
            S.op("dve", lambda e, n=n, s2=s2: e.reciprocal(rstd[s2][:, 0:n], rstd[s2][:, 0:n]),
                 reads=[("rstd", s2)], writes=[("rstd", s2)])
            for c in range(NCH):
                S.op("dve", lambda e, c=c, t0=t0, n=n, s2=s2, g=g: e.scalar_tensor_tensor(
                    tmp[0][:, c, 0:n], self.h[:, c, t0:t0 + n], gs[:, c, g:g + 1], rstd[s2][:, 0:n], ALU.mult, ALU.mult),
                    reads=[("h", c, t), ("rstd", s2), "gs1", "gs2"], writes=[("ntmp", c)])
                S.op("act", lambda e, c=c, t0=t0, n=n, g=g: e.activation(
                    out=self.a[:, c, t0:t0 + n], in_=tmp[0][:, c, 0:n], func=AF.Identity,
                    bias=self.modT[:, sh_base + c, g:g + 1], scale=1.0),
                    reads=[("ntmp", c), "modT"], writes=[("a", c, t)])

    def ffn_phase(self, i, ntile):
        nc, S, dr = self.nc, self.S, self.dr
        GC = 2
        NG = HID // 128 // GC
        with ExitStack() as st0:
            self.norm(st0, self.gs2, 24, ntile, f"f{i}")
            S.flush()
        with ExitStack() as st:
            wa = [st.enter_context(nc.sbuf_tensor(f"wa{s}_{i}", [128, NCH, GC * 128], BF16)) for s in range(2)]
            wg = [st.enter_context(nc.sbuf_tensor(f"wg{s}_{i}", [128, NCH, GC * 128], BF16)) for s in range(2)]
            wo = [st.enter_context(nc.sbuf_tensor(f"wo{s}_{i}", [128, GC, D], BF16)) for s in range(2)]
            ug = [st.enter_context(nc.sbuf_tensor(f"ug{s}_{i}", [128, GC, T], BF16)) for s in range(2)]
            su = [st.enter_context(nc.sbuf_tensor(f"su{s}_{i}", [128, 512], F32)) for s in range(2)]
            psA = [st.enter_context(nc.psum_tensor(f"psA{s}_{i}", [128, 512], F32)) for s in range(2)]
            psG = [st.enter_context(nc.psum_tensor(f"psG{s}_{i}", [128, 512], F32)) for s in range(2)]
            psY = [st.enter_context(nc.psum_tensor(f"psY{s}_{i}", [128, 512], F32)) for s in range(2)]
            win = self.I("ffn_w_in")
            wout = self.I("ffn_w_out")
            cnt = 0
            ycnt = 0
            for G in range(NG):
                s = G % 2
                c0 = G * GC * 128
                S.dma("pool", wa[s][:], win[i, :, c0:c0 + GC * 128].rearrange("(k p) c -> p k c", p=128), writes=[("wa", s)])
                S.dma("pool", wg[s][:], win[i, :, HID + c0:HID + c0 + GC * 128].rearrange("(k p) c -> p k c", p=128), writes=[("wg", s)])
                S.dma("pool", wo[s][:], wout[i, c0:c0 + GC * 128, :].rearrange("(j p) o -> p j o", p=128), writes=[("wo", s)])
                for t in range(ntile):
                    t0, n = TT[t]
                    for j in range(GC):
                        ps = cnt % 2
                        cnt += 1

                        def mm(e, s=s, j=j, t0=t0, n=n, ps=ps):
                            last = None
                            for k in range(NCH):
                                e.matmul(psA[ps][:, 0:n], wa[s][:, k, j * 128:(j + 1) * 128], self.a[:, k, t0:t0 + n],
                                         start=(k == 0), stop=(k == NCH - 1))
                            for k in range(NCH):
                                last = e.matmul(psG[ps][:, 0:n], wg[s][:, k, j * 128:(j + 1) * 128], self.a[:, k, t0:t0 + n],
                                                start=(k == 0), stop=(k == NCH - 1))
                            return last
                        S.op("pe", mm, reads=[("wa", s), ("wg", s)] + [("a", c, t) for c in range(NCH)],
                             writes=[("psA", ps), ("psG", ps)])
                        S.op("act", lambda e, ps=ps, n=n: e.activation(out=su[ps][:, 0:n], in_=psA[ps][:, 0:n], func=AF.Silu),
                             reads=[("psA", ps)], writes=[("su", ps)])
                        S.op("dve", lambda e, ps=ps, n=n, s=s, j=j, t0=t0: e.tensor_tensor(
                            ug[s][:, j, t0:t0 + n], psG[ps][:, 0:n], su[ps][:, 0:n], ALU.mult),
                            reads=[("psG", ps), ("su", ps)], writes=[("ug", s, t)])
                for t in range(ntile):
                    t0, n = TT[t]
                    g = 0 if t < 4 else 1
                    for o in range(NCH):
                        py = ycnt % 2
                        ycnt += 1

                        def mm2(e, s=s, o=o, t0=t0, n=n, py=py):
                            last = None
                            for j in range(GC):
                                last = e.matmul(psY[py][:, 0:n], wo[s][:, j, o * 128:(o + 1) * 128], ug[s][:, j, t0:t0 + n],
                                                start=(j == 0), stop=(j == GC - 1))
                            return last
                        S.op("pe", mm2, reads=[("wo", s), ("ug", s, t)], writes=[("psY", py)])
                        S.op("dve", lambda e, o=o, t0=t0, n=n, py=py, g=g: e.scalar_tensor_tensor(
                            self.h[:, o, t0:t0 + n], psY[py][:, 0:n], self.modT[:, 40 + o, g:g + 1], self.h[:, o, t0:t0 + n],
                            ALU.mult, ALU.add),
                            reads=[("psY", py), ("h", o, t), "modT"], writes=[("h", o, t)])
            S.flush()

    def final_phase(self):
        nc, S, dr = self.nc, self.S, self.dr
        with ExitStack() as st:
            sq = [st.enter_context(nc.sbuf_tensor(f"fsq{s}", [128, NCH, 512], BF16)) for s in range(2)]
            rstd = [st.enter_context(nc.sbuf_tensor(f"frstd{s}", [128, 512], F32)) for s in range(2)]
            psN = [st.enter_context(nc.psum_tensor(f"fpsN{s}", [128, 512], F32)) for s in range(2)]
            for t in range(4):
                t0, n = TT[t]
                s2 = t % 2
                hs = [("h", c, t) for c in range(NCH)]
                S.op("act", lambda e, t0=t0, n=n, s2=s2: e.activation(out=sq[s2][:, :, 0:n], in_=self.h[:, :, t0:t0 + n], func=AF.Square),
                     reads=hs, writes=[("sq", s2)])

                def mm(e, n=n, s2=s2):
                    last = None
                    for k in range(NCH):
                        last = e.matmul(psN[s2][:, 0:n], self.ones[:], sq[s2][:, k, 0:n], start=(k == 0), stop=(k == NCH - 1))
                    return last
                S.op("pe", mm, reads=[("sq", s2), "ones"], writes=[("psN", s2)])
                S.op("act", lambda e, n=n, s2=s2: e.activation(out=rstd[s2][:, 0:n], in_=psN[s2][:, 0:n], func=AF.Sqrt,
                                                               bias=self.epsc[:, 0:1], scale=1.0 / D),
                     reads=[("psN", s2), "epsc"], writes=[("rstd", s2)])
                S.op("dve", lambda e, n=n, s2=s2: e.reciprocal(rstd[s2][:, 0:n], rstd[s2][:, 0:n]),
                     reads=[("rstd", s2)], writes=[("rstd", s2)])
                for c in range(NCH):
                    S.op("dve", lambda e, c=c, t0=t0, n=n, s2=s2: e.scalar_tensor_tensor(
                        self.h[:, c, t0:t0 + n], self.h[:, c, t0:t0 + n], self.nfin[:, c:c + 1], rstd[s2][:, 0:n], ALU.mult, ALU.mult),
                        reads=[("h", c, t), ("rstd", s2), "nfin"], writes=[("h", c, t)])
            for c in range(NCH):
                S.dma("sp", dr["y"][c * 128:(c + 1) * 128, :], self.h[:, c, 0:S_LAT],
                      reads=[("h", c, t) for t in range(4)], is_output=True)

    def attn_phase(self, i, kind):
        nc, S = self.nc, self.S
        with ExitStack() as st0:
            self.norm(st0, self.gs1, 0, 5, f"m{i}")
            S.flush()
        rope = kind in ("gqa", "swa")
        if kind == "na":
            GH, NKV, NG = 2, 2, 8
            wqkv, wo_d = self.I("na_w_qkv"), self.I("na_w_o")
            qcol = lambda g: g * 128
            kcol = lambda g: 1024 + g * 128
            vcol = lambda g: 2048 + g * 128
        elif kind == "gqa":
            GH, NKV, NG = 4, 1, 4
            wqkv, wqkvp, wo_d = self.I("gqa_w_qkv"), self.I("gqa_w_qkv_p"), self.I("gqa_w_o")
            qcol = lambda g: g * 256
            kcol = lambda g: 1024 + g * 64
            vcol = lambda g: 1280 + g * 64
        else:
            GH, NKV, NG = 4, 1, 4
            wqkv, wqkvp, wo_d = self.I("swa_w_qkv"), self.I("swa_w_qkv_p"), self.I("swa_w_o")
            qcol = lambda g: g * 256
            kcol = lambda g: 1024 + (g // 2) * 64
            vcol = lambda g: 1152 + (g // 2) * 64
        with ExitStack() as st:
            sb = lambda name, shape, dt: st.enter_context(nc.sbuf_tensor(f"{name}_{i}", list(shape), dt))
            ps = lambda name: st.enter_context(nc.psum_tensor(f"{name}_{i}", [128, 512], F32))
            wq = sb("wq", [128, NCH, GH * 64], BF16)
            wk = sb("wk", [128, NCH, NKV * 64], BF16)
            wv = sb("wv", [128, NCH, NKV * 64], BF16)
            wo = sb("wo", [64, GH, D], BF16)
            if rope:
                wqp = sb("wqp", [128, NCH, GH * 64], BF16)
                wkp = sb("wkp", [128, NCH, NKV * 64], BF16)
                cs = sb("cs", [64, 2, T], F32)
                S.dma("sp", cs[:], self.I("rope_cs")[:, :, :], writes=["cs"])
                t1 = [sb(f"t1{x}", [64, 512], F32) for x in range(1)] * 2
                t2 = [sb(f"t2{x}", [64, 512], F32) for x in range(1)] * 2
            if kind == "gqa":
                gains = sb("gains", [64, 4], F32)
                S.dma("sp", gains[:], self.I("gqa_gains")[:, :], writes=["gains"])
                sqh = [sb(f"sqh{x}", [64, 512], BF16) for x in range(1)] * 2
                rs = [sb(f"rs{x}", [64, 512], F32) for x in range(1)] * 2
            if kind == "swa":
                sinkx = sb("sinkx", [64, 16], F32)
                S.dma("sp", sinkx[:], self.I("swa_sinks")[:, :], writes=["sinkx"])
                S.op("act", lambda e: e.activation(out=sinkx[:], in_=sinkx[:], func=AF.Exp), reads=["sinkx"], writes=["sinkx"])
                masks = sb("masks", [128, 2, 128], BF16)
                S.dma("pool", masks[:], self.I("swa_masks")[:, :, :], writes=["masks"])
            if kind == "na":
                stg = sb("stg", [128, 15, 64], F32)
                E2 = sb("E2", [128, 15, 64], BF16)
                Fm = sb("Fm", [128, GH, 25, 128], BF16)
            qg = sb("qg", [64, GH, T], BF16)
            kT = sb("kT", [64, NKV, T], BF16)
            V = sb("V", [128, 18, NKV * 64], BF16)
            Og = sb("Og", [64, GH, T], BF16)
            P = [sb(f"P{x}", [128, 512], BF16) for x in range(3)]
            rec = [sb(f"rec{x}", [64, 512], F32) for x in range(2)]
            psS = [ps(f"psS{x}") for x in range(2)]
            psO = [ps(f"psO{x}") for x in range(2)]
            psD = [ps(f"psD{x}") for x in range(2)]
            psP = [ps(f"psP{x}") for x in range(2)]
            cntP = [0]
            evtog = [0]

            def r0(qr):
                return min(max(qr - 4, 0), 24)

            def tw0(R):
                return min(max(R - 2, 0), 11)
            VAR_R = [0, 1, 2, 14, 15]

            def var_of(R):
                return {0: 0, 1: 1, 14: 3, 15: 4}.get(R, 2)

            def proj(wt, wtp, col0, dst, hh_dst, is_q, t):
                t0, n = TT[t]
                pp = cntP[0] % 2
                cntP[0] += 1
                areads = [("a", c, t) for c in range(NCH)]

                def mm(e, pp=pp, t0=t0, n=n):
                    last = None
                    for k in range(NCH):
                        last = e.matmul(psP[pp][0:64, 0:n], wt[:, k, col0:col0 + 64], self.a[:, k, t0:t0 + n],
                                        start=(k == 0), stop=(k == NCH - 1))
                    return last
                if not rope:
                    S.op("pe", mm, reads=areads + ["wq", "wk"], writes=[("psP", pp)])
                    eng = "act" if evtog[0] % 2 == 0 else "dve"
                    evtog[0] += 1
                    if eng == "act":
                        S.op("act", lambda e, pp=pp, t0=t0, n=n: e.activation(out=dst[:, hh_dst, t0:t0 + n], in_=psP[pp][0:64, 0:n], func=AF.Copy),
                             reads=[("psP", pp)], writes=[("qk", id(dst))])
                    else:
                        S.op("dve", lambda e, pp=pp, t0=t0, n=n: e.tensor_copy(dst[:, hh_dst, t0:t0 + n], psP[pp][0:64, 0:n]),
                             reads=[("psP", pp)], writes=[("qk", id(dst))])
                    return
                pq = cntP[0] % 2
                cntP[0] += 1
                S.op("pe", mm, reads=areads + ["wq", "wk"], writes=[("psP", pp)])

                def mm2(e, pq=pq, t0=t0, n=n):
                    last = None
                    for k in range(NCH):
                        last = e.matmul(psP[pq][0:64, 0:n], wtp[:, k, col0:col0 + 64], self.a[:, k, t0:t0 + n],
                                        start=(k == 0), stop=(k == NCH - 1))
                    return last
                S.op("pe", mm2, reads=areads + ["wqp", "wkp"], writes=[("psP", pq)])
                x = 0
                if kind == "gqa":
                    gc = 0 if is_q else 2
                    S.op("act", lambda e, pp=pp, n=n, x=x: e.activation(out=sqh[x][:, 0:n], in_=psP[pp][0:64, 0:n], func=AF.Square),
                         reads=[("psP", pp)], writes=[("sqh", x)])
                    S.op("dve", lambda e, pp=pp, n=n, x=x, t0=t0, gc=gc: e.scalar_tensor_tensor(
                        t1[x][:, 0:n], psP[pp][0:64, 0:n], gains[:, gc:gc + 1], cs[:, 0, t0:t0 + n], ALU.mult, ALU.mult),
                        reads=[("psP", pp), "gains", "cs"], writes=[("t1", x)])
                    S.op("dve", lambda e, pq=pq, n=n, x=x, t0=t0, gc=gc: e.scalar_tensor_tensor(
                        t2[x][:, 0:n], psP[pq][0:64, 0:n], gains[:, gc + 1:gc + 2], cs[:, 1, t0:t0 + n], ALU.mult, ALU.mult),
                        reads=[("psP", pq), "gains", "cs"], writes=[("t2", x)])
                    S.op("pe", lambda e, pp=pp, n=n, x=x: e.matmul(psP[pp][0:64, 0:n], self.ones[0:64, 0:64], sqh[x][:, 0:n], start=True, stop=True),
                         reads=[("sqh", x), "ones"], writes=[("psP", pp)])
                    S.op("act", lambda e, pp=pp, n=n, x=x: e.activation(out=rs[x][:, 0:n], in_=psP[pp][0:64, 0:n], func=AF.Sqrt,
                                                                       bias=self.epsc[0:64, 0:1], scale=1.0 / 64),
                         reads=[("psP", pp), "epsc"], writes=[("rs", x)])
                    S.op("dve", lambda e, n=n, x=x: e.reciprocal(rs[x][:, 0:n], rs[x][:, 0:n]), reads=[("rs", x)], writes=[("rs", x)])
                    S.op("dve", lambda e, n=n, x=x: e.tensor_tensor(t1[x][:, 0:n], t1[x][:, 0:n], t2[x][:, 0:n], ALU.add),
                         reads=[("t1", x), ("t2", x)], writes=[("t1", x)])
                    S.op("dve", lambda e, n=n, x=x, t0=t0: e.tensor_tensor(dst[:, hh_dst, t0:t0 + n], t1[x][:, 0:n], rs[x][:, 0:n], ALU.mult),
                         reads=[("t1", x), ("rs", x)], writes=[("qk", id(dst))])
                else:
                    S.op("dve", lambda e, pp=pp, n=n, x=x, t0=t0: e.tensor_tensor(t1[x][:, 0:n], psP[pp][0:64, 0:n], cs[:, 0, t0:t0 + n], ALU.mult),
                         reads=[("psP", pp), "cs"], writes=[("t1", x)])
                    S.op("dve", lambda e, pq=pq, n=n, x=x, t0=t0: e.tensor_tensor(t2[x][:, 0:n], psP[pq][0:64, 0:n], cs[:, 1, t0:t0 + n], ALU.mult),
                         reads=[("psP", pq), "cs"], writes=[("t2", x)])
                    S.op("dve", lambda e, n=n, x=x, t0=t0: e.tensor_tensor(dst[:, hh_dst, t0:t0 + n], t1[x][:, 0:n], t2[x][:, 0:n], ALU.add),
                         reads=[("t1", x), ("t2", x)], writes=[("qk", id(dst))])

            prev_kv = None
            for g in range(NG):
                wsrc = lambda col, n: wqkv[:, col:col + n].rearrange("(k p) c -> p k c", p=128)
                S.dma("pool", wq[:], wsrc(qcol(g), GH * 64), writes=["wq"])
                if rope:
                    S.dma("pool", wqp[:], wqkvp[:, qcol(g):qcol(g) + GH * 64].rearrange("(k p) c -> p k c", p=128), writes=["wqp"])
                new_kv = (kind != "swa") or (g % 2 == 0)
                if new_kv:
                    S.dma("pool", wk[:], wsrc(kcol(g), NKV * 64), writes=["wk"])
                    S.dma("pool", wv[:], wsrc(vcol(g), NKV * 64), writes=["wv"])
                    if rope:
                        S.dma("pool", wkp[:], wqkvp[:, kcol(g):kcol(g) + NKV * 64].rearrange("(k p) c -> p k c", p=128), writes=["wkp"])
                S.dma("pool", wo[:], wo_d[g * GH * 64:(g + 1) * GH * 64, :].rearrange("(hh d) o -> d hh o", d=64), writes=["wo"])
                if kind == "na":
                    for hh in range(GH):
                        hd = g * GH + hh
                        S.dma("sp", stg[0:64], self.I("nab")[hd], writes=["stg"])
                        S.dma("sp", stg[64:128], self.I("nab")[hd], writes=["stg"])
                        S.op("act", lambda e: e.activation(out=E2[:], in_=stg[:], func=AF.Exp), reads=["stg"], writes=["E2"])
                        S.op("pool", lambda e, hh=hh: e.memset(Fm[:, hh], 0.0), writes=[("F", hh)])
                        for v in range(5):
                            R = VAR_R[v]
                            for w in range(5):
                                tk = tw0(R) + w
                                for krp in range(2):
                                    for qrp in range(2):
                                        kr, qr = 2 * tk + krp, 2 * R + qrp
                                        if not (r0(qr) <= kr < r0(qr) + 8):
                                            continue
                                        dr_ = kr - qr + 7
                                        S.op("pool", lambda e, hh=hh, v=v, w=w, krp=krp, qrp=qrp, dr_=dr_: e.tensor_copy(
                                            Fm[krp * 64:(krp + 1) * 64, hh, v * 5 + w, qrp * 64:(qrp + 1) * 64],
                                            E2[krp * 64:(krp + 1) * 64, dr_, :]),
                                            reads=["E2"], writes=[("F", hh)])
                for hh in range(GH):
                    for t in range(5):
                        proj(wq, wqp if rope else None, hh * 64, qg, hh, True, t)
                if new_kv:
                    for u in range(NKV):
                        for t in range(5):
                            proj(wk, wkp if rope else None, u * 64, kT, u, False, t)
                    for tk in range(18):
                        pp = cntP[0] % 2
                        cntP[0] += 1
                        tt_ = min(tk // 4, 4)

                        def mmv(e, tk=tk, pp=pp):
                            last = None
                            for k in range(NCH):
                                last = e.matmul(psP[pp][:, 0:NKV * 64], self.a[:, k, tk * 128:(tk + 1) * 128], wv[:, k, :],
                                                start=(k == 0), stop=(k == NCH - 1))
                            return last
                        S.op("pe", mmv, reads=[("a", c, tt_) for c in range(NCH)] + ["wv"], writes=[("psP", pp)])
                        S.op("act", lambda e, tk=tk, pp=pp: e.activation(out=V[:, tk, :], in_=psP[pp][:, 0:NKV * 64], func=AF.Copy),
                             reads=[("psP", pp)], writes=["V"])
                units = [(u, [u]) for u in range(GH)] if kind == "na" else [(0, list(range(GH)))]
                def run_unit(u, heads, g):
                    nh = len(heads)
                    N = nh * 128
                    h0 = heads[0]
                    steps = []
                    for qb in range(18):
                        if qb >= 16:
                            kl = [(16, None), (17, None)]
                        elif kind == "na":
                            kl = [(tw0(qb) + w, ("F", var_of(qb) * 5 + w)) for w in range(5)] + [(16, None), (17, None)]
                        elif kind == "gqa":
                            kl = [(tk, None) for tk in range(18)]
                        else:
                            kl = []
                            if qb >= 1:
                                kl.append((qb - 1, ("M", 0)))
                            kl.append((qb, None))
                            if qb <= 14:
                                kl.append((qb + 1, ("M", 1)))
                            kl += [(16, None), (17, None)]
                        for idx, (tk, fm) in enumerate(kl):
                            steps.append((qb, tk, fm, idx == 0, idx == len(kl) - 1))

                    def qk(sidx):
                        qb, tk, fm, first, last = steps[sidx]
                        r = sidx % 2
                        S.op("pe", lambda e, tk=tk, qb=qb, r=r: e.matmul(
                            psS[r][:, 0:N], kT[:, u, tk * 128:(tk + 1) * 128], qg[:, h0:h0 + nh, qb * 128:(qb + 1) * 128],
                            start=True, stop=True),
                            reads=[("qk", id(kT)), ("qk", id(qg))], writes=[("psS", r)])

                    def rest(sidx):
                        qb, tk, fm, first, last = steps[sidx]
                        r = sidx % 2
                        pr = sidx % 3
                        ob = qb % 2
                        S.op("act", lambda e, r=r, pr=pr: e.activation(out=P[pr][:, 0:N], in_=psS[r][:, 0:N], func=AF.Exp, scale=0.125),
                             reads=[("psS", r)], writes=[("P", pr)])
                        if fm is not None and fm[0] == "F":
                            S.op("dve", lambda e, pr=pr, fi=fm[1]: e.tensor_tensor(P[pr][:, 0:128], P[pr][:, 0:128], Fm[:, u, fi, :], ALU.mult),
                                 reads=[("P", pr), ("F", u)], writes=[("P", pr)])
                        elif fm is not None:
                            def mk(e, pr=pr, m=fm[1]):
                                last_ = None
                                for x in range(nh):
                                    last_ = e.tensor_tensor(P[pr][:, x * 128:(x + 1) * 128], P[pr][:, x * 128:(x + 1) * 128], masks[:, m, :], ALU.mult)
                                return last_
                            S.op("pool", mk, reads=[("P", pr), "masks"], writes=[("P", pr)])

                        def pv(e, tk=tk, pr=pr, ob=ob, first=first, last=last):
                            e.matmul(psO[ob][0:64, 0:N], V[:, tk, u * 64:(u + 1) * 64], P[pr][:, 0:N], start=first, stop=last)
                            return e.matmul(psD[ob][0:64, 0:N], self.ones[:, 0:64], P[pr][:, 0:N], start=first, stop=last)
                        S.op("pe", pv, reads=[("P", pr), "V", "ones"], writes=[("psO", ob)])
                        if last:
                            if kind == "swa":
                                def addsink(e, ob=ob):
                                    last_ = None
                                    for x in range(nh):
                                        hd = g * GH + heads[x]
                                        last_ = e.tensor_scalar(rec[ob][:, x * 128:(x + 1) * 128], psD[ob][0:64, x * 128:(x + 1) * 128],
                                                                sinkx[:, hd:hd + 1], None, ALU.add)
                                    return last_
                                S.op("dve", addsink, reads=[("psO", ob), "sinkx"], writes=[("rec", ob)])
                                S.op("dve", lambda e, ob=ob: e.reciprocal(rec[ob][:, 0:N], rec[ob][:, 0:N]),
                                     reads=[("rec", ob)], writes=[("rec", ob)])
                            else:
                                S.op("dve", lambda e, ob=ob: e.reciprocal(rec[ob][:, 0:N], psD[ob][0:64, 0:N]),
                                     reads=[("psO", ob)], writes=[("rec", ob)])
                            S.op("dve", lambda e, ob=ob, qb=qb: e.tensor_tensor(
                                Og[:, h0:h0 + nh, qb * 128:(qb + 1) * 128], psO[ob][0:64, 0:N].rearrange("p (a b) -> p a b", a=nh),
                                rec[ob][:, 0:N].rearrange("p (a b) -> p a b", a=nh), ALU.mult),
                                reads=[("psO", ob), ("rec", ob)], writes=["Og"])
                    qk(0)
                    for sidx in range(len(steps)):
                        if sidx + 1 < len(steps):
                            qk(sidx + 1)
                        rest(sidx)
                for (u_, heads_) in units:
                    run_unit(u_, heads_, g)
                if self.debug and g == 0:
                    for nm_, tl_, shp in (("dbg_qg", qg, [64, GH, T]), ("dbg_kT", kT, [64, NKV, T]), ("dbg_V", V, [128, 18, NKV * 64]), ("dbg_Og", Og, [64, GH, T])):
                        d_ = self.nc.dram_tensor(nm_, shp, BF16, kind="ExternalOutput").ap()
                        self.dr[nm_] = d_
                        S.dma("sp", d_[:, :, :], tl_[:], reads=[("qk", id(tl_)), "V", "Og"], is_output=True)
                for t in range(5):
                    t0, n = TT[t]
                    gi = 0 if t < 4 else 1
                    for o in range(NCH):
                        pp = cntP[0] % 2
                        cntP[0] += 1

                        def mmo(e, o=o, t0=t0, n=n, pp=pp):
                            last = None
                            for hh in range(GH):
                                last = e.matmul(psP[pp][:, 0:n], wo[:, hh, o * 128:(o + 1) * 128], Og[:, hh, t0:t0 + n],
                                                start=(hh == 0), stop=(hh == GH - 1))
                            return last
                        S.op("pe", mmo, reads=["wo", "Og"], writes=[("psP", pp)])
                        S.op("dve", lambda e, o=o, t0=t0, n=n, pp=pp, gi=gi: e.scalar_tensor_tensor(
                            self.h[:, o, t0:t0 + n], psP[pp][:, 0:n], self.modT[:, 16 + o, gi:gi + 1], self.h[:, o, t0:t0 + n],
                            ALU.mult, ALU.add),
                            reads=[("psP", pp), ("h", o, t), "modT"], writes=[("h", o, t)])
            S.flush()

    def lru_phase(self, i):
        nc, S = self.nc, self.S
        with ExitStack() as st0:
            self.norm(st0, self.gs1, 0, 5, f"m{i}")
            S.flush()
        with ExitStack() as stz:
            z = stz.enter_context(nc.sbuf_tensor(f"z_{i}", [128, NCH, S_LAT], BF16))
            with ExitStack() as st:
                sb = lambda name, shape, dt: st.enter_context(nc.sbuf_tensor(f"{name}_{i}", list(shape), dt))
                ps = lambda name: st.enter_context(nc.psum_tensor(f"{name}_{i}", [128, 512], F32))
                vec = sb("lvec", [128, 12, NCH], F32)
                nsp = sb("nsp", [128, 2, NCH], F32)
                win = [sb(f"lwin{x}", [128, NCH, 256], BF16) for x in range(1)] * 2
                Wbd = sb("Wbd", [128, 2, 2, 128], BF16)
                A = sb("A", [128, T], F32)
                xc = sb("xc", [128, T], F32)
                xb = sb("xb", [128, T], BF16)
                gl = sb("gl", [128, S_LAT], BF16)
                Bd = sb("Bd", [128, T], F32)
                Hf = sb("Hf", [128, T], F32)
                Hb = sb("Hb", [128, T], F32)
                tm = [sb(f"tm{x}", [128, 512], F32) for x in range(2)]
                psX = [ps(f"psX{x}") for x in range(2)]
                psR = [ps(f"psR{x}") for x in range(2)]
                psI = [ps(f"psI{x}") for x in range(2)]
                S.dma("sp", vec[:], self.I("lru_vecs")[:, :, :], writes=["lvec"])
                S.op("act", lambda e: e.activation(out=nsp[:], in_=vec[:, 9:11, :], func=AF.Exp, scale=-1.0), reads=["lvec"], writes=["nsp"])
                S.op("act", lambda e: e.activation(out=nsp[:], in_=nsp[:], func=AF.Ln, bias=1.0, scale=1.0), reads=["nsp"], writes=["nsp"])
                S.op("dve", lambda e: e.tensor_scalar(nsp[:], nsp[:], -8.0, None, ALU.mult), reads=["nsp"], writes=["nsp"])
                S.op("pool", lambda e: e.memset(Wbd[:], 0.0), writes=["Wbd"])
                w_in = self.I("lru_w_in")
                wa_d, wx_d = self.I("lru_w_a"), self.I("lru_w_x")
                cx = [0]
                for c in range(NCH):
                    wsl = 0
                    S.dma("pool", win[wsl][:, :, 0:128], w_in[:, c * 128:(c + 1) * 128].rearrange("(k p) c -> p k c", p=128), writes=[("lwin", wsl)])
                    S.dma("pool", win[wsl][:, :, 128:256], w_in[:, D + c * 128:D + (c + 1) * 128].rearrange("(k p) c -> p k c", p=128), writes=[("lwin", wsl)])
                    for d in range(2):
                        for gt, wd in enumerate((wa_d, wx_d)):
                            for half in range(2):
                                S.dma("pool", Wbd[half * 64:(half + 1) * 64, d, gt, half * 64:(half + 1) * 64], wd[d, 2 * c + half], writes=["Wbd"])
                    for t in range(5):
                        t0, n = TT[t]
                        px = cx[0] % 2
                        cx[0] += 1

                        def mm(e, t0=t0, n=n, px=px, wsl=wsl):
                            last = None
                            for k in range(NCH):
                                last = e.matmul(psX[px][:, 0:n], win[wsl][:, k, 0:128], self.a[:, k, t0:t0 + n], start=(k == 0), stop=(k == NCH - 1))
                            return last
                        S.op("pe", mm, reads=[("lwin", wsl)] + [("a", k, t) for k in range(NCH)], writes=[("psX", px)])
                        S.op("act", lambda e, t0=t0, n=n, px=px: e.activation(out=A[:, t0:t0 + n], in_=psX[px][:, 0:n], func=AF.Copy),
                             reads=[("psX", px)], writes=["A"])
                    for t in range(4):
                        t0, n = TT[t]
                        px = cx[0] % 2
                        cx[0] += 1

                        def mmg(e, t0=t0, n=n, px=px, wsl=wsl):
                            last = None
                            for k in range(NCH):
                                last = e.matmul(psX[px][:, 0:n], win[wsl][:, k, 128:256], self.a[:, k, t0:t0 + n], start=(k == 0), stop=(k == NCH - 1))
                            return last
                        S.op("pe", mmg, reads=[("lwin", wsl)] + [("a", k, t) for k in range(NCH)], writes=[("psX", px)])
                        S.op("act", lambda e, t0=t0, n=n, px=px: e.activation(out=gl[:, t0:t0 + n], in_=psX[px][:, 0:n], func=AF.Gelu_apprx_tanh),
                             reads=[("psX", px)], writes=["gl"])
                    S.op("act", lambda e, c=c: e.activation(out=xc[:], in_=A[:], func=AF.Identity, bias=vec[:, 4, c:c + 1], scale=vec[:, 2, c:c + 1]),
                         reads=["A", "lvec"], writes=["xc"])
                    for (kk, sh) in ((0, 2), (1, 1), (3, -1)):
                        for (lo, hi) in ((0, S_LAT), (S_LAT, T)):
                            if sh > 0:
                                o0, o1, i0, i1 = lo + sh, hi, lo, hi - sh
                            else:
                                o0, o1, i0, i1 = lo, hi + sh, lo - sh, hi
                            S.op("dve", lambda e, c=c, kk=kk, o0=o0, o1=o1, i0=i0, i1=i1: e.scalar_tensor_tensor(
                                xc[:, o0:o1], A[:, i0:i1], vec[:, kk, c:c + 1], xc[:, o0:o1], ALU.mult, ALU.add),
                                reads=["A", "lvec", "xc"], writes=["xc"])
                    S.op("act", lambda e: e.activation(out=xb[:], in_=xc[:], func=AF.Copy), reads=["xc"], writes=["xb"])
                    for d in range(2):
                        Hd = Hf if d == 0 else Hb
                        for t in range(5):
                            t0, n = TT[t]
                            px = cx[0] % 2
                            cx[0] += 1
                            S.op("pe", lambda e, d=d, t0=t0, n=n, px=px: e.matmul(psR[px][:, 0:n], Wbd[:, d, 0, :], xb[:, t0:t0 + n], start=True, stop=True),
                                 reads=["Wbd", "xb"], writes=[("psR", px)])
                            S.op("pe", lambda e, d=d, t0=t0, n=n, px=px: e.matmul(psI[px][:, 0:n], Wbd[:, d, 1, :], xb[:, t0:t0 + n], start=True, stop=True),
                                 reads=["Wbd", "xb"], writes=[("psI", px)])
                            S.op("act", lambda e, d=d, c=c, t0=t0, n=n, px=px: e.activation(out=A[:, t0:t0 + n], in_=psR[px][:, 0:n], func=AF.Sigmoid,
                                                                                         bias=vec[:, 5 + d, c:c + 1], scale=1.0),
                                 reads=[("psR", px), "lvec", "xb"], writes=[("A", t)])
                            S.op("act", lambda e, d=d, c=c, t0=t0, n=n: e.activation(out=A[:, t0:t0 + n], in_=A[:, t0:t0 + n], func=AF.Exp,
                                                                                scale=nsp[:, d, c:c + 1]),
                                 reads=[("A", t), "nsp"], writes=[("A", t)])
                            S.op("act", lambda e, d=d, c=c, t0=t0, n=n, px=px: e.activation(out=Bd[:, t0:t0 + n], in_=psI[px][:, 0:n], func=AF.Sigmoid,
                                                                                         bias=vec[:, 7 + d, c:c + 1], scale=1.0),
                                 reads=[("psI", px), "lvec", "Hscan"], writes=[("Bd", t)])
                            S.op("dve", lambda e, t0=t0, n=n: e.tensor_tensor(Bd[:, t0:t0 + n], Bd[:, t0:t0 + n], xc[:, t0:t0 + n], ALU.mult),
                                 reads=[("Bd", t), "xc"], writes=[("Bd", t)])
                            S.op("pool", lambda e, t0=t0, n=n, px=px: e.tensor_tensor(tm[px][:, 0:n], A[:, t0:t0 + n], A[:, t0:t0 + n], ALU.mult),
                                 reads=[("A", t)], writes=[("tm", px)])
                            S.op("act", lambda e, n=n, px=px: e.activation(out=tm[px][:, 0:n], in_=tm[px][:, 0:n], func=AF.Sqrt, bias=1.0, scale=-1.0),
                                 reads=[("tm", px)], writes=[("tm", px)])
                            S.op("dve", lambda e, t0=t0, n=n, px=px: e.tensor_tensor(Bd[:, t0:t0 + n], Bd[:, t0:t0 + n], tm[px][:, 0:n], ALU.mult),
                                 reads=[("Bd", t), ("tm", px)], writes=[("Bd", t)])
                        allA = [("A", t) for t in range(5)]
                        allB = [("Bd", t) for t in range(5)]
                        if d == 0:
                            S.op("dve", lambda e: e.tensor_tensor_scan(Hf[:, S_LAT:T], A[:, S_LAT:T], Bd[:, S_LAT:T], 0.0, ALU.mult, ALU.add),
                                 reads=allA + allB, writes=["Hf"])
                            S.op("dve", lambda e: e.tensor_tensor_scan(Hf[:, 0:S_LAT], A[:, 0:S_LAT], Bd[:, 0:S_LAT], Hf[:, T - 1:T], ALU.mult, ALU.add),
                                 reads=allA + allB + ["Hf"], writes=["Hf", "Hscan"])
                        else:
                            S.op("dve", lambda e: e.tensor_tensor_scan(Hb[:, S_LAT:T][:, ::-1], A[:, S_LAT:T][:, ::-1], Bd[:, S_LAT:T][:, ::-1], 0.0, ALU.mult, ALU.add),
                                 reads=allA + allB, writes=["Hb"])
                            S.op("dve", lambda e: e.tensor_tensor_scan(Hb[:, 0:S_LAT][:, ::-1], A[:, 0:S_LAT][:, ::-1], Bd[:, 0:S_LAT][:, ::-1], Hb[:, S_LAT:S_LAT + 1], ALU.mult, ALU.add),
                                 reads=allA + allB + ["Hb"], writes=["Hb", "Hscan"])
                        S.op("dve", lambda e: e.engine_nop() if False else e.memset(tm[0][:, 0:1], 0.0), reads=["Hscan"], writes=allA + allB + [("tm", 0), "A"])
                    S.op("dve", lambda e: e.tensor_tensor(Hf[:, 0:S_LAT], Hf[:, 0:S_LAT], Hb[:, 0:S_LAT], ALU.add), reads=["Hf", "Hb"], writes=["Hf"])
                    S.op("dve", lambda e, c=c: e.tensor_tensor(z[:, c, :], Hf[:, 0:S_LAT], gl[:, :], ALU.mult), reads=["Hf", "gl"], writes=[("z", c)])
                S.flush()
            with ExitStack() as st:
                wo_t = st.enter_context(nc.sbuf_tensor(f"lwo_{i}", [128, NCH, D], BF16))
                psP = [st.enter_context(nc.psum_tensor(f"lpsP{x}_{i}", [128, 512], F32)) for x in range(2)]
                S.dma("pool", wo_t[:], self.I("lru_w_out").rearrange("(k p) o -> p k o", p=128), writes=["lwo"])
                cp = 0
                for t in range(4):
                    t0, n = TT[t]
                    for o in range(NCH):
                        pp = cp % 2
                        cp += 1

                        def mmo(e, o=o, t0=t0, n=n, pp=pp):
                            last = None
                            for k in range(NCH):
                                last = e.matmul(psP[pp][:, 0:n], wo_t[:, k, o * 128:(o + 1) * 128], z[:, k, t0:t0 + n], start=(k == 0), stop=(k == NCH - 1))
                            return last
                        S.op("pe", mmo, reads=["lwo"] + [("z", k) for k in range(NCH)], writes=[("psP", pp)])
                        S.op("dve", lambda e, o=o, t0=t0, n=n, pp=pp: e.scalar_tensor_tensor(
                            self.h[:, o, t0:t0 + n], psP[pp][:, 0:n], self.modT[:, 16 + o, 0:1], self.h[:, o, t0:t0 + n], ALU.mult, ALU.add),
                            reads=[("psP", pp), ("h", o, t), "modT"], writes=[("h", o, t)])
                S.flush()

def _fm(v):
    v = np.asarray(v, np.float32)
    lead = v.shape[:-1]
    r = v.reshape(lead + (NCH, 128))
    return np.ascontiguousarray(np.moveaxis(r, -1, 0))


def prep_shared(inp):
    sh = {}
    sh["ada_w"] = np.ascontiguousarray(inp["ada_w"], np.float32)
    sh["ada_bT"] = _fm(inp["ada_b"].reshape(DEPTH, 48, 128).reshape(DEPTH, 48 * 128)) if False else \
        np.ascontiguousarray(np.moveaxis(np.asarray(inp["ada_b"], np.float32).reshape(DEPTH, 48, 128), -1, 0))
    sh["nmT"] = _fm(inp["norm_mix"])
    sh["nfT"] = _fm(inp["norm_ffn"])
    sh["nfinT"] = _fm(inp["norm_final"])
    sh["ffn_w_in"] = np.ascontiguousarray(inp["ffn_w_in"], np.float32)
    sh["ffn_w_out"] = np.ascontiguousarray(inp["ffn_w_out"], np.float32)
    sh["na_w_qkv"] = np.ascontiguousarray(inp["na_w_qkv"][0], np.float32)
    sh["na_w_o"] = np.ascontiguousarray(inp["na_w_o"][0], np.float32)
    sh["nab"] = na_bias_table(inp["na_rpb"][0])
    d = np.arange(64)
    partner = np.where((d // 16) % 2 == 0, d + 16, d - 16)
    for nm_, nq, nkv in (("gqa", 16, 4), ("swa", 16, 2)):
        w = np.ascontiguousarray(inp[nm_ + "_w_qkv"][0], np.float32)
        perm = np.arange(w.shape[1])
        for hd in range(nq + nkv):
            perm[hd * 64:(hd + 1) * 64] = hd * 64 + partner
        sh[nm_ + "_w_qkv"] = w
        sh[nm_ + "_w_qkv_p"] = np.ascontiguousarray(w[:, perm])
        sh[nm_ + "_w_o"] = np.ascontiguousarray(inp[nm_ + "_w_o"][0], np.float32)
    qg_, kg_ = np.asarray(inp["gqa_q_gain"][0], np.float32), np.asarray(inp["gqa_k_gain"][0], np.float32)
    sh["gqa_gains"] = np.ascontiguousarray(np.stack([qg_, qg_[partner], kg_, kg_[partner]], axis=1))
    sh["swa_sinks"] = np.ascontiguousarray(np.broadcast_to(np.asarray(inp["swa_sinks"][0], np.float32)[None, :], (64, 16)))
    t = np.arange(S_LAT)
    row = (t // GRID_W).astype(np.float32)
    col = (t % GRID_W).astype(np.float32)
    inv_freq = (1.0 / (np.float32(10000.0) ** (np.arange(0, 32, 2, dtype=np.float32) / np.float32(32)))).astype(np.float32)
    ang_row = row[:, None] * inv_freq
    ang_col = col[:, None] * inv_freq
    cs = np.zeros((64, 2, T), np.float32)
    cs[:, 0, :] = 1.0
    for dd in range(64):
        seg, f = dd // 16, dd % 16
        ang = ang_row[:, f] if seg < 2 else ang_col[:, f]
        cs[dd, 0, :S_LAT] = np.cos(ang)
        cs[dd, 1, :S_LAT] = np.sin(ang) * (-1.0 if seg % 2 == 0 else 1.0)
    sh["rope_cs"] = cs
    kk = np.arange(128)[:, None]
    qq = np.arange(128)[None, :]
    vecs = np.zeros((12, D), np.float32)
    vecs[0:4] = inp["lru_conv_w"][0]
    vecs[4] = inp["lru_conv_b"][0]
    vecs[5:7] = inp["lru_b_a"][0]
    vecs[7:9] = inp["lru_b_x"][0]
    vecs[9:11] = inp["lru_lam"][0]
    sh["lru_vecs"] = _fm(vecs)
    sh["lru_w_in"] = np.ascontiguousarray(inp["lru_w_in"][0], np.float32)
    sh["lru_w_a"] = np.ascontiguousarray(inp["lru_w_a"][0], np.float32)
    sh["lru_w_x"] = np.ascontiguousarray(inp["lru_w_x"][0], np.float32)
    sh["lru_w_out"] = np.ascontiguousarray(inp["lru_w_out"][0], np.float32)
    sh["swa_masks"] = np.ascontiguousarray(np.stack([(qq <= kk), (kk <= qq)], axis=1).astype(np.float32))
    return sh


def prep_core(inp, b):
    c = {}
    x = np.asarray(inp["x"][b], np.float32)
    ctx = np.asarray(inp["ctx"][b], np.float32)
    c["hin"] = np.ascontiguousarray(np.concatenate([x, ctx], axis=0).T)
    cT = np.stack([_fm(inp["c"][b]), _fm(inp["c_ctx"])], axis=-1)
    c["cT"] = np.ascontiguousarray(cT)
    return c


def na_bias_table(rpb):
    rpb = np.asarray(rpb, np.float32)
    kc = np.arange(64)[:, None]
    qc = np.arange(64)[None, :]
    dcol = np.clip(kc - qc + 15, 0, 30)
    cstart = np.clip(qc - 8, 0, 48)
    col_in = (kc >= cstart) & (kc < cstart + 16)
    g = rpb[:, :, dcol]
    g = np.where(col_in[None, None], g, np.float32(MASKV))
    return np.ascontiguousarray(np.transpose(g, (0, 2, 1, 3)).astype(np.float32))


_CACHE = {}


def kernel(**inputs):
    inputs = {k: np.asarray(v) for k, v in inputs.items()}
    n_cores = 8
    if "nc" not in _CACHE:
        B = Builder([0, 1, 2, 3], final=True)
        _CACHE["nc"] = (B, B.build())
    B, nc = _CACHE["nc"]
    sh = prep_shared(inputs)
    in_maps = []
    for b in range(n_cores):
        core = prep_core(inputs, b)
        in_maps.append({name: (core[name] if name in core else sh[name]) for name in B.dr if name != "y"})
    res = run_bass_kernel_spmd(nc, in_maps, core_ids=list(range(n_cores)))
    out = np.stack([np.asarray(res.results[b]["y"]).T for b in range(n_cores)], axis=0)
    return np.ascontiguousarray(out.astype(np.float32))
```

```python
from contextlib import ExitStack
import numpy as np
import concourse.bass as bass
import concourse.mybir as mybir
from concourse.bass_utils import run_bass_kernel_spmd

F32 = mybir.dt.float32
BF16 = mybir.dt.bfloat16
AF = mybir.ActivationFunctionType
ALU = mybir.AluOpType

D = 1024
S_LAT = 2048
L_CTX = 256
T = S_LAT + L_CTX
DEPTH = 4
HID = 2816
NCH = D // 128
TT = [(0, 512), (512, 512), (1024, 512), (1536, 512), (2048, 256)]
EPS = 1e-6
GRID_W = 64
MASKV = -30000.0


class EngQ:
    def __init__(self, name, sem):
        self.name = name
        self.sem = sem
        self.cnt = 0
        self.ops = []
        self.seen = {}


class Sched:
    ENGS = ("pe", "act", "dve", "pool", "sp")

    def __init__(self, nc, stack, n_dma_slots=8):
        self.nc = nc
        self.sems = {}
        self.q = {}
        for e in self.ENGS:
            self.sems[("c", e)] = stack.enter_context(nc.semaphore("c_" + e))
            self.q[e] = EngQ(e, ("c", e))
        self.dma_n = {}
        self.n_dma_slots = n_dma_slots
        for e in ("sp", "pool"):
            self.dma_n[e] = 0
            for i in range(n_dma_slots):
                self.sems[("d", e, i)] = stack.enter_context(nc.semaphore(f"d_{e}{i}"))
        self.w = {}
        self.r = {}
        self.out_dma = []

    def _deps(self, reads, writes):
        d = {}

        def add(k, v):
            if v > d.get(k, 0):
                d[k] = v
        for t in reads:
            if t in self.w:
                add(*self.w[t])
        for t in writes:
            if t in self.w:
                add(*self.w[t])
            for k, v in self.r.get(t, {}).items():
                add(k, v)
        return d

    def _commit(self, reads, writes, cid):
        k, v = cid
        for t in writes:
            self.w[t] = cid
            self.r[t] = {}
        for t in reads:
            rr = self.r.setdefault(t, {})
            if v > rr.get(k, 0):
                rr[k] = v

    def _waits(self, q, deps):
        ws = []
        for k, v in deps.items():
            if q.seen.get(k, 0) < v:
                q.seen[k] = v
                ws.append((k, v))
        return ws

    @staticmethod
    def _is_psum(t):
        n = t[0] if isinstance(t, tuple) else t
        return isinstance(n, str) and n.startswith("ps")

    def op(self, eng, fn, reads=(), writes=()):
        writes = list(writes) + [t for t in reads if self._is_psum(t)]
        reads = [t for t in reads if not self._is_psum(t)]
        q = self.q[eng]
        deps = self._deps(reads, writes)
        if eng == "pe":
            deps.pop(q.sem, None)
        ws = self._waits(q, deps)
        q.cnt += 1
        cid = (q.sem, q.cnt)
        q.ops.append((ws, fn, q.sem, 1))
        self._commit(reads, writes, cid)
        return cid

    def dma(self, eng, out, in_, reads=(), writes=(), is_output=False):
        q = self.q[eng]
        n = self.dma_n[eng]
        self.dma_n[eng] = n + 1
        slot = n % self.n_dma_slots
        semk = ("d", eng, slot)
        prev = 16 * (n // self.n_dma_slots)
        deps = self._deps(reads, writes)
        if prev > 0 and deps.get(semk, 0) < prev:
            deps[semk] = prev
        ws = self._waits(q, deps)
        cid = (semk, prev + 16)

        def fn(e, out=out, in_=in_):
            return e.dma_start(out=out, in_=in_)
        q.ops.append((ws, fn, semk, 16))
        self._commit(reads, writes, cid)
        if is_output:
            self.out_dma.append(cid)
        return cid

    def finish(self, eng="sp"):
        q = self.q[eng]
        d = {}
        for k, v in self.out_dma:
            d[k] = max(d.get(k, 0), v)
        ws = self._waits(q, d)
        q.ops.append((ws, None, None, 0))

    def flush(self):
        sems = self.sems
        with self.nc.Block() as block:
            def run(q, e):
                for ws, fn, isem, ival in q.ops:
                    for k, v in ws:
                        e.wait_ge(sems[k], v)
                    if fn is None:
                        continue
                    ins = fn(e)
                    if isem is not None:
                        ins.then_inc(sems[isem], ival)
                q.ops = []

            @block.tensor
            def _(e):
                run(self.q["pe"], e)

            @block.scalar
            def _(e):
                run(self.q["act"], e)

            @block.vector
            def _(e):
                run(self.q["dve"], e)

            @block.gpsimd
            def _(e):
                run(self.q["pool"], e)

            @block.sync
            def _(e):
                run(self.q["sp"], e)


class Builder:
    def __init__(self, layers, final, skip_mixer=False):
        self.layers = layers
        self.final = final
        self.skip_mixer = skip_mixer
        self.debug = False
        self.nc = bass.Bass("TRN2", target_bir_lowering=False)
        self.dr = {}

    def I(self, name):
        if name not in self.dr:
            self.dr[name] = self.nc.dram_tensor(name, list(self.shapes[name]), F32, kind="ExternalInput").ap()
        return self.dr[name]

    def dout(self, name, shape, dt=F32):
        t = self.nc.dram_tensor(name, list(shape), dt, kind="ExternalOutput").ap()
        self.dr[name] = t
        return t

    def build(self):
        nc = self.nc
        dr = self.dr
        self.shapes = {
            "hin": [D, T], "cT": [128, NCH, 2], "ada_w": [DEPTH, D, 6 * D], "ada_bT": [128, DEPTH, 48],
            "nmT": [128, DEPTH, NCH], "nfT": [128, DEPTH, NCH], "nfinT": [128, NCH],
            "ffn_w_in": [DEPTH, D, 2 * HID], "ffn_w_out": [DEPTH, HID, D],
            "na_w_qkv": [D, 3 * D], "na_w_o": [D, D], "nab": [16, 64, 15, 64],
            "gqa_w_qkv": [D, 1536], "gqa_w_qkv_p": [D, 1536], "gqa_gains": [128, 4], "gqa_w_o": [D, D],
            "swa_w_qkv": [D, 1280], "swa_w_qkv_p": [D, 1280], "swa_sinks": [128, 8], "swa_w_o": [D, D],
            "swa_masks": [128, 2, 128], "rope_cs": [128, 2, T],
            "lru_w_in": [D, 2 * D], "lru_vecs": [128, 12, NCH], "lru_w_a": [2, 16, 64, 64],
            "lru_w_x": [2, 16, 64, 64], "lru_w_out": [D, D],
        }
        if self.final:
            self.dout("y", [D, S_LAT])
        else:
            self.dout("hout", [D, T])

        with ExitStack() as st:
            self.S = S = Sched(nc, st)
            sb = lambda name, shape, dt: st.enter_context(nc.sbuf_tensor(name, list(shape), dt))
            self.h = sb("h", [128, NCH, T], F32)
            self.a = sb("a", [128, NCH, T], BF16)
            self.ones = sb("ones", [128, 128], BF16)
            self.cT = sb("cTs", [128, NCH, 2], F32)
            self.scT = sb("scT", [128, NCH, 2], BF16)
            self.adab = sb("adab", [128, DEPTH, 48], F32)
            self.nm = sb("nm", [128, DEPTH, NCH], F32)
            self.nf = sb("nf", [128, DEPTH, NCH], F32)
            self.nfin = sb("nfin", [128, NCH], F32)
            self.modT = sb("modT", [128, 48, 2], F32)
            self.gs1 = sb("gs1", [128, NCH, 2], F32)
            self.gs2 = sb("gs2", [128, NCH, 2], F32)
            self.epsc = sb("epsc", [128, 1], F32)

            for c in range(NCH):
                S.dma("sp", self.h[:, c, :], self.I("hin")[c * 128:(c + 1) * 128, :],
                      writes=[("h", c, t) for t in range(5)])
            S.dma("sp", self.cT[:], self.I("cT")[:, :, :], writes=["cT"])
            S.dma("sp", self.adab[:], self.I("ada_bT")[:, :, :], writes=["adab"])
            S.dma("sp", self.nm[:], self.I("nmT")[:, :, :], writes=["nm"])
            S.dma("sp", self.nf[:], self.I("nfT")[:, :, :], writes=["nf"])
            S.dma("sp", self.nfin[:], self.I("nfinT")[:, :], writes=["nfin"])
            S.op("dve", lambda e: e.memset(self.ones[:], 1.0), writes=["ones"])
            S.op("dve", lambda e: e.memset(self.epsc[:], EPS), writes=["epsc"])
            S.op("act", lambda e: e.activation(out=self.scT[:], in_=self.cT[:], func=AF.Silu),
                 reads=["cT"], writes=["scT"])
            S.flush()

            for i in self.layers:
                last = (i == DEPTH - 1)
                self.mods_phase(i)
                kind = i % 4
                ntile = 4 if last else 5
                if self.skip_mixer:
                    pass
                elif kind == 0:
                    self.attn_phase(i, "na")
                elif kind == 1:
                    self.attn_phase(i, "gqa")
                elif kind == 2:
                    self.attn_phase(i, "swa")
                else:
                    self.lru_phase(i)
                self.ffn_phase(i, ntile)
            if self.final:
                self.final_phase()
            else:
                for c in range(NCH):
                    S.dma("sp", dr["hout"][c * 128:(c + 1) * 128, :], self.h[:, c, :],
                          reads=[("h", c, t) for t in range(5)], is_output=True)
            S.finish()
            S.flush()
        return nc

    def mods_phase(self, i):
        nc, S, dr = self.nc, self.S, self.dr
        with ExitStack() as st:
            ring = [st.enter_context(nc.sbuf_tensor(f"adaw{s}_{i}", [128, NCH, 256], BF16)) for s in range(2)]
            psM = st.enter_context(nc.psum_tensor(f"psM_{i}", [128, 96], F32))
            for jb in range(24):
                slot = jb % 2
                S.dma("pool", ring[slot][:],
                      self.I("ada_w")[i, :, jb * 256:(jb + 1) * 256].rearrange("(k p) c -> p k c", p=128),
                      writes=[("adaw", slot)])

                def mm(e, jb=jb, slot=slot):
                    last = None
                    for jj in range(2):
                        j = jb * 2 + jj
                        for k in range(NCH):
                            last = e.matmul(psM[:, j * 2:(j + 1) * 2], ring[slot][:, k, jj * 128:(jj + 1) * 128],
                                            self.scT[:, k, :], start=(k == 0), stop=(k == NCH - 1))
                    return last
                S.op("pe", mm, reads=[("adaw", slot), "scT"], writes=["psM"])
            psv = psM[:].rearrange("p (j g) -> p j g", g=2)
            for g in range(2):
                S.op("dve", lambda e, g=g: e.tensor_tensor(self.modT[:, :, g], psv[:, :, g], self.adab[:, i, :], ALU.add),
                     reads=["psM", "adab"], writes=["modT"])
            for g in range(2):
                S.op("dve", lambda e, g=g: e.scalar_tensor_tensor(self.gs1[:, :, g], self.modT[:, 8:16, g], 1.0,
                                                                  self.nm[:, i, :], ALU.add, ALU.mult),
                     reads=["modT", "nm"], writes=["gs1"])
                S.op("dve", lambda e, g=g: e.scalar_tensor_tensor(self.gs2[:, :, g], self.modT[:, 32:40, g], 1.0,
                                                                  self.nf[:, i, :], ALU.add, ALU.mult),
                     reads=["modT", "nf"], writes=["gs2"])
            S.flush()

    def norm(self, st, gs, sh_base, ntile, tag):
        nc, S = self.nc, self.S
        sq = [st.enter_context(nc.sbuf_tensor(f"sq{s}_{tag}", [128, NCH, 512], BF16)) for s in range(2)]
        tmp = [st.enter_context(nc.sbuf_tensor(f"ntmp{s}_{tag}", [128, NCH, 512], F32)) for s in range(1)]
        rstd = [st.enter_context(nc.sbuf_tensor(f"rstd{s}_{tag}", [128, 512], F32)) for s in range(2)]
        psN = [st.enter_context(nc.psum_tensor(f"psN{s}_{tag}", [128, 512], F32)) for s in range(2)]
        for t in range(ntile):
            t0, n = TT[t]
            g = 0 if t < 4 else 1
            s2 = t % 2
            hs = [("h", c, t) for c in range(NCH)]
            S.op("act", lambda e, t0=t0, n=n, s2=s2: e.activation(out=sq[s2][:, :, 0:n], in_=self.h[:, :, t0:t0 + n], func=AF.Square),
                 reads=hs, writes=[("sq", s2)])

            def mm(e, n=n, s2=s2):
                last = None
                for k in range(NCH):
                    last = e.matmul(psN[s2][:, 0:n], self.ones[:], sq[s2][:, k, 0:n], start=(k == 0), stop=(k == NCH - 1))
                return last
            S.op("pe", mm, reads=[("sq", s2), "ones"], writes=[("psN", s2)])
            S.op("act", lambda e, n=n, s2=s2: e.activation(out=rstd[s2][:, 0:n], in_=psN[s2][:, 0:n], func=AF.Sqrt,
                                                           bias=self.epsc[:, 0:1], scale=1.0 / D),
                 reads=[("psN", s2), "epsc"], writes=[("rstd", s2)])
            S.op("dve", lambda e, n=n, s2=s2: e.reciprocal(rstd[s2][:, 0:n], rstd[s2][:, 0:n]),
                 reads=[("rstd", s2)], writes=[("rstd", s2)])
            for c in range(NCH):
                S.op("dve", lambda e, c=c, t0=t0, n=n, s2=s2, g=g: e.scalar_tensor_tensor(
                    tmp[0][:, c, 0:n], self.h[:, c, t0:t0 + n], gs[:, c, g:g + 1], rstd[s2][:, 0:n], ALU.mult, ALU.mult),
                    reads=[("h", c, t), ("rstd", s2), "gs1", "gs2"], writes=[("ntmp", c)])
                S.op("act", lambda e, c=c, t0=t0, n=n, g=g: e.activation(
                    out=self.a[:, c, t0:t0 + n], in_=tmp[0][:, c, 0:n], func=AF.Identity,
                    bias=self.modT[:, sh_base + c, g:g + 1], scale=1.0),
                    reads=[("ntmp", c), "modT"], writes=[("a", c, t)])

    def ffn_phase(self, i, ntile):
        nc, S, dr = self.nc, self.S, self.dr
        GC = 2
        NG = HID // 128 // GC
        with ExitStack() as st0:
            self.norm(st0, self.gs2, 24, ntile, f"f{i}")
            S.flush()
        with ExitStack() as st:
            wa = [st.enter_context(nc.sbuf_tensor(f"wa{s}_{i}", [128, NCH, GC * 128], BF16)) for s in range(2)]
            wg = [st.enter_context(nc.sbuf_tensor(f"wg{s}_{i}", [128, NCH, GC * 128], BF16)) for s in range(2)]
            wo = [st.enter_context(nc.sbuf_tensor(f"wo{s}_{i}", [128, GC, D], BF16)) for s in range(2)]
            ug = [st.enter_context(nc.sbuf_tensor(f"ug{s}_{i}", [128, GC, T], BF16)) for s in range(2)]
            su = [st.enter_context(nc.sbuf_tensor(f"su{s}_{i}", [128, 512], F32)) for s in range(2)]
            psA = [st.enter_context(nc.psum_tensor(f"psA{s}_{i}", [128, 512], F32)) for s in range(2)]
            psG = [st.enter_context(nc.psum_tensor(f"psG{s}_{i}", [128, 512], F32)) for s in range(2)]
            psY = [st.enter_context(nc.psum_tensor(f"psY{s}_{i}", [128, 512], F32)) for s in range(2)]
            win = self.I("ffn_w_in")
            wout = self.I("ffn_w_out")
            cnt = 0
            ycnt = 0
            for G in range(NG):
                s = G % 2
                c0 = G * GC * 128
                S.dma("pool", wa[s][:], win[i, :, c0:c0 + GC * 128].rearrange("(k p) c -> p k c", p=128), writes=[("wa", s)])
                S.dma("pool", wg[s][:], win[i, :, HID + c0:HID + c0 + GC * 128].rearrange("(k p) c -> p k c", p=128), writes=[("wg", s)])
                S.dma("pool", wo[s][:], wout[i, c0:c0 + GC * 128, :].rearrange("(j p) o -> p j o", p=128), writes=[("wo", s)])
                for t in range(ntile):
                    t0, n = TT[t]
                    for j in range(GC):
                        ps = cnt % 2
                        cnt += 1

                        def mm(e, s=s, j=j, t0=t0, n=n, ps=ps):
                            last = None
                            for k in range(NCH):
                                e.matmul(psA[ps][:, 0:n], wa[s][:, k, j * 128:(j + 1) * 128], self.a[:, k, t0:t0 + n],
                                         start=(k == 0), stop=(k == NCH - 1))
                            for k in range(NCH):
                                last = e.matmul(psG[ps][:, 0:n], wg[s][:, k, j * 128:(j + 1) * 128], self.a[:, k, t0:t0 + n],
                                                start=(k == 0), stop=(k == NCH - 1))
                            return last
                        S.op("pe", mm, reads=[("wa", s), ("wg", s)] + [("a", c, t) for c in range(NCH)],
                             writes=[("psA", ps), ("psG", ps)])
                        S.op("act", lambda e, ps=ps, n=n: e.activation(out=su[ps][:, 0:n], in_=psA[ps][:, 0:n], func=AF.Silu),
                             reads=[("psA", ps)], writes=[("su", ps)])
                        S.op("dve", lambda e, ps=ps, n=n, s=s, j=j, t0=t0: e.tensor_tensor(
                            ug[s][:, j, t0:t0 + n], psG[ps][:, 0:n], su[ps][:, 0:n], ALU.mult),
                            reads=[("psG", ps), ("su", ps)], writes=[("ug", s, t)])
                for t in range(ntile):
                    t0, n = TT[t]
                    g = 0 if t < 4 else 1
                    for o in range(NCH):
                        py = ycnt % 2
                        ycnt += 1

                        def mm2(e, s=s, o=o, t0=t0, n=n, py=py):
                            last = None
                            for j in range(GC):
                                last = e.matmul(psY[py][:, 0:n], wo[s][:, j, o * 128:(o + 1) * 128], ug[s][:, j, t0:t0 + n],
                                                start=(j == 0), stop=(j == GC - 1))
                            return last
                        S.op("pe", mm2, reads=[("wo", s), ("ug", s, t)], writes=[("psY", py)])
                        S.op("dve", lambda e, o=o, t0=t0, n=n, py=py, g=g: e.scalar_tensor_tensor(
                            self.h[:, o, t0:t0 + n], psY[py][:, 0:n], self.modT[:, 40 + o, g:g + 1], self.h[:, o, t0:t0 + n],
                            ALU.mult, ALU.add),
                            reads=[("psY", py), ("h", o, t), "modT"], writes=[("h", o, t)])
            S.flush()

    def final_phase(self):
        nc, S, dr = self.nc, self.S, self.dr
        with ExitStack() as st:
            sq = [st.enter_context(nc.sbuf_tensor(f"fsq{s}", [128, NCH, 512], BF16)) for s in range(2)]
            rstd = [st.enter_context(nc.sbuf_tensor(f"frstd{s}", [128, 512], F32)) for s in range(2)]
            psN = [st.enter_context(nc.psum_tensor(f"fpsN{s}", [128, 512], F32)) for s in range(2)]
            for t in range(4):
                t0, n = TT[t]
                s2 = t % 2
                hs = [("h", c, t) for c in range(NCH)]
                S.op("act", lambda e, t0=t0, n=n, s2=s2: e.activation(out=sq[s2][:, :, 0:n], in_=self.h[:, :, t0:t0 + n], func=AF.Square),
                     reads=hs, writes=[("sq", s2)])

                def mm(e, n=n, s2=s2):
                    last = None
                    for k in range(NCH):
                        last = e.matmul(psN[s2][:, 0:n], self.ones[:], sq[s2][:, k, 0:n], start=(k == 0), stop=(k == NCH - 1))
                    return last
                S.op("pe", mm, reads=[("sq", s2), "ones"], writes=[("psN", s2)])
                S.op("act", lambda e, n=n, s2=s2: e.activation(out=rstd[s2][:, 0:n], in_=psN[s2][:, 0:n], func=AF.Sqrt,
                                                               bias=self.epsc[:, 0:1], scale=1.0 / D),
                     reads=[("psN", s2), "epsc"], writes=[("rstd", s2)])
                S.op("dve", lambda e, n=n, s2=s2: e.reciprocal(rstd[s2][:, 0:n], rstd[s2][:, 0:n]),
                     reads=[("rstd", s2)], writes=[("rstd", s2)])
                for c in range(NCH):
                    S.op("dve", lambda e, c=c, t0=t0, n=n, s2=s2: e.scalar_tensor_tensor(
                        self.h[:, c, t0:t0 + n], self.h[:, c, t0:t0 + n], self.nfin[:, c:c + 1], rstd[s2][:, 0:n], ALU.mult, ALU.mult),
                        reads=[("h", c, t), ("rstd", s2), "nfin"], writes=[("h", c, t)])
            for c in range(NCH):
                S.dma("sp", dr["y"][c * 128:(c + 1) * 128, :], self.h[:, c, 0:S_LAT],
                      reads=[("h", c, t) for t in range(4)], is_output=True)

    def attn_phase(self, i, kind):
        nc, S = self.nc, self.S
        with ExitStack() as st0:
            self.norm(st0, self.gs1, 0, 5, f"m{i}")
            S.flush()
        rope = kind in ("gqa", "swa")
        if kind == "na":
            NP, NG, QB, VW = 1, 8, 128, 128
            wqkv, wo_d = self.I("na_w_qkv"), self.I("na_w_o")
            qcol = lambda g: g * 128
            kcol = lambda g: 1024 + g * 128
            vcol = lambda g: 2048 + g * 128
        elif kind == "gqa":
            NP, NG, QB, VW = 2, 4, 256, 64
            wqkv, wqkvp, wo_d = self.I("gqa_w_qkv"), self.I("gqa_w_qkv_p"), self.I("gqa_w_o")
            qcol = lambda g: g * 256
            kcol = lambda g: 1024 + g * 64
            vcol = lambda g: 1280 + g * 64
        else:
            NP, NG, QB, VW = 2, 4, 128, 64
            wqkv, wqkvp, wo_d = self.I("swa_w_qkv"), self.I("swa_w_qkv_p"), self.I("swa_w_o")
            qcol = lambda g: g * 256
            kcol = lambda g: 1024 + (g // 2) * 64
            vcol = lambda g: 1152 + (g // 2) * 64
        N = NP * QB
        GM = 512 // N
        with ExitStack() as st:
            sb = lambda name, shape, dt: st.enter_context(nc.sbuf_tensor(f"{name}_{i}", list(shape), dt))
            ps = lambda name: st.enter_context(nc.psum_tensor(f"{name}_{i}", [128, 512], F32))
            wq = [sb(f"wq{x}", [128, NCH, NP * 128], BF16) for x in range(2)]
            wk = [sb(f"wk{x}", [128, NCH, 128], BF16) for x in range(2)]
            wv = [sb(f"wv{x}", [128, NCH, VW], BF16) for x in range(2)]
            wo = [sb(f"awo{x}", [128, NP, D], BF16) for x in range(2)]
            if rope:
                wqp = [sb(f"wqp{x}", [128, NCH, NP * 128], BF16) for x in range(2)]
                wkp = [sb(f"wkp{x}", [128, NCH, 128], BF16) for x in range(2)]
                cs = sb("cs", [128, 2, T], F32)
                S.dma("sp", cs[:], self.I("rope_cs")[:, :, :], writes=["cs"])
                t1 = [sb(f"t1{x}", [128, 512], F32) for x in range(2)]
                t2 = [sb(f"t2{x}", [128, 512], F32) for x in range(1)] * 2
            if kind == "gqa":
                gains = sb("gains", [128, 4], F32)
                S.dma("sp", gains[:], self.I("gqa_gains")[:, :], writes=["gains"])
                sqh = [sb(f"sqh{x}", [128, 512], BF16) for x in range(1)] * 2
                rs = [sb(f"rs{x}", [128, 512], F32) for x in range(1)] * 2
                bd = sb("bd", [128, 128], BF16)
                S.op("dve", lambda e: e.memset(bd[:], 0.0), writes=["bd"])
                S.op("dve", lambda e: e.memset(bd[0:64, 0:64], 1.0), writes=["bd"])
                S.op("dve", lambda e: e.memset(bd[64:128, 64:128], 1.0), writes=["bd"])
            if kind == "swa":
                sinkx = sb("sinkx", [128, 8], F32)
                S.dma("sp", sinkx[:], self.I("swa_sinks")[:, :], writes=["sinkx"])
                S.op("act", lambda e: e.activation(out=sinkx[:], in_=sinkx[:], func=AF.Exp), reads=["sinkx"], writes=["sinkx"])
                masks = sb("masks", [128, 2, 128], BF16)
                S.dma("pool", masks[:], self.I("swa_masks")[:, :, :], writes=["masks"])
            if kind == "na":
                stg = sb("stg", [128, 15, 64], F32)
                E2 = sb("E2", [128, 15, 64], BF16)
                Fm = sb("Fm", [128, 2, 25, 128], BF16)
            qg = sb("qg", [128, NP, T], BF16)
            kT = sb("kT", [128, T], BF16)
            V = sb("V", [128, 18, VW], BF16)
            Og = sb("Og", [128, NP, T], BF16)
            P = [sb(f"P{x}", [128, 1024], BF16) for x in range(3)]
            rec = [sb(f"rec{x}", [128, 512], F32) for x in range(2)]
            psS = [st.enter_context(nc.psum_tensor(f"psS{x}_{i}", [128, 1024], F32)) for x in range(2)]
            psO = [ps(f"psO{x}") for x in range(2)]
            psD = [ps(f"psD{x}") for x in range(2)]
            psP = psS
            cntP = [0]
            evtog = [0]

            def r0(qr):
                return min(max(qr - 4, 0), 24)

            def tw0(R):
                return min(max(R - 2, 0), 11)
            VAR_R = [0, 1, 2, 14, 15]

            def var_of(R):
                return {0: 0, 1: 1, 14: 3, 15: 4}.get(R, 2)

            def proj(wt, wtp, col0, dst_fn, is_q, t, wtok, dtok):
                t0, n = TT[t]
                pp = cntP[0] % 2
                cntP[0] += 1
                areads = [("a", c, t) for c in range(NCH)]

                def mm(e, pp=pp, t0=t0, n=n):
                    last = None
                    for k in range(NCH):
                        last = e.matmul(psP[pp][:, 0:n], wt[:, k, col0:col0 + 128], self.a[:, k, t0:t0 + n],
                                        start=(k == 0), stop=(k == NCH - 1))
                    return last
                S.op("pe", mm, reads=areads + wtok, writes=[("psS", pp)])
                dst = dst_fn(t0, n)
                if not rope:
                    evtog[0] += 1
                    if evtog[0] % 2 == 0:
                        S.op("act", lambda e, pp=pp, n=n: e.activation(out=dst, in_=psP[pp][:, 0:n], func=AF.Copy),
                             reads=[("psS", pp)], writes=[dtok])
                    else:
                        S.op("dve", lambda e, pp=pp, n=n: e.tensor_copy(dst, psP[pp][:, 0:n]),
                             reads=[("psS", pp)], writes=[dtok])
                    return
                pq = cntP[0] % 2
                cntP[0] += 1

                def mm2(e, pq=pq, t0=t0, n=n):
                    last = None
                    for k in range(NCH):
                        last = e.matmul(psP[pq][:, 0:n], wtp[:, k, col0:col0 + 128], self.a[:, k, t0:t0 + n],
                                        start=(k == 0), stop=(k == NCH - 1))
                    return last
                S.op("pe", mm2, reads=areads + wtok, writes=[("psS", pq)])
                x = evtog[0] % 2
                evtog[0] += 1
                if kind == "gqa":
                    gc = 0 if is_q else 2
                    S.op("act", lambda e, pp=pp, n=n, x=x: e.activation(out=sqh[x][:, 0:n], in_=psP[pp][:, 0:n], func=AF.Square),
                         reads=[("psS", pp)], writes=[("sqh", 0)])
                    S.op("dve", lambda e, pp=pp, n=n, x=x, t0=t0, gc=gc: e.scalar_tensor_tensor(
                        t1[x][:, 0:n], psP[pp][:, 0:n], gains[:, gc:gc + 1], cs[:, 0, t0:t0 + n], ALU.mult, ALU.mult),
                        reads=[("psS", pp), "gains", "cs"], writes=[("t1", x)])
                    S.op("dve", lambda e, pq=pq, n=n, x=x, t0=t0, gc=gc: e.scalar_tensor_tensor(
                        t2[x][:, 0:n], psP[pq][:, 0:n], gains[:, gc + 1:gc + 2], cs[:, 1, t0:t0 + n], ALU.mult, ALU.mult),
                        reads=[("psS", pq), "gains", "cs"], writes=[("t2", 0)])
                    S.op("pe", lambda e, pp=pp, n=n, x=x: e.matmul(psP[pp][:, 0:n], bd[:], sqh[x][:, 0:n], start=True, stop=True),
                         reads=[("sqh", 0), "bd"], writes=[("psS", pp)])
                    S.op("act", lambda e, pp=pp, n=n, x=x: e.activation(out=rs[x][:, 0:n], in_=psP[pp][:, 0:n], func=AF.Sqrt,
                                                                       bias=self.epsc[:, 0:1], scale=1.0 / 64),
                         reads=[("psS", pp), "epsc"], writes=[("rs", 0)])
                    S.op("dve", lambda e, n=n, x=x: e.reciprocal(rs[x][:, 0:n], rs[x][:, 0:n]), reads=[("rs", 0)], writes=[("rs", 0)])
                    S.op("pool", lambda e, n=n, x=x: e.tensor_tensor(t1[x][:, 0:n], t1[x][:, 0:n], t2[x][:, 0:n], ALU.add),
                         reads=[("t1", x), ("t2", 0)], writes=[("t1", x)])
                    S.op("pool", lambda e, n=n, x=x: e.tensor_tensor(dst, t1[x][:, 0:n], rs[x][:, 0:n], ALU.mult),
                         reads=[("t1", x), ("rs", 0)], writes=[dtok])
                else:
                    S.op("dve", lambda e, pp=pp, n=n, x=x, t0=t0: e.tensor_tensor(t1[x][:, 0:n], psP[pp][:, 0:n], cs[:, 0, t0:t0 + n], ALU.mult),
                         reads=[("psS", pp), "cs"], writes=[("t1", x)])
                    S.op("dve", lambda e, pq=pq, n=n, x=x, t0=t0: e.tensor_tensor(t2[x][:, 0:n], psP[pq][:, 0:n], cs[:, 1, t0:t0 + n], ALU.mult),
                         reads=[("psS", pq), "cs"], writes=[("t2", 0)])
                    S.op("pool", lambda e, n=n, x=x: e.tensor_tensor(dst, t1[x][:, 0:n], t2[x][:, 0:n], ALU.add),
                         reads=[("t1", x), ("t2", 0)], writes=[dtok])

            def load_weights(g):
                sl = g % 2
                wsrc = lambda col, n: wqkv[:, col:col + n].rearrange("(k p) c -> p k c", p=128)
                wsrcp = lambda col, n: wqkvp[:, col:col + n].rearrange("(k p) c -> p k c", p=128)
                S.dma("pool", wq[sl][:], wsrc(qcol(g), NP * 128), writes=[("wq", sl)])
                if rope:
                    S.dma("pool", wqp[sl][:], wsrcp(qcol(g), NP * 128), writes=[("wq", sl)])
                if kind == "na":
                    S.dma("pool", wk[sl][:], wsrc(kcol(g), 128), writes=[("wk", sl)])
                else:
                    for half in range(2):
                        S.dma("pool", wk[sl][:, :, half * 64:(half + 1) * 64], wsrc(kcol(g), 64), writes=[("wk", sl)])
                        S.dma("pool", wkp[sl][:, :, half * 64:(half + 1) * 64], wsrcp(kcol(g), 64), writes=[("wk", sl)])
                S.dma("pool", wv[sl][:], wsrc(vcol(g), VW), writes=[("wk", sl)])
                S.dma("pool", wo[sl][:], wo_d[g * NP * 128:(g + 1) * NP * 128, :].rearrange("(p r) o -> r p o", r=128), writes=[("wo", sl)])

            def run_group(g):
                sl = g % 2
                if kind == "na":
                    for hh in range(2):
                        hd = g * 2 + hh
                        S.dma("sp", stg[0:64], self.I("nab")[hd], writes=["stg"])
                        S.dma("sp", stg[64:128], self.I("nab")[hd], writes=["stg"])
                        S.op("act", lambda e: e.activation(out=E2[:], in_=stg[:], func=AF.Exp), reads=["stg"], writes=["E2"])
                        S.op("pool", lambda e, hh=hh: e.memset(Fm[:, hh], 0.0), writes=[("F", hh)])
                        for v in range(5):
                            R = VAR_R[v]
                            for w in range(5):
                                tk = tw0(R) + w
                                for krp in range(2):
                                    for qrp in range(2):
                                        kr, qr = 2 * tk + krp, 2 * R + qrp
                                        if not (r0(qr) <= kr < r0(qr) + 8):
                                            continue
                                        dr_ = kr - qr + 7
                                        S.op("pool", lambda e, hh=hh, v=v, w=w, krp=krp, qrp=qrp, dr_=dr_: e.tensor_copy(
                                            Fm[krp * 64:(krp + 1) * 64, hh, v * 5 + w, qrp * 64:(qrp + 1) * 64],
                                            E2[krp * 64:(krp + 1) * 64, dr_, :]),
                                            reads=["E2"], writes=[("F", hh)])
                for p in range(NP):
                    for t in range(5):
                        proj(wq[sl], wqp[sl] if rope else None, p * 128, lambda t0, n, p=p: qg[:, p, t0:t0 + n], True, t, [("wq", sl)], "qg")
                new_kv = (kind != "swa") or (g % 2 == 0)
                if new_kv:
                    for t in range(5):
                        proj(wk[sl], wkp[sl] if rope else None, 0, lambda t0, n: kT[:, t0:t0 + n], False, t, [("wk", sl)], "kT")
                    for tk in range(18):
                        pp = cntP[0] % 2
                        cntP[0] += 1
                        tt_ = min(tk // 4, 4)

                        def mmv(e, tk=tk, pp=pp):
                            last = None
                            for k in range(NCH):
                                last = e.matmul(psP[pp][:, 0:VW], self.a[:, k, tk * 128:(tk + 1) * 128], wv[sl][:, k, :],
                                                start=(k == 0), stop=(k == NCH - 1))
                            return last
                        S.op("pe", mmv, reads=[("a", c, tt_) for c in range(NCH)] + [("wk", sl)], writes=[("psS", pp)])
                        S.op("act", lambda e, tk=tk, pp=pp: e.activation(out=V[:, tk, :], in_=psP[pp][:, 0:VW], func=AF.Copy),
                             reads=[("psS", pp)], writes=["V"])
                nlat = S_LAT // QB
                nctx = L_CTX // QB
                steps = []
                for qb in range(nlat + nctx):
                    if qb >= nlat:
                        kl = [(16, None), (17, None)]
                    elif kind == "na":
                        kl = [(tw0(qb) + w, ("F", var_of(qb) * 5 + w)) for w in range(5)] + [(16, None), (17, None)]
                    elif kind == "gqa":
                        kl = [(tk, None) for tk in range(18)]
                    else:
                        kl = []
                        if qb >= 1:
                            kl.append((qb - 1, ("M", 0)))
                        kl.append((qb, None))
                        if qb <= 14:
                            kl.append((qb + 1, ("M", 1)))
                        kl += [(16, None), (17, None)]
                    for idx, (tk, fm) in enumerate(kl):
                        steps.append((qb, tk, fm, idx == 0, idx == len(kl) - 1))
                msteps = []
                for st_ in steps:
                    if msteps and msteps[-1][0][0] == st_[0] and len(msteps[-1]) < GM:
                        msteps[-1].append(st_)
                    else:
                        msteps.append([st_])
                vcs = (slice(0, 64), slice(64, 128)) if kind == "na" else (slice(0, 64), slice(0, 64))

                def qk(midx):
                    ms = msteps[midx]
                    r = midx % 2
                    q0 = ms[0][0] * QB

                    def f(e, ms=ms, r=r, q0=q0):
                        last_ = None
                        for x, (_, tk, _, _, _) in enumerate(ms):
                            for sd in range(2):
                                pl = slice(sd * 64, (sd + 1) * 64)
                                last_ = e.matmul(psS[r][:, sd * 512 + x * N:sd * 512 + (x + 1) * N], kT[pl, tk * 128:(tk + 1) * 128],
                                                 qg[pl, 0:NP, q0:q0 + QB], start=True, stop=True)
                        return last_
                    S.op("pe", f, reads=["kT", "qg"], writes=[("psS", r)])

                def rest(midx):
                    ms = msteps[midx]
                    r = midx % 2
                    pr = midx % 3
                    qb = ms[0][0]
                    q0 = qb * QB
                    ob = qb % 2
                    W = len(ms) * N
                    S.op("act", lambda e, r=r, pr=pr, W=W: e.activation(
                        out=P[pr][:].rearrange("p (s c) -> p s c", s=2)[:, :, 0:W],
                        in_=psS[r][:].rearrange("p (s c) -> p s c", s=2)[:, :, 0:W], func=AF.Exp, scale=0.125),
                        reads=[("psS", r)], writes=[("P", pr)])
                    fms = [(x, st_[2]) for x, st_ in enumerate(ms) if st_[2] is not None]
                    if fms and fms[0][1][0] == "F":
                        x0, nf_, fi0 = fms[0][0], len(fms), fms[0][1][1]
                        for sd in range(2):
                            S.op("dve" if sd == 0 else "pool", lambda e, pr=pr, x0=x0, nf_=nf_, fi0=fi0, sd=sd: e.tensor_tensor(
                                P[pr][:, sd * 512 + x0 * 128:sd * 512 + (x0 + nf_) * 128].rearrange("p (a b) -> p a b", a=nf_),
                                P[pr][:, sd * 512 + x0 * 128:sd * 512 + (x0 + nf_) * 128].rearrange("p (a b) -> p a b", a=nf_),
                                Fm[:, sd, fi0:fi0 + nf_, :], ALU.mult),
                                reads=[("P", pr), ("F", sd)], writes=[("P", pr)])
                    elif fms:
                        def mk(e, pr=pr, fms=fms):
                            last_ = None
                            for (x, fm) in fms:
                                for sd in range(2):
                                    for hx in range(NP):
                                        b0 = sd * 512 + x * N + hx * QB
                                        sl_ = P[pr][:, b0:b0 + 128]
                                        last_ = e.tensor_tensor(sl_, sl_, masks[:, fm[1], :], ALU.mult)
                            return last_
                        S.op("pool", mk, reads=[("P", pr), "masks"], writes=[("P", pr)])

                    def pv(e, ms=ms, pr=pr, ob=ob):
                        last_ = None
                        for x, (_, tk, _, first, last) in enumerate(ms):
                            for sd in range(2):
                                pl = slice(sd * 64, (sd + 1) * 64)
                                e.matmul(psO[ob][pl, 0:N], V[:, tk, vcs[sd]], P[pr][:, sd * 512 + x * N:sd * 512 + (x + 1) * N], start=first, stop=last)
                            for sd in range(2):
                                pl = slice(sd * 64, (sd + 1) * 64)
                                last_ = e.matmul(psD[ob][pl, 0:N], self.ones[:, 0:64], P[pr][:, sd * 512 + x * N:sd * 512 + (x + 1) * N], start=first, stop=last)
                        return last_
                    S.op("pe", pv, reads=[("P", pr), "V", "ones"], writes=[("psO", ob)])
                    if ms[-1][4]:
                        if kind == "swa":
                            def addsink(e, ob=ob):
                                last_ = None
                                for hx in range(NP):
                                    pair = g * NP + hx
                                    last_ = e.tensor_scalar(rec[ob][:, hx * QB:(hx + 1) * QB], psD[ob][:, hx * QB:(hx + 1) * QB],
                                                            sinkx[:, pair:pair + 1], None, ALU.add)
                                return last_
                            S.op("dve", addsink, reads=[("psO", ob), "sinkx"], writes=[("rec", ob)])
                            S.op("dve", lambda e, ob=ob: e.reciprocal(rec[ob][:, 0:N], rec[ob][:, 0:N]),
                                 reads=[("rec", ob)], writes=[("rec", ob)])
                        else:
                            S.op("dve", lambda e, ob=ob: e.reciprocal(rec[ob][:, 0:N], psD[ob][:, 0:N]),
                                 reads=[("psO", ob)], writes=[("rec", ob)])
                        S.op("dve", lambda e, ob=ob, q0=q0: e.tensor_tensor(
                            Og[:, 0:NP, q0:q0 + QB], psO[ob][:, 0:N].rearrange("p (a b) -> p a b", a=NP),
                            rec[ob][:, 0:N].rearrange("p (a b) -> p a b", a=NP), ALU.mult),
                            reads=[("psO", ob), ("rec", ob)], writes=["Og"])
                qk(0)
                for midx in range(len(msteps)):
                    if midx + 1 < len(msteps):
                        qk(midx + 1)
                    rest(midx)
                for t in range(5):
                    t0, n = TT[t]
                    gi = 0 if t < 4 else 1
                    for o in range(NCH):
                        pp = cntP[0] % 2
                        cntP[0] += 1

                        def mmo(e, o=o, t0=t0, n=n, pp=pp):
                            last = None
                            for p in range(NP):
                                last = e.matmul(psP[pp][:, 0:n], wo[sl][:, p, o * 128:(o + 1) * 128], Og[:, p, t0:t0 + n],
                                                start=(p == 0), stop=(p == NP - 1))
                            return last
                        S.op("pe", mmo, reads=[("wo", sl), "Og"], writes=[("psS", pp)])
                        S.op("dve", lambda e, o=o, t0=t0, n=n, pp=pp, gi=gi: e.scalar_tensor_tensor(
                            self.h[:, o, t0:t0 + n], psP[pp][:, 0:n], self.modT[:, 16 + o, gi:gi + 1], self.h[:, o, t0:t0 + n],
                            ALU.mult, ALU.add),
                            reads=[("psS", pp), ("h", o, t), "modT"], writes=[("h", o, t)])

            load_weights(0)
            for g in range(NG):
                if g + 1 < NG:
                    load_weights(g + 1)
                run_group(g)
            S.flush()

    def lru_phase(self, i):
        nc, S = self.nc, self.S
        with ExitStack() as st0:
            self.norm(st0, self.gs1, 0, 5, f"m{i}")
            S.flush()
        with ExitStack() as stz:
            z = stz.enter_context(nc.sbuf_tensor(f"z_{i}", [128, NCH, S_LAT], BF16))
            with ExitStack() as st:
                sb = lambda name, shape, dt: st.enter_context(nc.sbuf_tensor(f"{name}_{i}", list(shape), dt))
                ps = lambda name: st.enter_context(nc.psum_tensor(f"{name}_{i}", [128, 512], F32))
                vec = sb("lvec", [128, 12, NCH], F32)
                nsp = sb("nsp", [128, 2, NCH], F32)
                win = [sb(f"lwin{x}", [128, NCH, 256], BF16) for x in range(1)] * 2
                Wbd = sb("Wbd", [128, 2, 2, 128], BF16)
                A = sb("A", [128, T], F32)
                xc = sb("xc", [128, T], F32)
                xb = sb("xb", [128, T], BF16)
                gl = sb("gl", [128, S_LAT], BF16)
                Bd = sb("Bd", [128, T], F32)
                Hf = sb("Hf", [128, T], F32)
                Hb = sb("Hb", [128, T], F32)
                tm = [sb(f"tm{x}", [128, 512], F32) for x in range(2)]
                psX = [ps(f"psX{x}") for x in range(2)]
                psR = [ps(f"psR{x}") for x in range(2)]
                psI = [ps(f"psI{x}") for x in range(2)]
                S.dma("sp", vec[:], self.I("lru_vecs")[:, :, :], writes=["lvec"])
                S.op("act", lambda e: e.activation(out=nsp[:], in_=vec[:, 9:11, :], func=AF.Exp, scale=-1.0), reads=["lvec"], writes=["nsp"])
                S.op("act", lambda e: e.activation(out=nsp[:], in_=nsp[:], func=AF.Ln, bias=1.0, scale=1.0), reads=["nsp"], writes=["nsp"])
                S.op("dve", lambda e: e.tensor_scalar(nsp[:], nsp[:], -8.0, None, ALU.mult), reads=["nsp"], writes=["nsp"])
                S.op("pool", lambda e: e.memset(Wbd[:], 0.0), writes=["Wbd"])
                w_in = self.I("lru_w_in")
                wa_d, wx_d = self.I("lru_w_a"), self.I("lru_w_x")
                cx = [0]
                for c in range(NCH):
                    wsl = 0
                    S.dma("pool", win[wsl][:, :, 0:128], w_in[:, c * 128:(c + 1) * 128].rearrange("(k p) c -> p k c", p=128), writes=[("lwin", wsl)])
                    S.dma("pool", win[wsl][:, :, 128:256], w_in[:, D + c * 128:D + (c + 1) * 128].rearrange("(k p) c -> p k c", p=128), writes=[("lwin", wsl)])
                    for d in range(2):
                        for gt, wd in enumerate((wa_d, wx_d)):
                            for half in range(2):
                                S.dma("pool", Wbd[half * 64:(half + 1) * 64, d, gt, half * 64:(half + 1) * 64], wd[d, 2 * c + half], writes=["Wbd"])
                    for t in range(5):
                        t0, n = TT[t]
                        px = cx[0] % 2
                        cx[0] += 1

                        def mm(e, t0=t0, n=n, px=px, wsl=wsl):
                            last = None
                            for k in range(NCH):
                                last = e.matmul(psX[px][:, 0:n], win[wsl][:, k, 0:128], self.a[:, k, t0:t0 + n], start=(k == 0), stop=(k == NCH - 1))
                            return last
                        S.op("pe", mm, reads=[("lwin", wsl)] + [("a", k, t) for k in range(NCH)], writes=[("psX", px)])
                        S.op("act", lambda e, t0=t0, n=n, px=px: e.activation(out=A[:, t0:t0 + n], in_=psX[px][:, 0:n], func=AF.Copy),
                             reads=[("psX", px)], writes=["A"])
                    for t in range(4):
                        t0, n = TT[t]
                        px = cx[0] % 2
                        cx[0] += 1

                        def mmg(e, t0=t0, n=n, px=px, wsl=wsl):
                            last = None
                            for k in range(NCH):
                                last = e.matmul(psX[px][:, 0:n], win[wsl][:, k, 128:256], self.a[:, k, t0:t0 + n], start=(k == 0), stop=(k == NCH - 1))
                            return last
                        S.op("pe", mmg, reads=[("lwin", wsl)] + [("a", k, t) for k in range(NCH)], writes=[("psX", px)])
                        S.op("act", lambda e, t0=t0, n=n, px=px: e.activation(out=gl[:, t0:t0 + n], in_=psX[px][:, 0:n], func=AF.Gelu_apprx_tanh),
                             reads=[("psX", px)], writes=["gl"])
                    S.op("act", lambda e, c=c: e.activation(out=xc[:], in_=A[:], func=AF.Identity, bias=vec[:, 4, c:c + 1], scale=vec[:, 2, c:c + 1]),
                         reads=["A", "lvec"], writes=["xc"])
                    for (kk, sh) in ((0, 2), (1, 1), (3, -1)):
                        for (lo, hi) in ((0, S_LAT), (S_LAT, T)):
                            if sh > 0:
                                o0, o1, i0, i1 = lo + sh, hi, lo, hi - sh
                            else:
                                o0, o1, i0, i1 = lo, hi + sh, lo - sh, hi
                            S.op("dve", lambda e, c=c, kk=kk, o0=o0, o1=o1, i0=i0, i1=i1: e.scalar_tensor_tensor(
                                xc[:, o0:o1], A[:, i0:i1], vec[:, kk, c:c + 1], xc[:, o0:o1], ALU.mult, ALU.add),
                                reads=["A", "lvec", "xc"], writes=["xc"])
                    S.op("act", lambda e: e.activation(out=xb[:], in_=xc[:], func=AF.Copy), reads=["xc"], writes=["xb"])
                    for d in range(2):
                        Hd = Hf if d == 0 else Hb
                        for t in range(5):
                            t0, n = TT[t]
                            px = cx[0] % 2
                            cx[0] += 1
                            S.op("pe", lambda e, d=d, t0=t0, n=n, px=px: e.matmul(psR[px][:, 0:n], Wbd[:, d, 0, :], xb[:, t0:t0 + n], start=True, stop=True),
                                 reads=["Wbd", "xb"], writes=[("psR", px)])
                            S.op("pe", lambda e, d=d, t0=t0, n=n, px=px: e.matmul(psI[px][:, 0:n], Wbd[:, d, 1, :], xb[:, t0:t0 + n], start=True, stop=True),
                                 reads=["Wbd", "xb"], writes=[("psI", px)])
                            S.op("act", lambda e, d=d, c=c, t0=t0, n=n, px=px: e.activation(out=A[:, t0:t0 + n], in_=psR[px][:, 0:n], func=AF.Sigmoid,
                                                                                         bias=vec[:, 5 + d, c:c + 1], scale=1.0),
                                 reads=[("psR", px), "lvec", "xb"], writes=[("A", t)])
                            S.op("act", lambda e, d=d, c=c, t0=t0, n=n: e.activation(out=A[:, t0:t0 + n], in_=A[:, t0:t0 + n], func=AF.Exp,
                                                                                scale=nsp[:, d, c:c + 1]),
                                 reads=[("A", t), "nsp"], writes=[("A", t)])
                            S.op("act", lambda e, d=d, c=c, t0=t0, n=n, px=px: e.activation(out=Bd[:, t0:t0 + n], in_=psI[px][:, 0:n], func=AF.Sigmoid,
                                                                                         bias=vec[:, 7 + d, c:c + 1], scale=1.0),
                                 reads=[("psI", px), "lvec", "Hscan"], writes=[("Bd", t)])
                            S.op("dve", lambda e, t0=t0, n=n: e.tensor_tensor(Bd[:, t0:t0 + n], Bd[:, t0:t0 + n], xc[:, t0:t0 + n], ALU.mult),
                                 reads=[("Bd", t), "xc"], writes=[("Bd", t)])
                            S.op("pool", lambda e, t0=t0, n=n, px=px: e.tensor_tensor(tm[px][:, 0:n], A[:, t0:t0 + n], A[:, t0:t0 + n], ALU.mult),
                                 reads=[("A", t)], writes=[("tm", px)])
                            S.op("act", lambda e, n=n, px=px: e.activation(out=tm[px][:, 0:n], in_=tm[px][:, 0:n], func=AF.Sqrt, bias=1.0, scale=-1.0),
                                 reads=[("tm", px)], writes=[("tm", px)])
                            S.op("dve", lambda e, t0=t0, n=n, px=px: e.tensor_tensor(Bd[:, t0:t0 + n], Bd[:, t0:t0 + n], tm[px][:, 0:n], ALU.mult),
                                 reads=[("Bd", t), ("tm", px)], writes=[("Bd", t)])
                        allA = [("A", t) for t in range(5)]
                        allB = [("Bd", t) for t in range(5)]
                        if d == 0:
                            S.op("dve", lambda e: e.tensor_tensor_scan(Hf[:, S_LAT:T], A[:, S_LAT:T], Bd[:, S_LAT:T], 0.0, ALU.mult, ALU.add),
                                 reads=allA + allB, writes=["Hf"])
                            S.op("dve", lambda e: e.tensor_tensor_scan(Hf[:, 0:S_LAT], A[:, 0:S_LAT], Bd[:, 0:S_LAT], Hf[:, T - 1:T], ALU.mult, ALU.add),
                                 reads=allA + allB + ["Hf"], writes=["Hf", "Hscan"])
                        else:
                            S.op("dve", lambda e: e.tensor_tensor_scan(Hb[:, S_LAT:T][:, ::-1], A[:, S_LAT:T][:, ::-1], Bd[:, S_LAT:T][:, ::-1], 0.0, ALU.mult, ALU.add),
                                 reads=allA + allB, writes=["Hb"])
                            S.op("dve", lambda e: e.tensor_tensor_scan(Hb[:, 0:S_LAT][:, ::-1], A[:, 0:S_LAT][:, ::-1], Bd[:, 0:S_LAT][:, ::-1], Hb[:, S_LAT:S_LAT + 1], ALU.mult, ALU.add),
                                 reads=allA + allB + ["Hb"], writes=["Hb", "Hscan"])
                        S.op("dve", lambda e: e.engine_nop() if False else e.memset(tm[0][:, 0:1], 0.0), reads=["Hscan"], writes=allA + allB + [("tm", 0), "A"])
                    S.op("dve", lambda e: e.tensor_tensor(Hf[:, 0:S_LAT], Hf[:, 0:S_LAT], Hb[:, 0:S_LAT], ALU.add), reads=["Hf", "Hb"], writes=["Hf"])
                    S.op("dve", lambda e, c=c: e.tensor_tensor(z[:, c, :], Hf[:, 0:S_LAT], gl[:, :], ALU.mult), reads=["Hf", "gl"], writes=[("z", c)])
                S.flush()
            with ExitStack() as st:
                wo_t = st.enter_context(nc.sbuf_tensor(f"lwo_{i}", [128, NCH, D], BF16))
                psP = [st.enter_context(nc.psum_tensor(f"lpsP{x}_{i}", [128, 512], F32)) for x in range(2)]
                S.dma("pool", wo_t[:], self.I("lru_w_out").rearrange("(k p) o -> p k o", p=128), writes=["lwo"])
                cp = 0
                for t in range(4):
                    t0, n = TT[t]
                    for o in range(NCH):
                        pp = cp % 2
                        cp += 1

                        def mmo(e, o=o, t0=t0, n=n, pp=pp):
                            last = None
                            for k in range(NCH):
                                last = e.matmul(psP[pp][:, 0:n], wo_t[:, k, o * 128:(o + 1) * 128], z[:, k, t0:t0 + n], start=(k == 0), stop=(k == NCH - 1))
                            return last
                        S.op("pe", mmo, reads=["lwo"] + [("z", k) for k in range(NCH)], writes=[("psP", pp)])
                        S.op("dve", lambda e, o=o, t0=t0, n=n, pp=pp: e.scalar_tensor_tensor(
                            self.h[:, o, t0:t0 + n], psP[pp][:, 0:n], self.modT[:, 16 + o, 0:1], self.h[:, o, t0:t0 + n], ALU.mult, ALU.add),
                            reads=[("psP", pp), ("h", o, t), "modT"], writes=[("h", o, t)])
                S.flush()

def _fm(v):
    v = np.asarray(v, np.float32)
    lead = v.shape[:-1]
    r = v.reshape(lead + (NCH, 128))
    return np.ascontiguousarray(np.moveaxis(r, -1, 0))


def prep_shared(inp):
    sh = {}
    sh["ada_w"] = np.ascontiguousarray(inp["ada_w"], np.float32)
    sh["ada_bT"] = _fm(inp["ada_b"].reshape(DEPTH, 48, 128).reshape(DEPTH, 48 * 128)) if False else \
        np.ascontiguousarray(np.moveaxis(np.asarray(inp["ada_b"], np.float32).reshape(DEPTH, 48, 128), -1, 0))
    sh["nmT"] = _fm(inp["norm_mix"])
    sh["nfT"] = _fm(inp["norm_ffn"])
    sh["nfinT"] = _fm(inp["norm_final"])
    sh["ffn_w_in"] = np.ascontiguousarray(inp["ffn_w_in"], np.float32)
    sh["ffn_w_out"] = np.ascontiguousarray(inp["ffn_w_out"], np.float32)
    sh["na_w_qkv"] = np.ascontiguousarray(inp["na_w_qkv"][0], np.float32)
    sh["na_w_o"] = np.ascontiguousarray(inp["na_w_o"][0], np.float32)
    sh["nab"] = na_bias_table(inp["na_rpb"][0])
    d = np.arange(64)
    partner = np.where((d // 16) % 2 == 0, d + 16, d - 16)
    for nm_, nq, nkv in (("gqa", 16, 4), ("swa", 16, 2)):
        w = np.ascontiguousarray(inp[nm_ + "_w_qkv"][0], np.float32)
        perm = np.arange(w.shape[1])
        for hd in range(nq + nkv):
            perm[hd * 64:(hd + 1) * 64] = hd * 64 + partner
        sh[nm_ + "_w_qkv"] = w
        sh[nm_ + "_w_qkv_p"] = np.ascontiguousarray(w[:, perm])
        sh[nm_ + "_w_o"] = np.ascontiguousarray(inp[nm_ + "_w_o"][0], np.float32)
    qg_, kg_ = np.asarray(inp["gqa_q_gain"][0], np.float32), np.asarray(inp["gqa_k_gain"][0], np.float32)
    gg = np.stack([qg_, qg_[partner], kg_, kg_[partner]], axis=1)
    sh["gqa_gains"] = np.ascontiguousarray(np.concatenate([gg, gg], axis=0))
    sk = np.asarray(inp["swa_sinks"][0], np.float32)
    sh["swa_sinks"] = np.ascontiguousarray(np.concatenate([np.broadcast_to(sk[0::2][None, :], (64, 8)), np.broadcast_to(sk[1::2][None, :], (64, 8))], axis=0))
    t = np.arange(S_LAT)
    row = (t // GRID_W).astype(np.float32)
    col = (t % GRID_W).astype(np.float32)
    inv_freq = (1.0 / (np.float32(10000.0) ** (np.arange(0, 32, 2, dtype=np.float32) / np.float32(32)))).astype(np.float32)
    ang_row = row[:, None] * inv_freq
    ang_col = col[:, None] * inv_freq
    cs = np.zeros((64, 2, T), np.float32)
    cs[:, 0, :] = 1.0
    for dd in range(64):
        seg, f = dd // 16, dd % 16
        ang = ang_row[:, f] if seg < 2 else ang_col[:, f]
        cs[dd, 0, :S_LAT] = np.cos(ang)
        cs[dd, 1, :S_LAT] = np.sin(ang) * (-1.0 if seg % 2 == 0 else 1.0)
    sh["rope_cs"] = np.ascontiguousarray(np.concatenate([cs, cs], axis=0))
    kk = np.arange(128)[:, None]
    qq = np.arange(128)[None, :]
    vecs = np.zeros((12, D), np.float32)
    vecs[0:4] = inp["lru_conv_w"][0]
    vecs[4] = inp["lru_conv_b"][0]
    vecs[5:7] = inp["lru_b_a"][0]
    vecs[7:9] = inp["lru_b_x"][0]
    vecs[9:11] = inp["lru_lam"][0]
    sh["lru_vecs"] = _fm(vecs)
    sh["lru_w_in"] = np.ascontiguousarray(inp["lru_w_in"][0], np.float32)
    sh["lru_w_a"] = np.ascontiguousarray(inp["lru_w_a"][0], np.float32)
    sh["lru_w_x"] = np.ascontiguousarray(inp["lru_w_x"][0], np.float32)
    sh["lru_w_out"] = np.ascontiguousarray(inp["lru_w_out"][0], np.float32)
    sh["swa_masks"] = np.ascontiguousarray(np.stack([(qq <= kk), (kk <= qq)], axis=1).astype(np.float32))
    return sh


def prep_core(inp, b):
    c = {}
    x = np.asarray(inp["x"][b], np.float32)
    ctx = np.asarray(inp["ctx"][b], np.float32)
    c["hin"] = np.ascontiguousarray(np.concatenate([x, ctx], axis=0).T)
    cT = np.stack([_fm(inp["c"][b]), _fm(inp["c_ctx"])], axis=-1)
    c["cT"] = np.ascontiguousarray(cT)
    return c


def na_bias_table(rpb):
    rpb = np.asarray(rpb, np.float32)
    kc = np.arange(64)[:, None]
    qc = np.arange(64)[None, :]
    dcol = np.clip(kc - qc + 15, 0, 30)
    cstart = np.clip(qc - 8, 0, 48)
    col_in = (kc >= cstart) & (kc < cstart + 16)
    g = rpb[:, :, dcol]
    g = np.where(col_in[None, None], g, np.float32(MASKV))
    return np.ascontiguousarray(np.transpose(g, (0, 2, 1, 3)).astype(np.float32))


_CACHE = {}


def kernel(**inputs):
    inputs = {k: np.asarray(v) for k, v in inputs.items()}
    n_cores = 8
    if "nc" not in _CACHE:
        B = Builder([0, 1, 2, 3], final=True)
        _CACHE["nc"] = (B, B.build())
    B, nc = _CACHE["nc"]
    sh = prep_shared(inputs)
    in_maps = []
    for b in range(n_cores):
        core = prep_core(inputs, b)
        in_maps.append({name: (core[name] if name in core else sh[name]) for name in B.dr if name != "y"})
    res = run_bass_kernel_spmd(nc, in_maps, core_ids=list(range(n_cores)))
    out = np.stack([np.asarray(res.results[b]["y"]).T for b in range(n_cores)], axis=0)
    return np.ascontiguousarray(out.astype(np.float32))
```

```python
from contextlib import ExitStack
import numpy as np
import concourse.bass as bass
import concourse.mybir as mybir
from concourse.bass_utils import run_bass_kernel_spmd

F32 = mybir.dt.float32
BF16 = mybir.dt.bfloat16
AF = mybir.ActivationFunctionType
ALU = mybir.AluOpType

D = 1024
S_LAT = 2048
L_CTX = 256
T = S_LAT + L_CTX
DEPTH = 4
HID = 2816
NCH = D // 128
TT = [(0, 512), (512, 512), (1024, 512), (1536, 512), (2048, 256)]
EPS = 1e-6
GRID_W = 64
MASKV = -30000.0


class EngQ:
    def __init__(self, name, sem):
        self.name = name
        self.sem = sem
        self.cnt = 0
        self.ops = []
        self.seen = {}


class Sched:
    ENGS = ("pe", "act", "dve", "pool", "sp")

    def __init__(self, nc, stack, n_dma_slots=8):
        self.nc = nc
        self.sems = {}
        self.q = {}
        for e in self.ENGS:
            self.sems[("c", e)] = stack.enter_context(nc.semaphore("c_" + e))
            self.q[e] = EngQ(e, ("c", e))
        self.dma_n = {}
        self.n_dma_slots = n_dma_slots
        for e in ("sp", "pool"):
            self.dma_n[e] = 0
            for i in range(n_dma_slots):
                self.sems[("d", e, i)] = stack.enter_context(nc.semaphore(f"d_{e}{i}"))
        self.w = {}
        self.r = {}
        self.out_dma = []

    def _deps(self, reads, writes):
        d = {}

        def add(k, v):
            if v > d.get(k, 0):
                d[k] = v
        for t in reads:
            if t in self.w:
                add(*self.w[t])
        for t in writes:
            if t in self.w:
                add(*self.w[t])
            for k, v in self.r.get(t, {}).items():
                add(k, v)
        return d

    def _commit(self, reads, writes, cid):
        k, v = cid
        for t in writes:
            self.w[t] = cid
            self.r[t] = {}
        for t in reads:
            rr = self.r.setdefault(t, {})
            if v > rr.get(k, 0):
                rr[k] = v

    def _waits(self, q, deps):
        ws = []
        for k, v in deps.items():
            if q.seen.get(k, 0) < v:
                q.seen[k] = v
                ws.append((k, v))
        return ws

    @staticmethod
    def _is_psum(t):
        n = t[0] if isinstance(t, tuple) else t
        return isinstance(n, str) and n.startswith("ps")

    def op(self, eng, fn, reads=(), writes=()):
        writes = list(writes) + [t for t in reads if self._is_psum(t)]
        reads = [t for t in reads if not self._is_psum(t)]
        q = self.q[eng]
        deps = self._deps(reads, writes)
        if eng == "pe":
            deps.pop(q.sem, None)
        ws = self._waits(q, deps)
        q.cnt += 1
        cid = (q.sem, q.cnt)
        q.ops.append((ws, fn, q.sem, 1))
        self._commit(reads, writes, cid)
        return cid

    def dma(self, eng, out, in_, reads=(), writes=(), is_output=False):
        q = self.q[eng]
        n = self.dma_n[eng]
        self.dma_n[eng] = n + 1
        slot = n % self.n_dma_slots
        semk = ("d", eng, slot)
        prev = 16 * (n // self.n_dma_slots)
        deps = self._deps(reads, writes)
        if prev > 0 and deps.get(semk, 0) < prev:
            deps[semk] = prev
        ws = self._waits(q, deps)
        cid = (semk, prev + 16)

        def fn(e, out=out, in_=in_):
            return e.dma_start(out=out, in_=in_)
        q.ops.append((ws, fn, semk, 16))
        self._commit(reads, writes, cid)
        if is_output:
            self.out_dma.append(cid)
        return cid

    def finish(self, eng="sp"):
        q = self.q[eng]
        d = {}
        for k, v in self.out_dma:
            d[k] = max(d.get(k, 0), v)
        ws = self._waits(q, d)
        q.ops.append((ws, None, None, 0))

    def flush(self):
        sems = self.sems
        with self.nc.Block() as block:
            def run(q, e):
                for ws, fn, isem, ival in q.ops:
                    for k, v in ws:
                        e.wait_ge(sems[k], v)
                    if fn is None:
                        continue
                    ins = fn(e)
                    if isem is not None:
                        ins.then_inc(sems[isem], ival)
                q.ops = []

            @block.tensor
            def _(e):
                run(self.q["pe"], e)

            @block.scalar
            def _(e):
                run(self.q["act"], e)

            @block.vector
            def _(e):
                run(self.q["dve"], e)

            @block.gpsimd
            def _(e):
                run(self.q["pool"], e)

            @block.sync
            def _(e):
                run(self.q["sp"], e)


class Builder:
    def __init__(self, layers, final, skip_mixer=False):
        self.layers = layers
        self.final = final
        self.skip_mixer = skip_mixer
        self.debug = False
        self.nc = bass.Bass("TRN2", target_bir_lowering=False)
        self.dr = {}

    def I(self, name):
        if name not in self.dr:
            self.dr[name] = self.nc.dram_tensor(name, list(self.shapes[name]), F32, kind="ExternalInput").ap()
        return self.dr[name]

    def dout(self, name, shape, dt=F32):
        t = self.nc.dram_tensor(name, list(shape), dt, kind="ExternalOutput").ap()
        self.dr[name] = t
        return t

    def build(self):
        nc = self.nc
        dr = self.dr
        self.shapes = {
            "hin": [D, T], "cT": [128, NCH, 2], "ada_w": [DEPTH, D, 6 * D], "ada_bT": [128, DEPTH, 48],
            "nmT": [128, DEPTH, NCH], "nfT": [128, DEPTH, NCH], "nfinT": [128, NCH],
            "ffn_w_in": [DEPTH, D, 2 * HID], "ffn_w_out": [DEPTH, HID, D],
            "na_w_qkv": [D, 3 * D], "na_w_o": [D, D], "nab": [16, 64, 15, 64],
            "gqa_w_qkv": [D, 1536], "gqa_w_qkv_p": [D, 1536], "gqa_gains": [128, 4], "gqa_w_o": [D, D],
            "swa_w_qkv": [D, 1280], "swa_w_qkv_p": [D, 1280], "swa_sinks": [128, 8], "swa_w_o": [D, D],
            "swa_masks": [128, 2, 128], "rope_cs": [128, 2, T],
            "lru_w_in": [D, 2 * D], "lru_vecs": [128, 12, NCH], "lru_w_a": [2, 16, 64, 64],
            "lru_w_x": [2, 16, 64, 64], "lru_w_out": [D, D],
        }
        if self.final:
            self.dout("y", [D, S_LAT])
        else:
            self.dout("hout", [D, T])

        with ExitStack() as st:
            self.S = S = Sched(nc, st)
            sb = lambda name, shape, dt: st.enter_context(nc.sbuf_tensor(name, list(shape), dt))
            self.h = sb("h", [128, NCH, T], F32)
            self.a = sb("a", [128, NCH, T], BF16)
            self.ones = sb("ones", [128, 128], BF16)
            self.cT = sb("cTs", [128, NCH, 2], F32)
            self.scT = sb("scT", [128, NCH, 2], BF16)
            self.adab = sb("adab", [128, DEPTH, 48], F32)
            self.nm = sb("nm", [128, DEPTH, NCH], F32)
            self.nf = sb("nf", [128, DEPTH, NCH], F32)
            self.nfin = sb("nfin", [128, NCH], F32)
            self.modTs = [sb(f"modT{x}", [128, 48, 2], F32) for x in range(2)]
            self.gs1s = [sb(f"gs1{x}", [128, NCH, 2], F32) for x in range(2)]
            self.gs2s = [sb(f"gs2{x}", [128, NCH, 2], F32) for x in range(2)]
            self.epsc = sb("epsc", [128, 1], F32)

            for c in range(NCH):
                S.dma("sp", self.h[:, c, :], self.I("hin")[c * 128:(c + 1) * 128, :],
                      writes=[("h", c, t) for t in range(5)])
            S.dma("sp", self.cT[:], self.I("cT")[:, :, :], writes=["cT"])
            S.dma("sp", self.adab[:], self.I("ada_bT")[:, :, :], writes=["adab"])
            S.dma("sp", self.nm[:], self.I("nmT")[:, :, :], writes=["nm"])
            S.dma("sp", self.nf[:], self.I("nfT")[:, :, :], writes=["nf"])
            S.dma("sp", self.nfin[:], self.I("nfinT")[:, :], writes=["nfin"])
            S.op("dve", lambda e: e.memset(self.ones[:], 1.0), writes=["ones"])
            S.op("dve", lambda e: e.memset(self.epsc[:], EPS), writes=["epsc"])
            S.op("act", lambda e: e.activation(out=self.scT[:], in_=self.cT[:], func=AF.Silu),
                 reads=["cT"], writes=["scT"])
            S.flush()

            for li, i in enumerate(self.layers):
                last = (i == DEPTH - 1)
                self.modT, self.gs1, self.gs2 = self.modTs[i % 2], self.gs1s[i % 2], self.gs2s[i % 2]
                self.mtok = ("modT", i % 2)
                if li == 0:
                    self.mods_phase(i)
                self.next_layer = self.layers[li + 1] if li + 1 < len(self.layers) else None
                kind = i % 4
                ntile = 4 if last else 5
                if self.skip_mixer:
                    pass
                elif kind == 0:
                    self.attn_phase(i, "na")
                elif kind == 1:
                    self.attn_phase(i, "gqa")
                elif kind == 2:
                    self.attn_phase(i, "swa")
                else:
                    self.lru_phase(i)
                self.ffn_phase(i, ntile)
            if self.final:
                self.final_phase()
            else:
                for c in range(NCH):
                    S.dma("sp", dr["hout"][c * 128:(c + 1) * 128, :], self.h[:, c, :],
                          reads=[("h", c, t) for t in range(5)], is_output=True)
            S.finish()
            S.flush()
        return nc

    def mods_block(self, i, jb, ring, psM):
        S = self.S
        slot = jb % 2
        S.dma("pool", ring[slot][:],
              self.I("ada_w")[i, :, jb * 512:(jb + 1) * 512].rearrange("(k p) c -> p k c", p=128),
              writes=[("adaw", slot)])

    def mods_mm(self, i, jb, ring, psM):
        S = self.S
        slot = jb % 2

        def mm(e, jb=jb, slot=slot):
            last = None
            for jj in range(4):
                j = jb * 4 + jj
                for k in range(NCH):
                    last = e.matmul(psM[:, j * 2:(j + 1) * 2], ring[slot][:, k, jj * 128:(jj + 1) * 128],
                                    self.scT[:, k, :], start=(k == 0), stop=(k == NCH - 1))
            return last
        S.op("pe", mm, reads=[("adaw", slot), "scT"], writes=["psM"])

    def mods_finish(self, i, psM):
        S = self.S
        modT, gs1, gs2 = self.modTs[i % 2], self.gs1s[i % 2], self.gs2s[i % 2]
        tok = ("modT", i % 2)
        psv = psM[:].rearrange("p (j g) -> p j g", g=2)
        for g in range(2):
            S.op("dve", lambda e, g=g: e.tensor_tensor(modT[:, :, g], psv[:, :, g], self.adab[:, i, :], ALU.add),
                 reads=["psM", "adab"], writes=[tok])
        for g in range(2):
            S.op("dve", lambda e, g=g: e.scalar_tensor_tensor(gs1[:, :, g], modT[:, 8:16, g], 1.0,
                                                              self.nm[:, i, :], ALU.add, ALU.mult),
                 reads=[tok, "nm"], writes=[tok])
            S.op("dve", lambda e, g=g: e.scalar_tensor_tensor(gs2[:, :, g], modT[:, 32:40, g], 1.0,
                                                              self.nf[:, i, :], ALU.add, ALU.mult),
                 reads=[tok, "nf"], writes=[tok])

    def mods_phase(self, i):
        nc, S = self.nc, self.S
        with ExitStack() as st:
            ring = [st.enter_context(nc.sbuf_tensor(f"adaw{s}_{i}", [128, NCH, 512], BF16)) for s in range(2)]
            psM = st.enter_context(nc.psum_tensor(f"psM_{i}", [128, 96], F32))
            for jb in range(12):
                self.mods_block(i, jb, ring, psM)
                self.mods_mm(i, jb, ring, psM)
            self.mods_finish(i, psM)
            S.flush()

    def norm(self, st, gs, sh_base, ntile, tag):
        nc, S = self.nc, self.S
        modT = self.modT
        sq = [st.enter_context(nc.sbuf_tensor(f"sq{s}_{tag}", [128, NCH, 512], BF16)) for s in range(2)]
        tmp = [st.enter_context(nc.sbuf_tensor(f"ntmp{s}_{tag}", [128, NCH, 512], F32)) for s in range(1)]
        rstd = [st.enter_context(nc.sbuf_tensor(f"rstd{s}_{tag}", [128, 512], F32)) for s in range(2)]
        psN = [st.enter_context(nc.psum_tensor(f"psN{s}_{tag}", [128, 512], F32)) for s in range(2)]

        def stats(t):
            t0, n = TT[t]
            s2 = t % 2
            hs = [("h", c, t) for c in range(NCH)]
            S.op("dve", lambda e, t0=t0, n=n, s2=s2: e.tensor_tensor(sq[s2][:, :, 0:n], self.h[:, :, t0:t0 + n], self.h[:, :, t0:t0 + n], ALU.mult),
                 reads=hs, writes=[("sq", s2)])

            def mm(e, n=n, s2=s2):
                last = None
                for k in range(NCH):
                    last = e.matmul(psN[s2][:, 0:n], self.ones[:], sq[s2][:, k, 0:n], start=(k == 0), stop=(k == NCH - 1))
                return last
            S.op("pe", mm, reads=[("sq", s2), "ones"], writes=[("psN", s2)])
            S.op("act", lambda e, n=n, s2=s2: e.activation(out=rstd[s2][:, 0:n], in_=psN[s2][:, 0:n], func=AF.Sqrt,
                                                           bias=self.epsc[:, 0:1], scale=1.0 / D),
                 reads=[("psN", s2), "epsc"], writes=[("rstd", s2)])
            S.op("dve", lambda e, n=n, s2=s2: e.reciprocal(rstd[s2][:, 0:n], rstd[s2][:, 0:n]),
                 reads=[("rstd", s2)], writes=[("rstd", s2)])

        def apply(t):
            t0, n = TT[t]
            g = 0 if t < 4 else 1
            s2 = t % 2
            for c in range(NCH):
                S.op("dve" if c % 2 == 0 else "pool", lambda e, c=c, t0=t0, n=n, s2=s2: e.tensor_tensor(
                    tmp[0][:, c, 0:n], self.h[:, c, t0:t0 + n], rstd[s2][:, 0:n], ALU.mult),
                    reads=[("h", c, t), ("rstd", s2)], writes=[("ntmp", c)])
                S.op("act", lambda e, c=c, t0=t0, n=n, g=g: e.activation(
                    out=self.a[:, c, t0:t0 + n], in_=tmp[0][:, c, 0:n], func=AF.Identity,
                    bias=modT[:, sh_base + c, g:g + 1], scale=gs[:, c, g:g + 1]),
                    reads=[("ntmp", c), self.mtok], writes=[("a", c, t)])
        stats(0)
        for t in range(ntile):
            if t + 1 < ntile:
                stats(t + 1)
            apply(t)

    def ffn_phase(self, i, ntile):
        nc, S, dr = self.nc, self.S, self.dr
        GC = 2
        NG = HID // 128 // GC
        with ExitStack() as st0:
            self.norm(st0, self.gs2, 24, ntile, f"f{i}")
            S.flush()
        with ExitStack() as st:
            wa = [st.enter_context(nc.sbuf_tensor(f"wa{s}_{i}", [128, NCH, GC * 128], BF16)) for s in range(2)]
            wg = [st.enter_context(nc.sbuf_tensor(f"wg{s}_{i}", [128, NCH, GC * 128], BF16)) for s in range(2)]
            wo = [st.enter_context(nc.sbuf_tensor(f"wo{s}_{i}", [128, GC, D], BF16)) for s in range(2)]
            ug = [st.enter_context(nc.sbuf_tensor(f"ug{s}_{i}", [128, GC, T], BF16)) for s in range(2)]
            su = [st.enter_context(nc.sbuf_tensor(f"su{s}_{i}", [128, 512], F32)) for s in range(2)]
            psA = [st.enter_context(nc.psum_tensor(f"psA{s}_{i}", [128, 512], F32)) for s in range(2)]
            psG = [st.enter_context(nc.psum_tensor(f"psG{s}_{i}", [128, 512], F32)) for s in range(2)]
            psY = [st.enter_context(nc.psum_tensor(f"psY{s}_{i}", [128, 512], F32)) for s in range(2)]
            nxt = self.next_layer
            if nxt is not None:
                aring = [st.enter_context(nc.sbuf_tensor(f"adawf{s}_{i}", [128, NCH, 512], BF16)) for s in range(2)]
                psMf = st.enter_context(nc.psum_tensor(f"psMf_{i}", [128, 96], F32))
                self.mods_block(nxt, 0, aring, psMf)
            win = self.I("ffn_w_in")
            wout = self.I("ffn_w_out")
            cnt = 0
            ycnt = 0
            for G in range(NG):
                s = G % 2
                c0 = G * GC * 128
                if nxt is not None:
                    if G + 1 < 12:
                        self.mods_block(nxt, G + 1, aring, psMf)
                    self.mods_mm(nxt, G, aring, psMf)
                S.dma("pool", wa[s][:], win[i, :, c0:c0 + GC * 128].rearrange("(k p) c -> p k c", p=128), writes=[("wa", s)])
                S.dma("pool", wg[s][:], win[i, :, HID + c0:HID + c0 + GC * 128].rearrange("(k p) c -> p k c", p=128), writes=[("wg", s)])
                S.dma("pool", wo[s][:], wout[i, c0:c0 + GC * 128, :].rearrange("(j p) o -> p j o", p=128), writes=[("wo", s)])
                for t in range(ntile):
                    t0, n = TT[t]
                    for j in range(GC):
                        ps = cnt % 2
                        cnt += 1

                        def mm(e, s=s, j=j, t0=t0, n=n, ps=ps):
                            last = None
                            for k in range(NCH):
                                e.matmul(psA[ps][:, 0:n], wa[s][:, k, j * 128:(j + 1) * 128], self.a[:, k, t0:t0 + n],
                                         start=(k == 0), stop=(k == NCH - 1))
                            for k in range(NCH):
                                last = e.matmul(psG[ps][:, 0:n], wg[s][:, k, j * 128:(j + 1) * 128], self.a[:, k, t0:t0 + n],
                                                start=(k == 0), stop=(k == NCH - 1))
                            return last
                        S.op("pe", mm, reads=[("wa", s), ("wg", s)] + [("a", c, t) for c in range(NCH)],
                             writes=[("psA", ps), ("psG", ps)])
                        S.op("act", lambda e, ps=ps, n=n: e.activation(out=su[ps][:, 0:n], in_=psA[ps][:, 0:n], func=AF.Silu),
                             reads=[("psA", ps)], writes=[("su", ps)])
                        S.op("dve", lambda e, ps=ps, n=n, s=s, j=j, t0=t0: e.tensor_tensor(
                            ug[s][:, j, t0:t0 + n], psG[ps][:, 0:n], su[ps][:, 0:n], ALU.mult),
                            reads=[("psG", ps), ("su", ps)], writes=[("ug", s, t)])
                for t in range(ntile):
                    t0, n = TT[t]
                    g = 0 if t < 4 else 1
                    for o in range(NCH):
                        py = ycnt % 2
                        ycnt += 1

                        def mm2(e, s=s, o=o, t0=t0, n=n, py=py):
                            last = None
                            for j in range(GC):
                                last = e.matmul(psY[py][:, 0:n], wo[s][:, j, o * 128:(o + 1) * 128], ug[s][:, j, t0:t0 + n],
                                                start=(j == 0), stop=(j == GC - 1))
                            return last
                        S.op("pe", mm2, reads=[("wo", s), ("ug", s, t)], writes=[("psY", py)])
                        S.op("dve", lambda e, o=o, t0=t0, n=n, py=py, g=g: e.scalar_tensor_tensor(
                            self.h[:, o, t0:t0 + n], psY[py][:, 0:n], self.modT[:, 40 + o, g:g + 1], self.h[:, o, t0:t0 + n],
                            ALU.mult, ALU.add),
                            reads=[("psY", py), ("h", o, t), self.mtok], writes=[("h", o, t)])
            if nxt is not None:
                self.mods_mm(nxt, 11, aring, psMf)
                self.mods_finish(nxt, psMf)
            S.flush()

    def final_phase(self):
        nc, S, dr = self.nc, self.S, self.dr
        with ExitStack() as st:
            sq = [st.enter_context(nc.sbuf_tensor(f"fsq{s}", [128, NCH, 512], BF16)) for s in range(2)]
            rstd = [st.enter_context(nc.sbuf_tensor(f"frstd{s}", [128, 512], F32)) for s in range(2)]
            psN = [st.enter_context(nc.psum_tensor(f"fpsN{s}", [128, 512], F32)) for s in range(2)]
            for t in range(4):
                t0, n = TT[t]
                s2 = t % 2
                hs = [("h", c, t) for c in range(NCH)]
                S.op("act", lambda e, t0=t0, n=n, s2=s2: e.activation(out=sq[s2][:, :, 0:n], in_=self.h[:, :, t0:t0 + n], func=AF.Square),
                     reads=hs, writes=[("sq", s2)])

                def mm(e, n=n, s2=s2):
                    last = None
                    for k in range(NCH):
                        last = e.matmul(psN[s2][:, 0:n], self.ones[:], sq[s2][:, k, 0:n], start=(k == 0), stop=(k == NCH - 1))
                    return last
                S.op("pe", mm, reads=[("sq", s2), "ones"], writes=[("psN", s2)])
                S.op("act", lambda e, n=n, s2=s2: e.activation(out=rstd[s2][:, 0:n], in_=psN[s2][:, 0:n], func=AF.Sqrt,
                                                               bias=self.epsc[:, 0:1], scale=1.0 / D),
                     reads=[("psN", s2), "epsc"], writes=[("rstd", s2)])
                S.op("dve", lambda e, n=n, s2=s2: e.reciprocal(rstd[s2][:, 0:n], rstd[s2][:, 0:n]),
                     reads=[("rstd", s2)], writes=[("rstd", s2)])
                for c in range(NCH):
                    S.op("dve", lambda e, c=c, t0=t0, n=n, s2=s2: e.scalar_tensor_tensor(
                        self.h[:, c, t0:t0 + n], self.h[:, c, t0:t0 + n], self.nfin[:, c:c + 1], rstd[s2][:, 0:n], ALU.mult, ALU.mult),
                        reads=[("h", c, t), ("rstd", s2), "nfin"], writes=[("h", c, t)])
            for c in range(NCH):
                S.dma("sp", dr["y"][c * 128:(c + 1) * 128, :], self.h[:, c, 0:S_LAT],
                      reads=[("h", c, t) for t in range(4)], is_output=True)

    def attn_phase(self, i, kind):
        nc, S = self.nc, self.S
        with ExitStack() as st0:
            self.norm(st0, self.gs1, 0, 5, f"m{i}")
            S.flush()
        rope = kind in ("gqa", "swa")
        if kind == "na":
            NP, NG, QB, VW = 1, 8, 128, 128
            wqkv, wo_d = self.I("na_w_qkv"), self.I("na_w_o")
            qcol = lambda g: g * 128
            kcol = lambda g: 1024 + g * 128
            vcol = lambda g: 2048 + g * 128
        elif kind == "gqa":
            NP, NG, QB, VW = 2, 4, 256, 64
            wqkv, wqkvp, wo_d = self.I("gqa_w_qkv"), self.I("gqa_w_qkv_p"), self.I("gqa_w_o")
            qcol = lambda g: g * 256
            kcol = lambda g: 1024 + g * 64
            vcol = lambda g: 1280 + g * 64
        else:
            NP, NG, QB, VW = 2, 4, 128, 64
            wqkv, wqkvp, wo_d = self.I("swa_w_qkv"), self.I("swa_w_qkv_p"), self.I("swa_w_o")
            qcol = lambda g: g * 256
            kcol = lambda g: 1024 + (g // 2) * 64
            vcol = lambda g: 1152 + (g // 2) * 64
        N = NP * QB
        GM = 512 // N
        with ExitStack() as st:
            sb = lambda name, shape, dt: st.enter_context(nc.sbuf_tensor(f"{name}_{i}", list(shape), dt))
            ps = lambda name: st.enter_context(nc.psum_tensor(f"{name}_{i}", [128, 512], F32))
            wq = [sb(f"wq{x}", [128, NCH, NP * 128], BF16) for x in range(2)]
            wk = [sb(f"wk{x}", [128, NCH, 128], BF16) for x in range(2)]
            wv = [sb(f"wv{x}", [128, NCH, VW], BF16) for x in range(2)]
            wo = [sb(f"awo{x}", [128, NP, D], BF16) for x in range(2)]
            if rope:
                wqp = [sb(f"wqp{x}", [128, NCH, NP * 128], BF16) for x in range(2)]
                wkp = [sb(f"wkp{x}", [128, NCH, 128], BF16) for x in range(2)]
                cs = sb("cs", [128, 2, T], F32)
                S.dma("sp", cs[:], self.I("rope_cs")[:, :, :], writes=["cs"])
                t1 = [sb(f"t1{x}", [128, 512], F32) for x in range(2)]
                t2 = [sb(f"t2{x}", [128, 512], F32) for x in range(1)] * 2
            if kind == "gqa":
                gains = sb("gains", [128, 4], F32)
                S.dma("sp", gains[:], self.I("gqa_gains")[:, :], writes=["gains"])
                sqh = [sb(f"sqh{x}", [128, 512], BF16) for x in range(1)] * 2
                rs = [sb(f"rs{x}", [128, 512], F32) for x in range(1)] * 2
                bd = sb("bd", [128, 128], BF16)
                S.op("dve", lambda e: e.memset(bd[:], 0.0), writes=["bd"])
                S.op("dve", lambda e: e.memset(bd[0:64, 0:64], 1.0), writes=["bd"])
                S.op("dve", lambda e: e.memset(bd[64:128, 64:128], 1.0), writes=["bd"])
            if kind == "swa":
                sinkx = sb("sinkx", [128, 8], F32)
                S.dma("sp", sinkx[:], self.I("swa_sinks")[:, :], writes=["sinkx"])
                S.op("act", lambda e: e.activation(out=sinkx[:], in_=sinkx[:], func=AF.Exp), reads=["sinkx"], writes=["sinkx"])
                masks = sb("masks", [128, 2, 128], BF16)
                S.dma("pool", masks[:], self.I("swa_masks")[:, :, :], writes=["masks"])
            if kind == "na":
                stg = sb("stg", [128, 15, 64], F32)
                E2 = sb("E2", [128, 15, 64], BF16)
                Fm = sb("Fm", [128, 2, 2, 25, 128], BF16)
            qg = sb("qg", [128, NP, T], BF16)
            kT = sb("kT", [128, T], BF16)
            V = sb("V", [128, 18, VW], BF16)
            Og = sb("Og", [128, NP, T], BF16)
            P = [sb(f"P{x}", [128, 1024], BF16) for x in range(3)]
            rec = [sb(f"rec{x}", [128, 512], F32) for x in range(2)]
            psS = [st.enter_context(nc.psum_tensor(f"psS{x}_{i}", [128, 1024], F32)) for x in range(2)]
            psO = [ps(f"psO{x}") for x in range(2)]
            psD = [ps(f"psD{x}") for x in range(2)]
            psP = psS
            cntP = [0]
            evtog = [0]

            def r0(qr):
                return min(max(qr - 4, 0), 24)

            def tw0(R):
                return min(max(R - 2, 0), 11)
            VAR_R = [0, 1, 2, 14, 15]

            def var_of(R):
                return {0: 0, 1: 1, 14: 3, 15: 4}.get(R, 2)

            def proj(wt, wtp, col0, dst_fn, is_q, t, wtok, dtok):
                t0, n = TT[t]
                pp = cntP[0] % 2
                cntP[0] += 1
                areads = [("a", c, t) for c in range(NCH)]

                def mm(e, pp=pp, t0=t0, n=n):
                    last = None
                    for k in range(NCH):
                        last = e.matmul(psP[pp][:, 0:n], wt[:, k, col0:col0 + 128], self.a[:, k, t0:t0 + n],
                                        start=(k == 0), stop=(k == NCH - 1))
                    return last
                S.op("pe", mm, reads=areads + wtok, writes=[("psS", pp)])
                dst = dst_fn(t0, n)
                if not rope:
                    evtog[0] += 1
                    if evtog[0] % 2 == 0:
                        S.op("act", lambda e, pp=pp, n=n: e.activation(out=dst, in_=psP[pp][:, 0:n], func=AF.Copy),
                             reads=[("psS", pp)], writes=[dtok])
                    else:
                        S.op("dve", lambda e, pp=pp, n=n: e.tensor_copy(dst, psP[pp][:, 0:n]),
                             reads=[("psS", pp)], writes=[dtok])
                    return
                pq = cntP[0] % 2
                cntP[0] += 1

                def mm2(e, pq=pq, t0=t0, n=n):
                    last = None
                    for k in range(NCH):
                        last = e.matmul(psP[pq][:, 0:n], wtp[:, k, col0:col0 + 128], self.a[:, k, t0:t0 + n],
                                        start=(k == 0), stop=(k == NCH - 1))
                    return last
                S.op("pe", mm2, reads=areads + wtok, writes=[("psS", pq)])
                x = evtog[0] % 2
                evtog[0] += 1
                if kind == "gqa":
                    gc = 0 if is_q else 2
                    S.op("act", lambda e, pp=pp, n=n, x=x: e.activation(out=sqh[x][:, 0:n], in_=psP[pp][:, 0:n], func=AF.Square),
                         reads=[("psS", pp)], writes=[("sqh", 0)])
                    S.op("dve", lambda e, pp=pp, n=n, x=x, t0=t0, gc=gc: e.scalar_tensor_tensor(
                        t1[x][:, 0:n], psP[pp][:, 0:n], gains[:, gc:gc + 1], cs[:, 0, t0:t0 + n], ALU.mult, ALU.mult),
                        reads=[("psS", pp), "gains", "cs"], writes=[("t1", x)])
                    S.op("dve", lambda e, pq=pq, n=n, x=x, t0=t0, gc=gc: e.scalar_tensor_tensor(
                        t2[x][:, 0:n], psP[pq][:, 0:n], gains[:, gc + 1:gc + 2], cs[:, 1, t0:t0 + n], ALU.mult, ALU.mult),
                        reads=[("psS", pq), "gains", "cs"], writes=[("t2", 0)])
                    S.op("pe", lambda e, pp=pp, n=n, x=x: e.matmul(psP[pp][:, 0:n], bd[:], sqh[x][:, 0:n], start=True, stop=True),
                         reads=[("sqh", 0), "bd"], writes=[("psS", pp)])
                    S.op("act", lambda e, pp=pp, n=n, x=x: e.activation(out=rs[x][:, 0:n], in_=psP[pp][:, 0:n], func=AF.Sqrt,
                                                                       bias=self.epsc[:, 0:1], scale=1.0 / 64),
                         reads=[("psS", pp), "epsc"], writes=[("rs", 0)])
                    S.op("dve", lambda e, n=n, x=x: e.reciprocal(rs[x][:, 0:n], rs[x][:, 0:n]), reads=[("rs", 0)], writes=[("rs", 0)])
                    S.op("pool", lambda e, n=n, x=x: e.tensor_tensor(t1[x][:, 0:n], t1[x][:, 0:n], t2[x][:, 0:n], ALU.add),
                         reads=[("t1", x), ("t2", 0)], writes=[("t1", x)])
                    S.op("pool", lambda e, n=n, x=x: e.tensor_tensor(dst, t1[x][:, 0:n], rs[x][:, 0:n], ALU.mult),
                         reads=[("t1", x), ("rs", 0)], writes=[dtok])
                else:
                    S.op("dve", lambda e, pp=pp, n=n, x=x, t0=t0: e.tensor_tensor(t1[x][:, 0:n], psP[pp][:, 0:n], cs[:, 0, t0:t0 + n], ALU.mult),
                         reads=[("psS", pp), "cs"], writes=[("t1", x)])
                    S.op("dve", lambda e, pq=pq, n=n, x=x, t0=t0: e.tensor_tensor(t2[x][:, 0:n], psP[pq][:, 0:n], cs[:, 1, t0:t0 + n], ALU.mult),
                         reads=[("psS", pq), "cs"], writes=[("t2", 0)])
                    S.op("pool", lambda e, n=n, x=x: e.tensor_tensor(dst, t1[x][:, 0:n], t2[x][:, 0:n], ALU.add),
                         reads=[("t1", x), ("t2", 0)], writes=[dtok])

            def build_F(g):
                fp = g % 2
                for hh in range(2):
                    hd = g * 2 + hh
                    S.dma("sp", stg[0:64], self.I("nab")[hd], writes=["stg"])
                    S.dma("sp", stg[64:128], self.I("nab")[hd], writes=["stg"])
                    S.op("act", lambda e: e.activation(out=E2[:], in_=stg[:], func=AF.Exp), reads=["stg"], writes=["E2"])
                    S.op("pool", lambda e, hh=hh, fp=fp: e.memset(Fm[:, fp, hh], 0.0), writes=[("F", fp, hh)])
                    for v in range(5):
                        R = VAR_R[v]
                        for krp in range(2):
                            for qrp in range(2):
                                qr = 2 * R + qrp
                                ws_ = [w for w in range(5) if r0(qr) <= 2 * (tw0(R) + w) + krp < r0(qr) + 8]
                                if not ws_:
                                    continue
                                w_lo, nw = ws_[0], len(ws_)
                                assert ws_ == list(range(w_lo, w_lo + nw))
                                d0 = 2 * (tw0(R) + w_lo) + krp - qr + 7
                                S.op("pool", lambda e, hh=hh, fp=fp, v=v, krp=krp, qrp=qrp, w_lo=w_lo, nw=nw, d0=d0: e.tensor_copy(
                                    Fm[krp * 64:(krp + 1) * 64, fp, hh, v * 5 + w_lo:v * 5 + w_lo + nw, qrp * 64:(qrp + 1) * 64],
                                    E2[krp * 64:(krp + 1) * 64, d0:d0 + 2 * nw - 1:2, :]),
                                    reads=["E2"], writes=[("F", fp, hh)])

            def load_weights(g):
                sl = g % 2
                wsrc = lambda col, n: wqkv[:, col:col + n].rearrange("(k p) c -> p k c", p=128)
                wsrcp = lambda col, n: wqkvp[:, col:col + n].rearrange("(k p) c -> p k c", p=128)
                S.dma("pool", wq[sl][:], wsrc(qcol(g), NP * 128), writes=[("wq", sl)])
                if rope:
                    S.dma("pool", wqp[sl][:], wsrcp(qcol(g), NP * 128), writes=[("wq", sl)])
                if kind == "na":
                    S.dma("pool", wk[sl][:], wsrc(kcol(g), 128), writes=[("wk", sl)])
                else:
                    for half in range(2):
                        S.dma("pool", wk[sl][:, :, half * 64:(half + 1) * 64], wsrc(kcol(g), 64), writes=[("wk", sl)])
                        S.dma("pool", wkp[sl][:, :, half * 64:(half + 1) * 64], wsrcp(kcol(g), 64), writes=[("wk", sl)])
                S.dma("pool", wv[sl][:], wsrc(vcol(g), VW), writes=[("wk", sl)])
                S.dma("pool", wo[sl][:], wo_d[g * NP * 128:(g + 1) * NP * 128, :].rearrange("(p r) o -> r p o", r=128), writes=[("wo", sl)])

            def run_group(g):
                sl = g % 2
                for p in range(NP):
                    for t in range(5):
                        proj(wq[sl], wqp[sl] if rope else None, p * 128, lambda t0, n, p=p: qg[:, p, t0:t0 + n], True, t, [("wq", sl)], "qg")
                new_kv = (kind != "swa") or (g % 2 == 0)
                if new_kv:
                    for t in range(5):
                        proj(wk[sl], wkp[sl] if rope else None, 0, lambda t0, n: kT[:, t0:t0 + n], False, t, [("wk", sl)], "kT")
                    for tk in range(18):
                        pp = cntP[0] % 2
                        cntP[0] += 1
                        tt_ = min(tk // 4, 4)

                        def mmv(e, tk=tk, pp=pp):
                            last = None
                            for k in range(NCH):
                                last = e.matmul(psP[pp][:, 0:VW], self.a[:, k, tk * 128:(tk + 1) * 128], wv[sl][:, k, :],
                                                start=(k == 0), stop=(k == NCH - 1))
                            return last
                        S.op("pe", mmv, reads=[("a", c, tt_) for c in range(NCH)] + [("wk", sl)], writes=[("psS", pp)])
                        S.op("act", lambda e, tk=tk, pp=pp: e.activation(out=V[:, tk, :], in_=psP[pp][:, 0:VW], func=AF.Copy),
                             reads=[("psS", pp)], writes=["V"])
                nlat = S_LAT // QB
                nctx = L_CTX // QB
                steps = []
                for qb in range(nlat + nctx):
                    if qb >= nlat:
                        kl = [(16, None), (17, None)]
                    elif kind == "na":
                        kl = [(tw0(qb) + w, ("F", var_of(qb) * 5 + w)) for w in range(5)] + [(16, None), (17, None)]
                    elif kind == "gqa":
                        kl = [(tk, None) for tk in range(18)]
                    else:
                        kl = []
                        if qb >= 1:
                            kl.append((qb - 1, ("M", 0)))
                        kl.append((qb, None))
                        if qb <= 14:
                            kl.append((qb + 1, ("M", 1)))
                        kl += [(16, None), (17, None)]
                    for idx, (tk, fm) in enumerate(kl):
                        steps.append((qb, tk, fm, idx == 0, idx == len(kl) - 1))
                msteps = []
                for st_ in steps:
                    if msteps and msteps[-1][0][0] == st_[0] and len(msteps[-1]) < GM:
                        msteps[-1].append(st_)
                    else:
                        msteps.append([st_])
                vcs = (slice(0, 64), slice(64, 128)) if kind == "na" else (slice(0, 64), slice(0, 64))

                def qk(midx):
                    ms = msteps[midx]
                    r = midx % 2
                    q0 = ms[0][0] * QB

                    def f(e, ms=ms, r=r, q0=q0):
                        last_ = None
                        for x, (_, tk, _, _, _) in enumerate(ms):
                            for sd in range(2):
                                pl = slice(sd * 64, (sd + 1) * 64)
                                last_ = e.matmul(psS[r][:, sd * 512 + x * N:sd * 512 + (x + 1) * N], kT[pl, tk * 128:(tk + 1) * 128],
                                                 qg[pl, 0:NP, q0:q0 + QB], start=True, stop=True)
                        return last_
                    S.op("pe", f, reads=["kT", "qg"], writes=[("psS", r)])

                def rest(midx):
                    ms = msteps[midx]
                    r = midx % 2
                    pr = midx % 3
                    qb = ms[0][0]
                    q0 = qb * QB
                    ob = qb % 2
                    W = len(ms) * N
                    S.op("act", lambda e, r=r, pr=pr, W=W: e.activation(
                        out=P[pr][:].rearrange("p (s c) -> p s c", s=2)[:, :, 0:W],
                        in_=psS[r][:].rearrange("p (s c) -> p s c", s=2)[:, :, 0:W], func=AF.Exp, scale=0.125),
                        reads=[("psS", r)], writes=[("P", pr)])
                    fms = [(x, st_[2]) for x, st_ in enumerate(ms) if st_[2] is not None]
                    if fms and fms[0][1][0] == "F":
                        x0, nf_, fi0 = fms[0][0], len(fms), fms[0][1][1]
                        for sd in range(2):
                            S.op("dve" if sd == 0 else "pool", lambda e, pr=pr, x0=x0, nf_=nf_, fi0=fi0, sd=sd: e.tensor_tensor(
                                P[pr][:, sd * 512 + x0 * 128:sd * 512 + (x0 + nf_) * 128].rearrange("p (a b) -> p a b", a=nf_),
                                P[pr][:, sd * 512 + x0 * 128:sd * 512 + (x0 + nf_) * 128].rearrange("p (a b) -> p a b", a=nf_),
                                Fm[:, g % 2, sd, fi0:fi0 + nf_, :], ALU.mult),
                                reads=[("P", pr), ("F", g % 2, sd)], writes=[("P", pr)])
                    elif fms:
                        def mk(e, pr=pr, fms=fms):
                            last_ = None
                            for (x, fm) in fms:
                                for sd in range(2):
                                    for hx in range(NP):
                                        b0 = sd * 512 + x * N + hx * QB
                                        sl_ = P[pr][:, b0:b0 + 128]
                                        last_ = e.tensor_tensor(sl_, sl_, masks[:, fm[1], :], ALU.mult)
                            return last_
                        S.op("pool", mk, reads=[("P", pr), "masks"], writes=[("P", pr)])

                    def pv(e, ms=ms, pr=pr, ob=ob):
                        last_ = None
                        for x, (_, tk, _, first, last) in enumerate(ms):
                            for sd in range(2):
                                pl = slice(sd * 64, (sd + 1) * 64)
                                e.matmul(psO[ob][pl, 0:N], V[:, tk, vcs[sd]], P[pr][:, sd * 512 + x * N:sd * 512 + (x + 1) * N], start=first, stop=last)
                            for sd in range(2):
                                pl = slice(sd * 64, (sd + 1) * 64)
                                last_ = e.matmul(psD[ob][pl, 0:N], self.ones[:, 0:64], P[pr][:, sd * 512 + x * N:sd * 512 + (x + 1) * N], start=first, stop=last)
                        return last_
                    S.op("pe", pv, reads=[("P", pr), "V", "ones"], writes=[("psO", ob)])
                    if ms[-1][4]:
                        if kind == "swa":
                            def addsink(e, ob=ob):
                                last_ = None
                                for hx in range(NP):
                                    pair = g * NP + hx
                                    last_ = e.tensor_scalar(rec[ob][:, hx * QB:(hx + 1) * QB], psD[ob][:, hx * QB:(hx + 1) * QB],
                                                            sinkx[:, pair:pair + 1], None, ALU.add)
                                return last_
                            S.op("dve", addsink, reads=[("psO", ob), "sinkx"], writes=[("rec", ob)])
                            S.op("dve", lambda e, ob=ob: e.reciprocal(rec[ob][:, 0:N], rec[ob][:, 0:N]),
                                 reads=[("rec", ob)], writes=[("rec", ob)])
                        else:
                            S.op("dve", lambda e, ob=ob: e.reciprocal(rec[ob][:, 0:N], psD[ob][:, 0:N]),
                                 reads=[("psO", ob)], writes=[("rec", ob)])
                        S.op("dve", lambda e, ob=ob, q0=q0: e.tensor_tensor(
                            Og[:, 0:NP, q0:q0 + QB], psO[ob][:, 0:N].rearrange("p (a b) -> p a b", a=NP),
                            rec[ob][:, 0:N].rearrange("p (a b) -> p a b", a=NP), ALU.mult),
                            reads=[("psO", ob), ("rec", ob)], writes=["Og"])
                qk(0)
                for midx in range(len(msteps)):
                    if midx + 1 < len(msteps):
                        qk(midx + 1)
                    rest(midx)
                for t in range(5):
                    t0, n = TT[t]
                    gi = 0 if t < 4 else 1
                    for o in range(NCH):
                        pp = cntP[0] % 2
                        cntP[0] += 1

                        def mmo(e, o=o, t0=t0, n=n, pp=pp):
                            last = None
                            for p in range(NP):
                                last = e.matmul(psP[pp][:, 0:n], wo[sl][:, p, o * 128:(o + 1) * 128], Og[:, p, t0:t0 + n],
                                                start=(p == 0), stop=(p == NP - 1))
                            return last
                        S.op("pe", mmo, reads=[("wo", sl), "Og"], writes=[("psS", pp)])
                        S.op("dve", lambda e, o=o, t0=t0, n=n, pp=pp, gi=gi: e.scalar_tensor_tensor(
                            self.h[:, o, t0:t0 + n], psP[pp][:, 0:n], self.modT[:, 16 + o, gi:gi + 1], self.h[:, o, t0:t0 + n],
                            ALU.mult, ALU.add),
                            reads=[("psS", pp), ("h", o, t), self.mtok], writes=[("h", o, t)])

            load_weights(0)
            if kind == "na":
                build_F(0)
            for g in range(NG):
                if g + 1 < NG:
                    load_weights(g + 1)
                    if kind == "na":
                        build_F(g + 1)
                run_group(g)
            S.flush()

    def lru_phase(self, i):
        nc, S = self.nc, self.S
        with ExitStack() as st0:
            self.norm(st0, self.gs1, 0, 5, f"m{i}")
            S.flush()
        with ExitStack() as stz:
            z = stz.enter_context(nc.sbuf_tensor(f"z_{i}", [128, NCH, S_LAT], BF16))
            with ExitStack() as st:
                sb = lambda name, shape, dt: st.enter_context(nc.sbuf_tensor(f"{name}_{i}", list(shape), dt))
                ps = lambda name: st.enter_context(nc.psum_tensor(f"{name}_{i}", [128, 512], F32))
                vec = sb("lvec", [128, 12, NCH], F32)
                nsp = sb("nsp", [128, 2, NCH], F32)
                win = [sb(f"lwin{x}", [128, NCH, 256], BF16) for x in range(1)] * 2
                Wbd = sb("Wbd", [128, 2, 2, 128], BF16)
                A = sb("A", [128, T], F32)
                xc = sb("xc", [128, T], F32)
                xb = sb("xb", [128, T], BF16)
                gl = sb("gl", [128, S_LAT], BF16)
                Bd = sb("Bd", [128, T], F32)
                Hf = sb("Hf", [128, T], F32)
                Hb = sb("Hb", [128, T], F32)
                tm = [sb(f"tm{x}", [128, 512], F32) for x in range(2)]
                psX = [ps(f"psX{x}") for x in range(2)]
                psR = [ps(f"psR{x}") for x in range(2)]
                psI = [ps(f"psI{x}") for x in range(2)]
                S.dma("sp", vec[:], self.I("lru_vecs")[:, :, :], writes=["lvec"])
                S.op("act", lambda e: e.activation(out=nsp[:], in_=vec[:, 9:11, :], func=AF.Exp, scale=-1.0), reads=["lvec"], writes=["nsp"])
                S.op("act", lambda e: e.activation(out=nsp[:], in_=nsp[:], func=AF.Ln, bias=1.0, scale=1.0), reads=["nsp"], writes=["nsp"])
                S.op("dve", lambda e: e.tensor_scalar(nsp[:], nsp[:], -8.0, None, ALU.mult), reads=["nsp"], writes=["nsp"])
                S.op("pool", lambda e: e.memset(Wbd[:], 0.0), writes=["Wbd"])
                w_in = self.I("lru_w_in")
                wa_d, wx_d = self.I("lru_w_a"), self.I("lru_w_x")
                cx = [0]
                for c in range(NCH):
                    wsl = 0
                    S.dma("pool", win[wsl][:, :, 0:128], w_in[:, c * 128:(c + 1) * 128].rearrange("(k p) c -> p k c", p=128), writes=[("lwin", wsl)])
                    S.dma("pool", win[wsl][:, :, 128:256], w_in[:, D + c * 128:D + (c + 1) * 128].rearrange("(k p) c -> p k c", p=128), writes=[("lwin", wsl)])
                    for d in range(2):
                        for gt, wd in enumerate((wa_d, wx_d)):
                            for half in range(2):
                                S.dma("pool", Wbd[half * 64:(half + 1) * 64, d, gt, half * 64:(half + 1) * 64], wd[d, 2 * c + half], writes=["Wbd"])
                    for t in range(5):
                        t0, n = TT[t]
                        px = cx[0] % 2
                        cx[0] += 1

                        def mm(e, t0=t0, n=n, px=px, wsl=wsl):
                            last = None
                            for k in range(NCH):
                                last = e.matmul(psX[px][:, 0:n], win[wsl][:, k, 0:128], self.a[:, k, t0:t0 + n], start=(k == 0), stop=(k == NCH - 1))
                            return last
                        S.op("pe", mm, reads=[("lwin", wsl)] + [("a", k, t) for k in range(NCH)], writes=[("psX", px)])
                        S.op("act", lambda e, t0=t0, n=n, px=px: e.activation(out=A[:, t0:t0 + n], in_=psX[px][:, 0:n], func=AF.Copy),
                             reads=[("psX", px)], writes=["A"])
                    for t in range(4):
                        t0, n = TT[t]
                        px = cx[0] % 2
                        cx[0] += 1

                        def mmg(e, t0=t0, n=n, px=px, wsl=wsl):
                            last = None
                            for k in range(NCH):
                                last = e.matmul(psX[px][:, 0:n], win[wsl][:, k, 128:256], self.a[:, k, t0:t0 + n], start=(k == 0), stop=(k == NCH - 1))
                            return last
                        S.op("pe", mmg, reads=[("lwin", wsl)] + [("a", k, t) for k in range(NCH)], writes=[("psX", px)])
                        S.op("act", lambda e, t0=t0, n=n, px=px: e.activation(out=gl[:, t0:t0 + n], in_=psX[px][:, 0:n], func=AF.Gelu_apprx_tanh),
                             reads=[("psX", px)], writes=["gl"])
                    S.op("act", lambda e, c=c: e.activation(out=xc[:], in_=A[:], func=AF.Identity, bias=vec[:, 4, c:c + 1], scale=vec[:, 2, c:c + 1]),
                         reads=["A", "lvec"], writes=["xc"])
                    for (kk, sh) in ((0, 2), (1, 1), (3, -1)):
                        for (lo, hi) in ((0, S_LAT), (S_LAT, T)):
                            if sh > 0:
                                o0, o1, i0, i1 = lo + sh, hi, lo, hi - sh
                            else:
                                o0, o1, i0, i1 = lo, hi + sh, lo - sh, hi
                            S.op("dve", lambda e, c=c, kk=kk, o0=o0, o1=o1, i0=i0, i1=i1: e.scalar_tensor_tensor(
                                xc[:, o0:o1], A[:, i0:i1], vec[:, kk, c:c + 1], xc[:, o0:o1], ALU.mult, ALU.add),
                                reads=["A", "lvec", "xc"], writes=["xc"])
                    S.op("act", lambda e: e.activation(out=xb[:], in_=xc[:], func=AF.Copy), reads=["xc"], writes=["xb"])
                    for d in range(2):
                        Hd = Hf if d == 0 else Hb
                        for t in range(5):
                            t0, n = TT[t]
                            px = cx[0] % 2
                            cx[0] += 1
                            S.op("pe", lambda e, d=d, t0=t0, n=n, px=px: e.matmul(psR[px][:, 0:n], Wbd[:, d, 0, :], xb[:, t0:t0 + n], start=True, stop=True),
                                 reads=["Wbd", "xb"], writes=[("psR", px)])
                            S.op("pe", lambda e, d=d, t0=t0, n=n, px=px: e.matmul(psI[px][:, 0:n], Wbd[:, d, 1, :], xb[:, t0:t0 + n], start=True, stop=True),
                                 reads=["Wbd", "xb"], writes=[("psI", px)])
                            S.op("act", lambda e, d=d, c=c, t0=t0, n=n, px=px: e.activation(out=A[:, t0:t0 + n], in_=psR[px][:, 0:n], func=AF.Sigmoid,
                                                                                         bias=vec[:, 5 + d, c:c + 1], scale=1.0),
                                 reads=[("psR", px), "lvec", "xb"], writes=[("A", t)])
                            S.op("act", lambda e, d=d, c=c, t0=t0, n=n: e.activation(out=A[:, t0:t0 + n], in_=A[:, t0:t0 + n], func=AF.Exp,
                                                                                scale=nsp[:, d, c:c + 1]),
                                 reads=[("A", t), "nsp"], writes=[("A", t)])
                            S.op("act", lambda e, d=d, c=c, t0=t0, n=n, px=px: e.activation(out=Bd[:, t0:t0 + n], in_=psI[px][:, 0:n], func=AF.Sigmoid,
                                                                                         bias=vec[:, 7 + d, c:c + 1], scale=1.0),
                                 reads=[("psI", px), "lvec", "Hscan"], writes=[("Bd", t)])
                            S.op("dve", lambda e, t0=t0, n=n: e.tensor_tensor(Bd[:, t0:t0 + n], Bd[:, t0:t0 + n], xc[:, t0:t0 + n], ALU.mult),
                                 reads=[("Bd", t), "xc"], writes=[("Bd", t)])
                            S.op("pool", lambda e, t0=t0, n=n, px=px: e.tensor_tensor(tm[px][:, 0:n], A[:, t0:t0 + n], A[:, t0:t0 + n], ALU.mult),
                                 reads=[("A", t)], writes=[("tm", px)])
                            S.op("act", lambda e, n=n, px=px: e.activation(out=tm[px][:, 0:n], in_=tm[px][:, 0:n], func=AF.Sqrt, bias=1.0, scale=-1.0),
                                 reads=[("tm", px)], writes=[("tm", px)])
                            S.op("dve", lambda e, t0=t0, n=n, px=px: e.tensor_tensor(Bd[:, t0:t0 + n], Bd[:, t0:t0 + n], tm[px][:, 0:n], ALU.mult),
                                 reads=[("Bd", t), ("tm", px)], writes=[("Bd", t)])
                        allA = [("A", t) for t in range(5)]
                        allB = [("Bd", t) for t in range(5)]
                        if d == 0:
                            S.op("dve", lambda e: e.tensor_tensor_scan(Hf[:, S_LAT:T], A[:, S_LAT:T], Bd[:, S_LAT:T], 0.0, ALU.mult, ALU.add),
                                 reads=allA + allB, writes=["Hf"])
                            S.op("dve", lambda e: e.tensor_tensor_scan(Hf[:, 0:S_LAT], A[:, 0:S_LAT], Bd[:, 0:S_LAT], Hf[:, T - 1:T], ALU.mult, ALU.add),
                                 reads=allA + allB + ["Hf"], writes=["Hf", "Hscan"])
                        else:
                            S.op("dve", lambda e: e.tensor_tensor_scan(Hb[:, S_LAT:T][:, ::-1], A[:, S_LAT:T][:, ::-1], Bd[:, S_LAT:T][:, ::-1], 0.0, ALU.mult, ALU.add),
                                 reads=allA + allB, writes=["Hb"])
                            S.op("dve", lambda e: e.tensor_tensor_scan(Hb[:, 0:S_LAT][:, ::-1], A[:, 0:S_LAT][:, ::-1], Bd[:, 0:S_LAT][:, ::-1], Hb[:, S_LAT:S_LAT + 1], ALU.mult, ALU.add),
                                 reads=allA + allB + ["Hb"], writes=["Hb", "Hscan"])
                        S.op("dve", lambda e: e.engine_nop() if False else e.memset(tm[0][:, 0:1], 0.0), reads=["Hscan"], writes=allA + allB + [("tm", 0), "A"])
                    S.op("dve", lambda e: e.tensor_tensor(Hf[:, 0:S_LAT], Hf[:, 0:S_LAT], Hb[:, 0:S_LAT], ALU.add), reads=["Hf", "Hb"], writes=["Hf"])
                    S.op("dve", lambda e, c=c: e.tensor_tensor(z[:, c, :], Hf[:, 0:S_LAT], gl[:, :], ALU.mult), reads=["Hf", "gl"], writes=[("z", c)])
                S.flush()
            with ExitStack() as st:
                wo_t = st.enter_context(nc.sbuf_tensor(f"lwo_{i}", [128, NCH, D], BF16))
                psP = [st.enter_context(nc.psum_tensor(f"lpsP{x}_{i}", [128, 512], F32)) for x in range(2)]
                S.dma("pool", wo_t[:], self.I("lru_w_out").rearrange("(k p) o -> p k o", p=128), writes=["lwo"])
                cp = 0
                for t in range(4):
                    t0, n = TT[t]
                    for o in range(NCH):
                        pp = cp % 2
                        cp += 1

                        def mmo(e, o=o, t0=t0, n=n, pp=pp):
                            last = None
                            for k in range(NCH):
                                last = e.matmul(psP[pp][:, 0:n], wo_t[:, k, o * 128:(o + 1) * 128], z[:, k, t0:t0 + n], start=(k == 0), stop=(k == NCH - 1))
                            return last
                        S.op("pe", mmo, reads=["lwo"] + [("z", k) for k in range(NCH)], writes=[("psP", pp)])
                        S.op("dve", lambda e, o=o, t0=t0, n=n, pp=pp: e.scalar_tensor_tensor(
                            self.h[:, o, t0:t0 + n], psP[pp][:, 0:n], self.modT[:, 16 + o, 0:1], self.h[:, o, t0:t0 + n], ALU.mult, ALU.add),
                            reads=[("psP", pp), ("h", o, t), self.mtok], writes=[("h", o, t)])
                S.flush()

def _fm(v):
    v = np.asarray(v, np.float32)
    lead = v.shape[:-1]
    r = v.reshape(lead + (NCH, 128))
    return np.ascontiguousarray(np.moveaxis(r, -1, 0))


def prep_shared(inp):
    sh = {}
    sh["ada_w"] = np.ascontiguousarray(inp["ada_w"], np.float32)
    sh["ada_bT"] = _fm(inp["ada_b"].reshape(DEPTH, 48, 128).reshape(DEPTH, 48 * 128)) if False else \
        np.ascontiguousarray(np.moveaxis(np.asarray(inp["ada_b"], np.float32).reshape(DEPTH, 48, 128), -1, 0))
    sh["nmT"] = _fm(inp["norm_mix"])
    sh["nfT"] = _fm(inp["norm_ffn"])
    sh["nfinT"] = _fm(inp["norm_final"])
    sh["ffn_w_in"] = np.ascontiguousarray(inp["ffn_w_in"], np.float32)
    sh["ffn_w_out"] = np.ascontiguousarray(inp["ffn_w_out"], np.float32)
    sh["na_w_qkv"] = np.ascontiguousarray(inp["na_w_qkv"][0], np.float32)
    sh["na_w_o"] = np.ascontiguousarray(inp["na_w_o"][0], np.float32)
    sh["nab"] = na_bias_table(inp["na_rpb"][0])
    d = np.arange(64)
    partner = np.where((d // 16) % 2 == 0, d + 16, d - 16)
    for nm_, nq, nkv in (("gqa", 16, 4), ("swa", 16, 2)):
        w = np.ascontiguousarray(inp[nm_ + "_w_qkv"][0], np.float32)
        perm = np.arange(w.shape[1])
        for hd in range(nq + nkv):
            perm[hd * 64:(hd + 1) * 64] = hd * 64 + partner
        sh[nm_ + "_w_qkv"] = w
        sh[nm_ + "_w_qkv_p"] = np.ascontiguousarray(w[:, perm])
        sh[nm_ + "_w_o"] = np.ascontiguousarray(inp[nm_ + "_w_o"][0], np.float32)
    qg_, kg_ = np.asarray(inp["gqa_q_gain"][0], np.float32), np.asarray(inp["gqa_k_gain"][0], np.float32)
    gg = np.stack([qg_, qg_[partner], kg_, kg_[partner]], axis=1)
    sh["gqa_gains"] = np.ascontiguousarray(np.concatenate([gg, gg], axis=0))
    sk = np.asarray(inp["swa_sinks"][0], np.float32)
    sh["swa_sinks"] = np.ascontiguousarray(np.concatenate([np.broadcast_to(sk[0::2][None, :], (64, 8)), np.broadcast_to(sk[1::2][None, :], (64, 8))], axis=0))
    t = np.arange(S_LAT)
    row = (t // GRID_W).astype(np.float32)
    col = (t % GRID_W).astype(np.float32)
    inv_freq = (1.0 / (np.float32(10000.0) ** (np.arange(0, 32, 2, dtype=np.float32) / np.float32(32)))).astype(np.float32)
    ang_row = row[:, None] * inv_freq
    ang_col = col[:, None] * inv_freq
    cs = np.zeros((64, 2, T), np.float32)
    cs[:, 0, :] = 1.0
    for dd in range(64):
        seg, f = dd // 16, dd % 16
        ang = ang_row[:, f] if seg < 2 else ang_col[:, f]
        cs[dd, 0, :S_LAT] = np.cos(ang)
        cs[dd, 1, :S_LAT] = np.sin(ang) * (-1.0 if seg % 2 == 0 else 1.0)
    sh["rope_cs"] = np.ascontiguousarray(np.concatenate([cs, cs], axis=0))
    kk = np.arange(128)[:, None]
    qq = np.arange(128)[None, :]
    vecs = np.zeros((12, D), np.float32)
    vecs[0:4] = inp["lru_conv_w"][0]
    vecs[4] = inp["lru_conv_b"][0]
    vecs[5:7] = inp["lru_b_a"][0]
    vecs[7:9] = inp["lru_b_x"][0]
    vecs[9:11] = inp["lru_lam"][0]
    sh["lru_vecs"] = _fm(vecs)
    sh["lru_w_in"] = np.ascontiguousarray(inp["lru_w_in"][0], np.float32)
    sh["lru_w_a"] = np.ascontiguousarray(inp["lru_w_a"][0], np.float32)
    sh["lru_w_x"] = np.ascontiguousarray(inp["lru_w_x"][0], np.float32)
    sh["lru_w_out"] = np.ascontiguousarray(inp["lru_w_out"][0], np.float32)
    sh["swa_masks"] = np.ascontiguousarray(np.stack([(qq <= kk), (kk <= qq)], axis=1).astype(np.float32))
    return sh


def prep_core(inp, b):
    c = {}
    x = np.asarray(inp["x"][b], np.float32)
    ctx = np.asarray(inp["ctx"][b], np.float32)
    c["hin"] = np.ascontiguousarray(np.concatenate([x, ctx], axis=0).T)
    cT = np.stack([_fm(inp["c"][b]), _fm(inp["c_ctx"])], axis=-1)
    c["cT"] = np.ascontiguousarray(cT)
    return c


def na_bias_table(rpb):
    rpb = np.asarray(rpb, np.float32)
    kc = np.arange(64)[:, None]
    qc = np.arange(64)[None, :]
    dcol = np.clip(kc - qc + 15, 0, 30)
    cstart = np.clip(qc - 8, 0, 48)
    col_in = (kc >= cstart) & (kc < cstart + 16)
    g = rpb[:, :, dcol]
    g = np.where(col_in[None, None], g, np.float32(MASKV))
    return np.ascontiguousarray(np.transpose(g, (0, 2, 1, 3)).astype(np.float32))


_CACHE = {}


def kernel(**inputs):
    inputs = {k: np.asarray(v) for k, v in inputs.items()}
    n_cores = 8
    if "nc" not in _CACHE:
        B = Builder([0, 1, 2, 3], final=True)
        _CACHE["nc"] = (B, B.build())
    B, nc = _CACHE["nc"]
    sh = prep_shared(inputs)
    in_maps = []
    for b in range(n_cores):
        core = prep_core(inputs, b)
        in_maps.append({name: (core[name] if name in core else sh[name]) for name in B.dr if name != "y"})
    res = run_bass_kernel_spmd(nc, in_maps, core_ids=list(range(n_cores)))
    out = np.stack([np.asarray(res.results[b]["y"]).T for b in range(n_cores)], axis=0)
    return np.ascontiguousarray(out.astype(np.float32))
```

```python
from contextlib import ExitStack
import numpy as np
import concourse.bass as bass
import concourse.mybir as mybir
from concourse.bass_utils import run_bass_kernel_spmd

F32 = mybir.dt.float32
BF16 = mybir.dt.bfloat16
AF = mybir.ActivationFunctionType
ALU = mybir.AluOpType

D = 1024
S_LAT = 2048
L_CTX = 256
T = S_LAT + L_CTX
DEPTH = 4
HID = 2816
NCH = D // 128
TT = [(0, 512), (512, 512), (1024, 512), (1536, 512), (2048, 256)]
EPS = 1e-6
GRID_W = 64
MASKV = -30000.0


class EngQ:
    def __init__(self, name, sem):
        self.name = name
        self.sem = sem
        self.cnt = 0
        self.ops = []
        self.seen = {}


class Sched:
    ENGS = ("pe", "act", "dve", "pool", "sp")

    def __init__(self, nc, stack, n_dma_slots=8):
        self.nc = nc
        self.sems = {}
        self.q = {}
        for e in self.ENGS:
            self.sems[("c", e)] = stack.enter_context(nc.semaphore("c_" + e))
            self.q[e] = EngQ(e, ("c", e))
        self.dma_n = {}
        self.n_dma_slots = n_dma_slots
        for e in ("sp", "pool"):
            self.dma_n[e] = 0
            for i in range(n_dma_slots):
                self.sems[("d", e, i)] = stack.enter_context(nc.semaphore(f"d_{e}{i}"))
        self.w = {}
        self.r = {}
        self.out_dma = []

    def _deps(self, reads, writes):
        d = {}

        def add(k, v):
            if v > d.get(k, 0):
                d[k] = v
        for t in reads:
            if t in self.w:
                add(*self.w[t])
        for t in writes:
            if t in self.w:
                add(*self.w[t])
            for k, v in self.r.get(t, {}).items():
                add(k, v)
        return d

    def _commit(self, reads, writes, cid):
        k, v = cid
        for t in writes:
            self.w[t] = cid
            self.r[t] = {}
        for t in reads:
            rr = self.r.setdefault(t, {})
            if v > rr.get(k, 0):
                rr[k] = v

    def _waits(self, q, deps):
        ws = []
        for k, v in deps.items():
            if q.seen.get(k, 0) < v:
                q.seen[k] = v
                ws.append((k, v))
        return ws

    @staticmethod
    def _is_psum(t):
        n = t[0] if isinstance(t, tuple) else t
        return isinstance(n, str) and n.startswith("ps")

    def op(self, eng, fn, reads=(), writes=()):
        writes = list(writes) + [t for t in reads if self._is_psum(t)]
        reads = [t for t in reads if not self._is_psum(t)]
        q = self.q[eng]
        deps = self._deps(reads, writes)
        if eng == "pe":
            deps.pop(q.sem, None)
        ws = self._waits(q, deps)
        q.cnt += 1
        cid = (q.sem, q.cnt)
        q.ops.append((ws, fn, q.sem, 1))
        self._commit(reads, writes, cid)
        return cid

    def dma(self, eng, out, in_, reads=(), writes=(), is_output=False):
        q = self.q[eng]
        n = self.dma_n[eng]
        self.dma_n[eng] = n + 1
        slot = n % self.n_dma_slots
        semk = ("d", eng, slot)
        prev = 16 * (n // self.n_dma_slots)
        deps = self._deps(reads, writes)
        if prev > 0 and deps.get(semk, 0) < prev:
            deps[semk] = prev
        ws = self._waits(q, deps)
        cid = (semk, prev + 16)

        def fn(e, out=out, in_=in_):
            return e.dma_start(out=out, in_=in_)
        q.ops.append((ws, fn, semk, 16))
        self._commit(reads, writes, cid)
        if is_output:
            self.out_dma.append(cid)
        return cid

    def finish(self, eng="sp"):
        q = self.q[eng]
        d = {}
        for k, v in self.out_dma:
            d[k] = max(d.get(k, 0), v)
        ws = self._waits(q, d)
        q.ops.append((ws, None, None, 0))

    def flush(self):
        sems = self.sems
        with self.nc.Block() as block:
            def run(q, e):
                for ws, fn, isem, ival in q.ops:
                    for k, v in ws:
                        e.wait_ge(sems[k], v)
                    if fn is None:
                        continue
                    ins = fn(e)
                    if isem is not None:
                        ins.then_inc(sems[isem], ival)
                q.ops = []

            @block.tensor
            def _(e):
                run(self.q["pe"], e)

            @block.scalar
            def _(e):
                run(self.q["act"], e)

            @block.vector
            def _(e):
                run(self.q["dve"], e)

            @block.gpsimd
            def _(e):
                run(self.q["pool"], e)

            @block.sync
            def _(e):
                run(self.q["sp"], e)


class Builder:
    def __init__(self, layers, final, skip_mixer=False):
        self.layers = layers
        self.final = final
        self.skip_mixer = skip_mixer
        self.debug = False
        self.nc = bass.Bass("TRN2", target_bir_lowering=False)
        self.dr = {}

    def I(self, name):
        if name not in self.dr:
            self.dr[name] = self.nc.dram_tensor(name, list(self.shapes[name]), F32, kind="ExternalInput").ap()
        return self.dr[name]

    def dout(self, name, shape, dt=F32):
        t = self.nc.dram_tensor(name, list(shape), dt, kind="ExternalOutput").ap()
        self.dr[name] = t
        return t

    def build(self):
        nc = self.nc
        dr = self.dr
        self.shapes = {
            "hin": [D, T], "cT": [128, NCH, 2], "ada_w": [DEPTH, D, 6 * D], "ada_bT": [128, DEPTH, 48],
            "nmT": [128, DEPTH, NCH], "nfT": [128, DEPTH, NCH], "nfinT": [128, NCH],
            "ffn_w_in": [DEPTH, D, 2 * HID], "ffn_w_out": [DEPTH, HID, D],
            "na_w_qkv": [D, 3 * D], "na_w_o": [D, D], "nab": [16, 64, 15, 64],
            "gqa_w_qkv": [D, 1536], "gqa_w_qkv_p": [D, 1536], "gqa_gains": [128, 4], "gqa_w_o": [D, D],
            "swa_w_qkv": [D, 1280], "swa_w_qkv_p": [D, 1280], "swa_sinks": [128, 8], "swa_w_o": [D, D],
            "swa_masks": [128, 2, 128], "rope_cs": [128, 2, T],
            "lru_w_in": [D, 2 * D], "lru_vecs": [128, 12, NCH], "lru_w_a": [2, 16, 64, 64],
            "lru_w_x": [2, 16, 64, 64], "lru_w_out": [D, D],
        }
        if self.final:
            self.dout("y", [D, S_LAT])
        else:
            self.dout("hout", [D, T])

        with ExitStack() as st:
            self.S = S = Sched(nc, st)
            sb = lambda name, shape, dt: st.enter_context(nc.sbuf_tensor(name, list(shape), dt))
            self.h = sb("h", [128, NCH, T], F32)
            self.a = sb("a", [128, NCH, T], BF16)
            self.ones = sb("ones", [128, 128], BF16)
            self.cT = sb("cTs", [128, NCH, 2], F32)
            self.scT = sb("scT", [128, NCH, 2], BF16)
            self.adab = sb("adab", [128, DEPTH, 48], F32)
            self.nm = sb("nm", [128, DEPTH, NCH], F32)
            self.nf = sb("nf", [128, DEPTH, NCH], F32)
            self.nfin = sb("nfin", [128, NCH], F32)
            self.modTs = [sb(f"modT{x}", [128, 48, 2], F32) for x in range(2)]
            self.gs1s = [sb(f"gs1{x}", [128, NCH, 2], F32) for x in range(2)]
            self.gs2s = [sb(f"gs2{x}", [128, NCH, 2], F32) for x in range(2)]
            self.epsc = sb("epsc", [128, 1], F32)

            for c in range(NCH):
                S.dma("sp", self.h[:, c, :], self.I("hin")[c * 128:(c + 1) * 128, :],
                      writes=[("h", c, t) for t in range(5)])
            S.dma("sp", self.cT[:], self.I("cT")[:, :, :], writes=["cT"])
            S.dma("sp", self.adab[:], self.I("ada_bT")[:, :, :], writes=["adab"])
            S.dma("sp", self.nm[:], self.I("nmT")[:, :, :], writes=["nm"])
            S.dma("sp", self.nf[:], self.I("nfT")[:, :, :], writes=["nf"])
            S.dma("sp", self.nfin[:], self.I("nfinT")[:, :], writes=["nfin"])
            S.op("dve", lambda e: e.memset(self.ones[:], 1.0), writes=["ones"])
            S.op("dve", lambda e: e.memset(self.epsc[:], EPS), writes=["epsc"])
            S.op("act", lambda e: e.activation(out=self.scT[:], in_=self.cT[:], func=AF.Silu),
                 reads=["cT"], writes=["scT"])
            S.flush()

            for li, i in enumerate(self.layers):
                last = (i == DEPTH - 1)
                self.modT, self.gs1, self.gs2 = self.modTs[i % 2], self.gs1s[i % 2], self.gs2s[i % 2]
                self.mtok = ("modT", i % 2)
                if li == 0:
                    self.mods_phase(i)
                self.next_layer = self.layers[li + 1] if li + 1 < len(self.layers) else None
                kind = i % 4
                ntile = 4 if last else 5
                if self.skip_mixer:
                    pass
                elif kind == 0:
                    self.attn_phase(i, "na")
                elif kind == 1:
                    self.attn_phase(i, "gqa")
                elif kind == 2:
                    self.attn_phase(i, "swa")
                else:
                    self.lru_phase(i)
                self.ffn_phase(i, ntile)
            if self.final:
                self.final_phase()
            else:
                for c in range(NCH):
                    S.dma("sp", dr["hout"][c * 128:(c + 1) * 128, :], self.h[:, c, :],
                          reads=[("h", c, t) for t in range(5)], is_output=True)
            S.finish()
            S.flush()
        return nc

    def mods_block(self, i, jb, ring, psM):
        S = self.S
        slot = jb % 2
        S.dma("pool", ring[slot][:],
              self.I("ada_w")[i, :, jb * 512:(jb + 1) * 512].rearrange("(k p) c -> p k c", p=128),
              writes=[("adaw", slot)])

    def mods_mm(self, i, jb, ring, psM):
        S = self.S
        slot = jb % 2

        def mm(e, jb=jb, slot=slot):
            last = None
            for jj in range(4):
                j = jb * 4 + jj
                for k in range(NCH):
                    last = e.matmul(psM[:, j * 2:(j + 1) * 2], ring[slot][:, k, jj * 128:(jj + 1) * 128],
                                    self.scT[:, k, :], start=(k == 0), stop=(k == NCH - 1))
            return last
        S.op("pe", mm, reads=[("adaw", slot), "scT"], writes=["psM"])

    def mods_finish(self, i, psM):
        S = self.S
        modT, gs1, gs2 = self.modTs[i % 2], self.gs1s[i % 2], self.gs2s[i % 2]
        tok = ("modT", i % 2)
        psv = psM[:].rearrange("p (j g) -> p j g", g=2)
        for g in range(2):
            S.op("dve", lambda e, g=g: e.tensor_tensor(modT[:, :, g], psv[:, :, g], self.adab[:, i, :], ALU.add),
                 reads=["psM", "adab"], writes=[tok])
        for g in range(2):
            S.op("dve", lambda e, g=g: e.scalar_tensor_tensor(gs1[:, :, g], modT[:, 8:16, g], 1.0,
                                                              self.nm[:, i, :], ALU.add, ALU.mult),
                 reads=[tok, "nm"], writes=[tok])
            S.op("dve", lambda e, g=g: e.scalar_tensor_tensor(gs2[:, :, g], modT[:, 32:40, g], 1.0,
                                                              self.nf[:, i, :], ALU.add, ALU.mult),
                 reads=[tok, "nf"], writes=[tok])

    def mods_phase(self, i):
        nc, S = self.nc, self.S
        with ExitStack() as st:
            ring = [st.enter_context(nc.sbuf_tensor(f"adaw{s}_{i}", [128, NCH, 512], BF16)) for s in range(2)]
            psM = st.enter_context(nc.psum_tensor(f"psM_{i}", [128, 96], F32))
            for jb in range(12):
                self.mods_block(i, jb, ring, psM)
                self.mods_mm(i, jb, ring, psM)
            self.mods_finish(i, psM)
            S.flush()

    def norm(self, st, gs, sh_base, ntile, tag, psN=None, ptok="psN", nsq=2):
        nc, S = self.nc, self.S
        modT = self.modT
        sq = [st.enter_context(nc.sbuf_tensor(f"sq{s}_{tag}", [128, NCH, 512], BF16)) for s in range(nsq)] * (2 // nsq)
        tmp = [st.enter_context(nc.sbuf_tensor(f"ntmp{s}_{tag}", [128, NCH, 512], F32)) for s in range(1)]
        rstd = [st.enter_context(nc.sbuf_tensor(f"rstd{s}_{tag}", [128, 512], F32)) for s in range(2)]
        if psN is None:
            psN = [st.enter_context(nc.psum_tensor(f"psN{s}_{tag}", [128, 512], F32)) for s in range(2)]

        def stats(t):
            t0, n = TT[t]
            s2 = t % 2
            hs = [("h", c, t) for c in range(NCH)]
            S.op("dve", lambda e, t0=t0, n=n, s2=s2: e.tensor_tensor(sq[s2][:, :, 0:n], self.h[:, :, t0:t0 + n], self.h[:, :, t0:t0 + n], ALU.mult),
                 reads=hs, writes=[("sq", s2 % nsq)])

            def mm(e, n=n, s2=s2):
                last = None
                for k in range(NCH):
                    last = e.matmul(psN[s2][:, 0:n], self.ones[:], sq[s2][:, k, 0:n], start=(k == 0), stop=(k == NCH - 1))
                return last
            S.op("pe", mm, reads=[("sq", s2 % nsq), "ones"], writes=[(ptok, s2)])
            S.op("act", lambda e, n=n, s2=s2: e.activation(out=rstd[s2][:, 0:n], in_=psN[s2][:, 0:n], func=AF.Sqrt,
                                                           bias=self.epsc[:, 0:1], scale=1.0 / D),
                 reads=[(ptok, s2), "epsc"], writes=[("rstd", s2)])
            S.op("dve", lambda e, n=n, s2=s2: e.reciprocal(rstd[s2][:, 0:n], rstd[s2][:, 0:n]),
                 reads=[("rstd", s2)], writes=[("rstd", s2)])

        def apply(t):
            t0, n = TT[t]
            g = 0 if t < 4 else 1
            s2 = t % 2
            for c in range(NCH):
                S.op("dve" if c % 2 == 0 else "pool", lambda e, c=c, t0=t0, n=n, s2=s2: e.tensor_tensor(
                    tmp[0][:, c, 0:n], self.h[:, c, t0:t0 + n], rstd[s2][:, 0:n], ALU.mult),
                    reads=[("h", c, t), ("rstd", s2)], writes=[("ntmp", c)])
                S.op("act", lambda e, c=c, t0=t0, n=n, g=g: e.activation(
                    out=self.a[:, c, t0:t0 + n], in_=tmp[0][:, c, 0:n], func=AF.Identity,
                    bias=modT[:, sh_base + c, g:g + 1], scale=gs[:, c, g:g + 1]),
                    reads=[("ntmp", c), self.mtok], writes=[("a", c, t)])
        stats(0)
        for t in range(ntile):
            if t + 1 < ntile:
                stats(t + 1)
            apply(t)

    def ffn_phase(self, i, ntile):
        nc, S, dr = self.nc, self.S, self.dr
        GC = 2
        NG = HID // 128 // GC
        with ExitStack() as st:
            wa = [st.enter_context(nc.sbuf_tensor(f"wa{s}_{i}", [128, NCH, GC * 128], BF16)) for s in range(2)]
            wg = [st.enter_context(nc.sbuf_tensor(f"wg{s}_{i}", [128, NCH, GC * 128], BF16)) for s in range(2)]
            wo = [st.enter_context(nc.sbuf_tensor(f"wo{s}_{i}", [128, GC, D], BF16)) for s in range(2)]
            ug = [st.enter_context(nc.sbuf_tensor(f"ug{s}_{i}", [128, GC, T], BF16)) for s in range(2)]
            su = [st.enter_context(nc.sbuf_tensor(f"su{s}_{i}", [128, 512], F32)) for s in range(2)]
            psA = [st.enter_context(nc.psum_tensor(f"psA{s}_{i}", [128, 512], F32)) for s in range(2)]
            psG = [st.enter_context(nc.psum_tensor(f"psG{s}_{i}", [128, 512], F32)) for s in range(2)]
            psY = [st.enter_context(nc.psum_tensor(f"psY{s}_{i}", [128, 512], F32)) for s in range(2)]
            self.norm(st, self.gs2, 24, ntile, f"f{i}", psN=psY, ptok="psY", nsq=1)
            nxt = self.next_layer
            if nxt is not None:
                aring = [st.enter_context(nc.sbuf_tensor(f"adawf{s}_{i}", [128, NCH, 512], BF16)) for s in range(2)]
                psMf = st.enter_context(nc.psum_tensor(f"psMf_{i}", [128, 96], F32))
                self.mods_block(nxt, 0, aring, psMf)
            win = self.I("ffn_w_in")
            wout = self.I("ffn_w_out")
            cnt = 0
            ycnt = 0
            for G in range(NG):
                s = G % 2
                c0 = G * GC * 128
                if nxt is not None:
                    if G + 1 < 12:
                        self.mods_block(nxt, G + 1, aring, psMf)
                    self.mods_mm(nxt, G, aring, psMf)
                S.dma("pool", wa[s][:], win[i, :, c0:c0 + GC * 128].rearrange("(k p) c -> p k c", p=128), writes=[("wa", s)])
                S.dma("pool", wg[s][:], win[i, :, HID + c0:HID + c0 + GC * 128].rearrange("(k p) c -> p k c", p=128), writes=[("wg", s)])
                S.dma("pool", wo[s][:], wout[i, c0:c0 + GC * 128, :].rearrange("(j p) o -> p j o", p=128), writes=[("wo", s)])
                for t in range(ntile):
                    t0, n = TT[t]
                    for j in range(GC):
                        ps = cnt % 2
                        cnt += 1

                        def mm(e, s=s, j=j, t0=t0, n=n, ps=ps):
                            last = None
                            for k in range(NCH):
                                e.matmul(psA[ps][:, 0:n], wa[s][:, k, j * 128:(j + 1) * 128], self.a[:, k, t0:t0 + n],
                                         start=(k == 0), stop=(k == NCH - 1))
                            for k in range(NCH):
                                last = e.matmul(psG[ps][:, 0:n], wg[s][:, k, j * 128:(j + 1) * 128], self.a[:, k, t0:t0 + n],
                                                start=(k == 0), stop=(k == NCH - 1))
                            return last
                        S.op("pe", mm, reads=[("wa", s), ("wg", s)] + [("a", c, t) for c in range(NCH)],
                             writes=[("psA", ps), ("psG", ps)])
                        S.op("act", lambda e, ps=ps, n=n: e.activation(out=su[ps][:, 0:n], in_=psA[ps][:, 0:n], func=AF.Silu),
                             reads=[("psA", ps)], writes=[("su", ps)])
                        S.op("dve", lambda e, ps=ps, n=n, s=s, j=j, t0=t0: e.tensor_tensor(
                            ug[s][:, j, t0:t0 + n], psG[ps][:, 0:n], su[ps][:, 0:n], ALU.mult),
                            reads=[("psG", ps), ("su", ps)], writes=[("ug", s, t)])
                for t in range(ntile):
                    t0, n = TT[t]
                    g = 0 if t < 4 else 1
                    for o in range(NCH):
                        py = ycnt % 2
                        ycnt += 1

                        def mm2(e, s=s, o=o, t0=t0, n=n, py=py):
                            last = None
                            for j in range(GC):
                                last = e.matmul(psY[py][:, 0:n], wo[s][:, j, o * 128:(o + 1) * 128], ug[s][:, j, t0:t0 + n],
                                                start=(j == 0), stop=(j == GC - 1))
                            return last
                        S.op("pe", mm2, reads=[("wo", s), ("ug", s, t)], writes=[("psY", py)])
                        S.op("dve", lambda e, o=o, t0=t0, n=n, py=py, g=g: e.scalar_tensor_tensor(
                            self.h[:, o, t0:t0 + n], psY[py][:, 0:n], self.modT[:, 40 + o, g:g + 1], self.h[:, o, t0:t0 + n],
                            ALU.mult, ALU.add),
                            reads=[("psY", py), ("h", o, t), self.mtok], writes=[("h", o, t)])
            if nxt is not None:
                self.mods_mm(nxt, 11, aring, psMf)
                self.mods_finish(nxt, psMf)
            S.flush()

    def final_phase(self):
        nc, S, dr = self.nc, self.S, self.dr
        with ExitStack() as st:
            sq = [st.enter_context(nc.sbuf_tensor(f"fsq{s}", [128, NCH, 512], BF16)) for s in range(2)]
            rstd = [st.enter_context(nc.sbuf_tensor(f"frstd{s}", [128, 512], F32)) for s in range(2)]
            psN = [st.enter_context(nc.psum_tensor(f"fpsN{s}", [128, 512], F32)) for s in range(2)]
            for t in range(4):
                t0, n = TT[t]
                s2 = t % 2
                hs = [("h", c, t) for c in range(NCH)]
                S.op("act", lambda e, t0=t0, n=n, s2=s2: e.activation(out=sq[s2][:, :, 0:n], in_=self.h[:, :, t0:t0 + n], func=AF.Square),
                     reads=hs, writes=[("sq", s2)])

                def mm(e, n=n, s2=s2):
                    last = None
                    for k in range(NCH):
                        last = e.matmul(psN[s2][:, 0:n], self.ones[:], sq[s2][:, k, 0:n], start=(k == 0), stop=(k == NCH - 1))
                    return last
                S.op("pe", mm, reads=[("sq", s2), "ones"], writes=[("psN", s2)])
                S.op("act", lambda e, n=n, s2=s2: e.activation(out=rstd[s2][:, 0:n], in_=psN[s2][:, 0:n], func=AF.Sqrt,
                                                               bias=self.epsc[:, 0:1], scale=1.0 / D),
                     reads=[("psN", s2), "epsc"], writes=[("rstd", s2)])
                S.op("dve", lambda e, n=n, s2=s2: e.reciprocal(rstd[s2][:, 0:n], rstd[s2][:, 0:n]),
                     reads=[("rstd", s2)], writes=[("rstd", s2)])
                for c in range(NCH):
                    S.op("dve", lambda e, c=c, t0=t0, n=n, s2=s2: e.scalar_tensor_tensor(
                        self.h[:, c, t0:t0 + n], self.h[:, c, t0:t0 + n], self.nfin[:, c:c + 1], rstd[s2][:, 0:n], ALU.mult, ALU.mult),
                        reads=[("h", c, t), ("rstd", s2), "nfin"], writes=[("h", c, t)])
            for c in range(NCH):
                S.dma("sp", dr["y"][c * 128:(c + 1) * 128, :], self.h[:, c, 0:S_LAT],
                      reads=[("h", c, t) for t in range(4)], is_output=True)

    def attn_phase(self, i, kind):
        nc, S = self.nc, self.S
        with ExitStack() as st0:
            self.norm(st0, self.gs1, 0, 5, f"m{i}")
            S.flush()
        rope = kind in ("gqa", "swa")
        if kind == "na":
            NP, NG, QB, VW = 1, 8, 128, 128
            wqkv, wo_d = self.I("na_w_qkv"), self.I("na_w_o")
            qcol = lambda g: g * 128
            kcol = lambda g: 1024 + g * 128
            vcol = lambda g: 2048 + g * 128
        elif kind == "gqa":
            NP, NG, QB, VW = 2, 4, 256, 64
            wqkv, wqkvp, wo_d = self.I("gqa_w_qkv"), self.I("gqa_w_qkv_p"), self.I("gqa_w_o")
            qcol = lambda g: g * 256
            kcol = lambda g: 1024 + g * 64
            vcol = lambda g: 1280 + g * 64
        else:
            NP, NG, QB, VW = 2, 4, 128, 64
            wqkv, wqkvp, wo_d = self.I("swa_w_qkv"), self.I("swa_w_qkv_p"), self.I("swa_w_o")
            qcol = lambda g: g * 256
            kcol = lambda g: 1024 + (g // 2) * 64
            vcol = lambda g: 1152 + (g // 2) * 64
        N = NP * QB
        GM = 512 // N
        with ExitStack() as st:
            sb = lambda name, shape, dt: st.enter_context(nc.sbuf_tensor(f"{name}_{i}", list(shape), dt))
            ps = lambda name: st.enter_context(nc.psum_tensor(f"{name}_{i}", [128, 512], F32))
            wq = [sb(f"wq{x}", [128, NCH, NP * 128], BF16) for x in range(2)]
            wk = [sb(f"wk{x}", [128, NCH, 128], BF16) for x in range(2)]
            wv = [sb(f"wv{x}", [128, NCH, VW], BF16) for x in range(2)]
            wo = [sb(f"awo{x}", [128, NP, D], BF16) for x in range(2)]
            if rope:
                wqp = [sb(f"wqp{x}", [128, NCH, NP * 128], BF16) for x in range(2)]
                wkp = [sb(f"wkp{x}", [128, NCH, 128], BF16) for x in range(2)]
                cs = sb("cs", [128, 2, T], F32)
                S.dma("sp", cs[:], self.I("rope_cs")[:, :, :], writes=["cs"])
                t1 = [sb(f"t1{x}", [128, 512], F32) for x in range(2)]
                t2 = [sb(f"t2{x}", [128, 512], F32) for x in range(1)] * 2
            if kind == "gqa":
                gains = sb("gains", [128, 4], F32)
                S.dma("sp", gains[:], self.I("gqa_gains")[:, :], writes=["gains"])
                sqh = [sb(f"sqh{x}", [128, 512], BF16) for x in range(1)] * 2
                rs = [sb(f"rs{x}", [128, 512], F32) for x in range(1)] * 2
                bd = sb("bd", [128, 128], BF16)
                S.op("dve", lambda e: e.memset(bd[:], 0.0), writes=["bd"])
                S.op("dve", lambda e: e.memset(bd[0:64, 0:64], 1.0), writes=["bd"])
                S.op("dve", lambda e: e.memset(bd[64:128, 64:128], 1.0), writes=["bd"])
            if kind == "swa":
                sinkx = sb("sinkx", [128, 8], F32)
                S.dma("sp", sinkx[:], self.I("swa_sinks")[:, :], writes=["sinkx"])
                S.op("act", lambda e: e.activation(out=sinkx[:], in_=sinkx[:], func=AF.Exp), reads=["sinkx"], writes=["sinkx"])
                masks = sb("masks", [128, 2, 128], BF16)
                S.dma("pool", masks[:], self.I("swa_masks")[:, :, :], writes=["masks"])
            if kind == "na":
                stg = sb("stg", [128, 15, 64], F32)
                E2 = sb("E2", [128, 15, 64], BF16)
                Fm = sb("Fm", [128, 2, 2, 25, 128], BF16)
            qg = sb("qg", [128, NP, T], BF16)
            kT = sb("kT", [128, T], BF16)
            V = sb("V", [128, 18, VW], BF16)
            Og = sb("Og", [128, NP, T], BF16)
            P = [sb(f"P{x}", [128, 1024], BF16) for x in range(3)]
            rec = [sb(f"rec{x}", [128, 512], F32) for x in range(2)]
            psS = [st.enter_context(nc.psum_tensor(f"psS{x}_{i}", [128, 1024], F32)) for x in range(2)]
            psO = [ps(f"psO{x}") for x in range(2)]
            psD = [ps(f"psD{x}") for x in range(2)]
            psP = psS
            cntP = [0]
            evtog = [0]

            def r0(qr):
                return min(max(qr - 4, 0), 24)

            def tw0(R):
                return min(max(R - 2, 0), 11)
            VAR_R = [0, 1, 2, 14, 15]

            def var_of(R):
                return {0: 0, 1: 1, 14: 3, 15: 4}.get(R, 2)

            def proj(wt, wtp, col0, dst_fn, is_q, t, wtok, dtok):
                t0, n = TT[t]
                pp = cntP[0] % 2
                cntP[0] += 1
                areads = [("a", c, t) for c in range(NCH)]

                def mm(e, pp=pp, t0=t0, n=n):
                    last = None
                    for k in range(NCH):
                        last = e.matmul(psP[pp][:, 0:n], wt[:, k, col0:col0 + 128], self.a[:, k, t0:t0 + n],
                                        start=(k == 0), stop=(k == NCH - 1))
                    return last
                S.op("pe", mm, reads=areads + wtok, writes=[("psS", pp)])
                dst = dst_fn(t0, n)
                if not rope:
                    evtog[0] += 1
                    if evtog[0] % 2 == 0:
                        S.op("act", lambda e, pp=pp, n=n: e.activation(out=dst, in_=psP[pp][:, 0:n], func=AF.Copy),
                             reads=[("psS", pp)], writes=[dtok])
                    else:
                        S.op("dve", lambda e, pp=pp, n=n: e.tensor_copy(dst, psP[pp][:, 0:n]),
                             reads=[("psS", pp)], writes=[dtok])
                    return
                pq = cntP[0] % 2
                cntP[0] += 1

                def mm2(e, pq=pq, t0=t0, n=n):
                    last = None
                    for k in range(NCH):
                        last = e.matmul(psP[pq][:, 0:n], wtp[:, k, col0:col0 + 128], self.a[:, k, t0:t0 + n],
                                        start=(k == 0), stop=(k == NCH - 1))
                    return last
                S.op("pe", mm2, reads=areads + wtok, writes=[("psS", pq)])
                x = evtog[0] % 2
                evtog[0] += 1
                if kind == "gqa":
                    gc = 0 if is_q else 2
                    S.op("act", lambda e, pp=pp, n=n, x=x: e.activation(out=sqh[x][:, 0:n], in_=psP[pp][:, 0:n], func=AF.Square),
                         reads=[("psS", pp)], writes=[("sqh", 0)])
                    S.op("dve", lambda e, pp=pp, n=n, x=x, t0=t0, gc=gc: e.scalar_tensor_tensor(
                        t1[x][:, 0:n], psP[pp][:, 0:n], gains[:, gc:gc + 1], cs[:, 0, t0:t0 + n], ALU.mult, ALU.mult),
                        reads=[("psS", pp), "gains", "cs"], writes=[("t1", x)])
                    S.op("dve", lambda e, pq=pq, n=n, x=x, t0=t0, gc=gc: e.scalar_tensor_tensor(
                        t2[x][:, 0:n], psP[pq][:, 0:n], gains[:, gc + 1:gc + 2], cs[:, 1, t0:t0 + n], ALU.mult, ALU.mult),
                        reads=[("psS", pq), "gains", "cs"], writes=[("t2", 0)])
                    S.op("pe", lambda e, pp=pp, n=n, x=x: e.matmul(psP[pp][:, 0:n], bd[:], sqh[x][:, 0:n], start=True, stop=True),
                         reads=[("sqh", 0), "bd"], writes=[("psS", pp)])
                    S.op("act", lambda e, pp=pp, n=n, x=x: e.activation(out=rs[x][:, 0:n], in_=psP[pp][:, 0:n], func=AF.Sqrt,
                                                                       bias=self.epsc[:, 0:1], scale=1.0 / 64),
                         reads=[("psS", pp), "epsc"], writes=[("rs", 0)])
                    S.op("dve", lambda e, n=n, x=x: e.reciprocal(rs[x][:, 0:n], rs[x][:, 0:n]), reads=[("rs", 0)], writes=[("rs", 0)])
                    S.op("pool", lambda e, n=n, x=x: e.tensor_tensor(t1[x][:, 0:n], t1[x][:, 0:n], t2[x][:, 0:n], ALU.add),
                         reads=[("t1", x), ("t2", 0)], writes=[("t1", x)])
                    S.op("pool", lambda e, n=n, x=x: e.tensor_tensor(dst, t1[x][:, 0:n], rs[x][:, 0:n], ALU.mult),
                         reads=[("t1", x), ("rs", 0)], writes=[dtok])
                else:
                    S.op("dve", lambda e, pp=pp, n=n, x=x, t0=t0: e.tensor_tensor(t1[x][:, 0:n], psP[pp][:, 0:n], cs[:, 0, t0:t0 + n], ALU.mult),
                         reads=[("psS", pp), "cs"], writes=[("t1", x)])
                    S.op("dve", lambda e, pq=pq, n=n, x=x, t0=t0: e.tensor_tensor(t2[x][:, 0:n], psP[pq][:, 0:n], cs[:, 1, t0:t0 + n], ALU.mult),
                         reads=[("psS", pq), "cs"], writes=[("t2", 0)])
                    S.op("pool", lambda e, n=n, x=x: e.tensor_tensor(dst, t1[x][:, 0:n], t2[x][:, 0:n], ALU.add),
                         reads=[("t1", x), ("t2", 0)], writes=[dtok])

            def build_F(g):
                fp = g % 2
                for hh in range(2):
                    hd = g * 2 + hh
                    S.dma("sp", stg[0:64], self.I("nab")[hd], writes=["stg"])
                    S.dma("sp", stg[64:128], self.I("nab")[hd], writes=["stg"])
                    S.op("act", lambda e: e.activation(out=E2[:], in_=stg[:], func=AF.Exp), reads=["stg"], writes=["E2"])
                    S.op("pool", lambda e, hh=hh, fp=fp: e.memset(Fm[:, fp, hh], 0.0), writes=[("F", fp, hh)])
                    for v in range(5):
                        R = VAR_R[v]
                        for krp in range(2):
                            for qrp in range(2):
                                qr = 2 * R + qrp
                                ws_ = [w for w in range(5) if r0(qr) <= 2 * (tw0(R) + w) + krp < r0(qr) + 8]
                                if not ws_:
                                    continue
                                w_lo, nw = ws_[0], len(ws_)
                                assert ws_ == list(range(w_lo, w_lo + nw))
                                d0 = 2 * (tw0(R) + w_lo) + krp - qr + 7
                                S.op("pool", lambda e, hh=hh, fp=fp, v=v, krp=krp, qrp=qrp, w_lo=w_lo, nw=nw, d0=d0: e.tensor_copy(
                                    Fm[krp * 64:(krp + 1) * 64, fp, hh, v * 5 + w_lo:v * 5 + w_lo + nw, qrp * 64:(qrp + 1) * 64],
                                    E2[krp * 64:(krp + 1) * 64, d0:d0 + 2 * nw - 1:2, :]),
                                    reads=["E2"], writes=[("F", fp, hh)])

            def load_weights(g):
                sl = g % 2
                wsrc = lambda col, n: wqkv[:, col:col + n].rearrange("(k p) c -> p k c", p=128)
                wsrcp = lambda col, n: wqkvp[:, col:col + n].rearrange("(k p) c -> p k c", p=128)
                S.dma("pool", wq[sl][:], wsrc(qcol(g), NP * 128), writes=[("wq", sl)])
                if rope:
                    S.dma("pool", wqp[sl][:], wsrcp(qcol(g), NP * 128), writes=[("wq", sl)])
                if kind == "na":
                    S.dma("pool", wk[sl][:], wsrc(kcol(g), 128), writes=[("wk", sl)])
                else:
                    for half in range(2):
                        S.dma("pool", wk[sl][:, :, half * 64:(half + 1) * 64], wsrc(kcol(g), 64), writes=[("wk", sl)])
                        S.dma("pool", wkp[sl][:, :, half * 64:(half + 1) * 64], wsrcp(kcol(g), 64), writes=[("wk", sl)])
                S.dma("pool", wv[sl][:], wsrc(vcol(g), VW), writes=[("wk", sl)])
                S.dma("pool", wo[sl][:], wo_d[g * NP * 128:(g + 1) * NP * 128, :].rearrange("(p r) o -> r p o", r=128), writes=[("wo", sl)])

            def run_group(g):
                sl = g % 2
                for p in range(NP):
                    for t in range(5):
                        proj(wq[sl], wqp[sl] if rope else None, p * 128, lambda t0, n, p=p: qg[:, p, t0:t0 + n], True, t, [("wq", sl)], "qg")
                new_kv = (kind != "swa") or (g % 2 == 0)
                if new_kv:
                    for t in range(5):
                        proj(wk[sl], wkp[sl] if rope else None, 0, lambda t0, n: kT[:, t0:t0 + n], False, t, [("wk", sl)], "kT")
                    for tk in range(18):
                        pp = cntP[0] % 2
                        cntP[0] += 1
                        tt_ = min(tk // 4, 4)

                        def mmv(e, tk=tk, pp=pp):
                            last = None
                            for k in range(NCH):
                                last = e.matmul(psP[pp][:, 0:VW], self.a[:, k, tk * 128:(tk + 1) * 128], wv[sl][:, k, :],
                                                start=(k == 0), stop=(k == NCH - 1))
                            return last
                        S.op("pe", mmv, reads=[("a", c, tt_) for c in range(NCH)] + [("wk", sl)], writes=[("psS", pp)])
                        S.op("act", lambda e, tk=tk, pp=pp: e.activation(out=V[:, tk, :], in_=psP[pp][:, 0:VW], func=AF.Copy),
                             reads=[("psS", pp)], writes=["V"])
                nlat = S_LAT // QB
                nctx = L_CTX // QB
                steps = []
                for qb in range(nlat + nctx):
                    if qb >= nlat:
                        kl = [(16, None), (17, None)]
                    elif kind == "na":
                        kl = [(tw0(qb) + w, ("F", var_of(qb) * 5 + w)) for w in range(5)] + [(16, None), (17, None)]
                    elif kind == "gqa":
                        kl = [(tk, None) for tk in range(18)]
                    else:
                        kl = []
                        if qb >= 1:
                            kl.append((qb - 1, ("M", 0)))
                        kl.append((qb, None))
                        if qb <= 14:
                            kl.append((qb + 1, ("M", 1)))
                        kl += [(16, None), (17, None)]
                    for idx, (tk, fm) in enumerate(kl):
                        steps.append((qb, tk, fm, idx == 0, idx == len(kl) - 1))
                msteps = []
                for st_ in steps:
                    if msteps and msteps[-1][0][0] == st_[0] and len(msteps[-1]) < GM:
                        msteps[-1].append(st_)
                    else:
                        msteps.append([st_])
                vcs = (slice(0, 64), slice(64, 128)) if kind == "na" else (slice(0, 64), slice(0, 64))

                def qk(midx):
                    ms = msteps[midx]
                    r = midx % 2
                    q0 = ms[0][0] * QB

                    def f(e, ms=ms, r=r, q0=q0):
                        last_ = None
                        for x, (_, tk, _, _, _) in enumerate(ms):
                            for sd in range(2):
                                pl = slice(sd * 64, (sd + 1) * 64)
                                last_ = e.matmul(psS[r][:, sd * 512 + x * N:sd * 512 + (x + 1) * N], kT[pl, tk * 128:(tk + 1) * 128],
                                                 qg[pl, 0:NP, q0:q0 + QB], start=True, stop=True)
                        return last_
                    S.op("pe", f, reads=["kT", "qg"], writes=[("psS", r)])

                def rest(midx):
                    ms = msteps[midx]
                    r = midx % 2
                    pr = midx % 3
                    qb = ms[0][0]
                    q0 = qb * QB
                    ob = qb % 2
                    W = len(ms) * N
                    S.op("act", lambda e, r=r, pr=pr, W=W: e.activation(
                        out=P[pr][:].rearrange("p (s c) -> p s c", s=2)[:, :, 0:W],
                        in_=psS[r][:].rearrange("p (s c) -> p s c", s=2)[:, :, 0:W], func=AF.Exp, scale=0.125),
                        reads=[("psS", r)], writes=[("P", pr)])
                    fms = [(x, st_[2]) for x, st_ in enumerate(ms) if st_[2] is not None]
                    if fms and fms[0][1][0] == "F":
                        x0, nf_, fi0 = fms[0][0], len(fms), fms[0][1][1]
                        for sd in range(2):
                            S.op("dve", lambda e, pr=pr, x0=x0, nf_=nf_, fi0=fi0, sd=sd: e.tensor_tensor(
                                P[pr][:, sd * 512 + x0 * 128:sd * 512 + (x0 + nf_) * 128].rearrange("p (a b) -> p a b", a=nf_),
                                P[pr][:, sd * 512 + x0 * 128:sd * 512 + (x0 + nf_) * 128].rearrange("p (a b) -> p a b", a=nf_),
                                Fm[:, g % 2, sd, fi0:fi0 + nf_, :], ALU.mult),
                                reads=[("P", pr), ("F", g % 2, sd)], writes=[("P", pr)])
                    elif fms:
                        def mk(e, pr=pr, fms=fms):
                            last_ = None
                            for (x, fm) in fms:
                                for sd in range(2):
                                    for hx in range(NP):
                                        b0 = sd * 512 + x * N + hx * QB
                                        sl_ = P[pr][:, b0:b0 + 128]
                                        last_ = e.tensor_tensor(sl_, sl_, masks[:, fm[1], :], ALU.mult)
                            return last_
                        S.op("pool", mk, reads=[("P", pr), "masks"], writes=[("P", pr)])

                    def pv(e, ms=ms, pr=pr, ob=ob):
                        last_ = None
                        for x, (_, tk, _, first, last) in enumerate(ms):
                            for sd in range(2):
                                pl = slice(sd * 64, (sd + 1) * 64)
                                e.matmul(psO[ob][pl, 0:N], V[:, tk, vcs[sd]], P[pr][:, sd * 512 + x * N:sd * 512 + (x + 1) * N], start=first, stop=last)
                            for sd in range(2):
                                pl = slice(sd * 64, (sd + 1) * 64)
                                last_ = e.matmul(psD[ob][pl, 0:N], self.ones[:, 0:64], P[pr][:, sd * 512 + x * N:sd * 512 + (x + 1) * N], start=first, stop=last)
                        return last_
                    S.op("pe", pv, reads=[("P", pr), "V", "ones"], writes=[("psO", ob)])
                    if ms[-1][4]:
                        if kind == "swa":
                            def addsink(e, ob=ob):
                                last_ = None
                                for hx in range(NP):
                                    pair = g * NP + hx
                                    last_ = e.tensor_scalar(rec[ob][:, hx * QB:(hx + 1) * QB], psD[ob][:, hx * QB:(hx + 1) * QB],
                                                            sinkx[:, pair:pair + 1], None, ALU.add)
                                return last_
                            S.op("dve", addsink, reads=[("psO", ob), "sinkx"], writes=[("rec", ob)])
                            S.op("dve", lambda e, ob=ob: e.reciprocal(rec[ob][:, 0:N], rec[ob][:, 0:N]),
                                 reads=[("rec", ob)], writes=[("rec", ob)])
                        else:
                            S.op("dve", lambda e, ob=ob: e.reciprocal(rec[ob][:, 0:N], psD[ob][:, 0:N]),
                                 reads=[("psO", ob)], writes=[("rec", ob)])
                        S.op("dve", lambda e, ob=ob, q0=q0: e.tensor_tensor(
                            Og[:, 0:NP, q0:q0 + QB], psO[ob][:, 0:N].rearrange("p (a b) -> p a b", a=NP),
                            rec[ob][:, 0:N].rearrange("p (a b) -> p a b", a=NP), ALU.mult),
                            reads=[("psO", ob), ("rec", ob)], writes=["Og"])
                qk(0)
                for midx in range(len(msteps)):
                    if midx + 1 < len(msteps):
                        qk(midx + 1)
                    rest(midx)
                for t in range(5):
                    t0, n = TT[t]
                    gi = 0 if t < 4 else 1
                    for o in range(NCH):
                        pp = cntP[0] % 2
                        cntP[0] += 1

                        def mmo(e, o=o, t0=t0, n=n, pp=pp):
                            last = None
                            for p in range(NP):
                                last = e.matmul(psP[pp][:, 0:n], wo[sl][:, p, o * 128:(o + 1) * 128], Og[:, p, t0:t0 + n],
                                                start=(p == 0), stop=(p == NP - 1))
                            return last
                        S.op("pe", mmo, reads=[("wo", sl), "Og"], writes=[("psS", pp)])
                        S.op("dve", lambda e, o=o, t0=t0, n=n, pp=pp, gi=gi: e.scalar_tensor_tensor(
                            self.h[:, o, t0:t0 + n], psP[pp][:, 0:n], self.modT[:, 16 + o, gi:gi + 1], self.h[:, o, t0:t0 + n],
                            ALU.mult, ALU.add),
                            reads=[("psS", pp), ("h", o, t), self.mtok], writes=[("h", o, t)])

            load_weights(0)
            if kind == "na":
                build_F(0)
            for g in range(NG):
                if g + 1 < NG:
                    load_weights(g + 1)
                    if kind == "na":
                        build_F(g + 1)
                run_group(g)
            S.flush()

    def lru_phase(self, i):
        nc, S = self.nc, self.S
        with ExitStack() as st0:
            self.norm(st0, self.gs1, 0, 5, f"m{i}")
            S.flush()
        if True:
            with ExitStack() as st:
                sb = lambda name, shape, dt: st.enter_context(nc.sbuf_tensor(f"{name}_{i}", list(shape), dt))
                ps = lambda name: st.enter_context(nc.psum_tensor(f"{name}_{i}", [128, 512], F32))
                vec = sb("lvec", [128, 12, NCH], F32)
                nsp = sb("nsp", [128, 2, NCH], F32)
                win = [sb(f"lwin{x}", [128, NCH, 256], BF16) for x in range(1)] * 2
                Wbds = [sb(f"Wbd{x}", [128, 2, 2, 128], BF16) for x in range(2)]
                A = [sb(f"A{x}", [128, T], F32) for x in range(2)]
                XR = A[1]
                Bd = [sb(f"Bd{x}", [128, T], F32) for x in range(2)]
                xcs = [sb(f"xc{x}", [128, T], F32) for x in range(2)]
                xbs = [sb(f"xb{x}", [128, T], BF16) for x in range(2)]
                gls = [sb(f"gl{x}", [128, S_LAT], BF16) for x in range(2)]
                zcs = [sb(f"zc{x}", [128, S_LAT], BF16) for x in range(2)]
                wos = [sb(f"lwo{x}", [128, D], BF16) for x in range(2)]
                psW = [ps(f"psW{x}") for x in range(2)]
                tm = [sb(f"tm{x}", [128, 512], F32) for x in range(3)]
                psX = [ps(f"psX{x}") for x in range(2)]
                psR = [ps(f"psR{x}") for x in range(2)]
                psI = [ps(f"psI{x}") for x in range(2)]
                S.dma("sp", vec[:], self.I("lru_vecs")[:, :, :], writes=["lvec"])
                S.op("act", lambda e: e.activation(out=nsp[:], in_=vec[:, 9:11, :], func=AF.Exp, scale=-1.0), reads=["lvec"], writes=["nsp"])
                S.op("act", lambda e: e.activation(out=nsp[:], in_=nsp[:], func=AF.Ln, bias=1.0, scale=1.0), reads=["nsp"], writes=["nsp"])
                S.op("dve", lambda e: e.tensor_scalar(nsp[:], nsp[:], -8.0, None, ALU.mult), reads=["nsp"], writes=["nsp"])
                S.op("pool", lambda e: e.memset(Wbds[0][:], 0.0), writes=[("Wbd", 0)])
                S.op("pool", lambda e: e.memset(Wbds[1][:], 0.0), writes=[("Wbd", 1)])
                cw = [0]
                w_in = self.I("lru_w_in")
                wa_d, wx_d = self.I("lru_w_a"), self.I("lru_w_x")
                cx = [0]
                ctm = [0]
                def lru_front(c):
                    wsl = 0
                    cp_ = c % 2
                    xc, xb, gl, zc, wo_c, Wbd = xcs[cp_], xbs[cp_], gls[cp_], zcs[cp_], wos[cp_], Wbds[cp_]
                    S.dma("pool", win[wsl][:, :, 0:128], w_in[:, c * 128:(c + 1) * 128].rearrange("(k p) c -> p k c", p=128), writes=[("lwin", wsl)])
                    S.dma("pool", win[wsl][:, :, 128:256], w_in[:, D + c * 128:D + (c + 1) * 128].rearrange("(k p) c -> p k c", p=128), writes=[("lwin", wsl)])
                    for d in range(2):
                        for gt, wd in enumerate((wa_d, wx_d)):
                            for half in range(2):
                                S.dma("pool", Wbd[half * 64:(half + 1) * 64, d, gt, half * 64:(half + 1) * 64], wd[d, 2 * c + half], writes=[("Wbd", cp_)])
                    for t in range(5):
                        t0, n = TT[t]
                        px = cx[0] % 2
                        cx[0] += 1

                        def mm(e, t0=t0, n=n, px=px, wsl=wsl):
                            last = None
                            for k in range(NCH):
                                last = e.matmul(psX[px][:, 0:n], win[wsl][:, k, 0:128], self.a[:, k, t0:t0 + n], start=(k == 0), stop=(k == NCH - 1))
                            return last
                        S.op("pe", mm, reads=[("lwin", wsl)] + [("a", k, t) for k in range(NCH)], writes=[("psX", px)])
                        S.op("act", lambda e, t0=t0, n=n, px=px: e.activation(out=XR[:, t0:t0 + n], in_=psX[px][:, 0:n], func=AF.Copy),
                             reads=[("psX", px)], writes=[("A", 1)])
                    for t in range(4):
                        t0, n = TT[t]
                        px = cx[0] % 2
                        cx[0] += 1

                        def mmg(e, t0=t0, n=n, px=px, wsl=wsl):
                            last = None
                            for k in range(NCH):
                                last = e.matmul(psX[px][:, 0:n], win[wsl][:, k, 128:256], self.a[:, k, t0:t0 + n], start=(k == 0), stop=(k == NCH - 1))
                            return last
                        S.op("pe", mmg, reads=[("lwin", wsl)] + [("a", k, t) for k in range(NCH)], writes=[("psX", px)])
                        S.op("act", lambda e, t0=t0, n=n, px=px, gl=gl: e.activation(out=gl[:, t0:t0 + n], in_=psX[px][:, 0:n], func=AF.Gelu_apprx_tanh),
                             reads=[("psX", px)], writes=[("gl", cp_)])
                    S.op("act", lambda e, c=c, xc=xc: e.activation(out=xc[:], in_=XR[:], func=AF.Identity, bias=vec[:, 4, c:c + 1], scale=vec[:, 2, c:c + 1]),
                         reads=[("A", 1), "lvec"], writes=[("xc", cp_)])
                    for (kk, sh) in ((0, 2), (1, 1), (3, -1)):
                        for (lo, hi) in ((0, S_LAT), (S_LAT, T)):
                            if sh > 0:
                                o0, o1, i0, i1 = lo + sh, hi, lo, hi - sh
                            else:
                                o0, o1, i0, i1 = lo, hi + sh, lo - sh, hi
                            S.op("dve", lambda e, c=c, kk=kk, o0=o0, o1=o1, i0=i0, i1=i1, xc=xc: e.scalar_tensor_tensor(
                                xc[:, o0:o1], XR[:, i0:i1], vec[:, kk, c:c + 1], xc[:, o0:o1], ALU.mult, ALU.add),
                                reads=[("A", 1), "lvec"], writes=[("xc", cp_)])
                    S.op("act", lambda e, xb=xb, xc=xc: e.activation(out=xb[:], in_=xc[:], func=AF.Copy), reads=[("xc", cp_)], writes=[("xb", cp_)])
                def lru_mid(c):
                    wsl = 0
                    cp_ = c % 2
                    xc, xb, gl, zc, wo_c, Wbd = xcs[cp_], xbs[cp_], gls[cp_], zcs[cp_], wos[cp_], Wbds[cp_]
                    S.dma("pool", wo_c[:], self.I("lru_w_out")[c * 128:(c + 1) * 128, :], writes=[("lwo", cp_)])
                    items = [(d, t) for d in (1, 0) for t in range(5)]
                    slots = {}

                    def s1(j):
                        d, t = items[j]
                        t0, n = TT[t]
                        px = cx[0] % 2
                        cx[0] += 1
                        tx = ctm[0] % 3
                        ctm[0] += 1
                        slots[j] = tx
                        Ad, Bdd = A[d], Bd[d]
                        S.op("pe", lambda e, d=d, t0=t0, n=n, px=px: e.matmul(psR[px][:, 0:n], Wbd[:, d, 0, :], xb[:, t0:t0 + n], start=True, stop=True),
                             reads=[("Wbd", cp_), ("xb", cp_)], writes=[("psR", px)])
                        S.op("pe", lambda e, d=d, t0=t0, n=n, px=px: e.matmul(psI[px][:, 0:n], Wbd[:, d, 1, :], xb[:, t0:t0 + n], start=True, stop=True),
                             reads=[("Wbd", cp_), ("xb", cp_)], writes=[("psI", px)])
                        S.op("act", lambda e, d=d, t0=t0, n=n, px=px, Ad=Ad: e.activation(out=Ad[:, t0:t0 + n], in_=psR[px][:, 0:n], func=AF.Sigmoid,
                                                                                     bias=vec[:, 5 + d, c:c + 1], scale=1.0),
                             reads=[("psR", px), "lvec"], writes=[("A", d), ("At", d, t)])
                        S.op("act", lambda e, d=d, t0=t0, n=n, px=px, Bdd=Bdd: e.activation(out=Bdd[:, t0:t0 + n], in_=psI[px][:, 0:n], func=AF.Sigmoid,
                                                                                       bias=vec[:, 7 + d, c:c + 1], scale=1.0),
                             reads=[("psI", px), "lvec"], writes=[("Bd", d), ("Bt", d, t)])

                    def s2(j):
                        d, t = items[j]
                        t0, n = TT[t]
                        Ad, Bdd = A[d], Bd[d]
                        S.op("act", lambda e, d=d, t0=t0, n=n, Ad=Ad: e.activation(out=Ad[:, t0:t0 + n], in_=Ad[:, t0:t0 + n], func=AF.Exp,
                                                                               scale=nsp[:, d, c:c + 1]),
                             reads=["nsp"], writes=[("At", d, t)])
                        S.op("dve", lambda e, t0=t0, n=n, Bdd=Bdd: e.tensor_tensor(Bdd[:, t0:t0 + n], Bdd[:, t0:t0 + n], xc[:, t0:t0 + n], ALU.mult),
                             reads=[("xc", cp_)], writes=[("Bt", d, t)])

                    def s3(j):
                        d, t = items[j]
                        t0, n = TT[t]
                        tx = slots[j]
                        Ad = A[d]
                        S.op("pool", lambda e, t0=t0, n=n, tx=tx, Ad=Ad: e.tensor_tensor(tm[tx][:, 0:n], Ad[:, t0:t0 + n], Ad[:, t0:t0 + n], ALU.mult),
                             reads=[("At", d, t)], writes=[("tm", tx)])

                    def s4(j):
                        d, t = items[j]
                        t0, n = TT[t]
                        tx = slots[j]
                        S.op("act", lambda e, n=n, tx=tx: e.activation(out=tm[tx][:, 0:n], in_=tm[tx][:, 0:n], func=AF.Sqrt, bias=1.0, scale=-1.0),
                             reads=[], writes=[("tm", tx)])

                    def s5(j):
                        d, t = items[j]
                        t0, n = TT[t]
                        tx = slots[j]
                        Ad, Bdd = A[d], Bd[d]
                        S.op("dve", lambda e, t0=t0, n=n, tx=tx, Bdd=Bdd: e.tensor_tensor(Bdd[:, t0:t0 + n], Bdd[:, t0:t0 + n], tm[tx][:, 0:n], ALU.mult),
                             reads=[("tm", tx)], writes=[("Bt", d, t)])
                        if t == 4:
                            allt = [("At", d, tt_) for tt_ in range(5)] + [("Bt", d, tt_) for tt_ in range(5)]
                            if d == 0:
                                S.op("dve", lambda e, Ad=Ad, Bdd=Bdd: e.tensor_tensor_scan(Bdd[:, S_LAT:T], Ad[:, S_LAT:T], Bdd[:, S_LAT:T], 0.0, ALU.mult, ALU.add),
                                     reads=allt, writes=[("Bd", d), ("A", d)])
                                S.op("dve", lambda e, Ad=Ad, Bdd=Bdd: e.tensor_tensor_scan(Bdd[:, 0:S_LAT], Ad[:, 0:S_LAT], Bdd[:, 0:S_LAT], Bdd[:, T - 1:T], ALU.mult, ALU.add),
                                     reads=allt, writes=[("Bd", d), ("A", d)] + allt)
                            else:
                                S.op("dve", lambda e, Ad=Ad, Bdd=Bdd: e.tensor_tensor_scan(Bdd[:, S_LAT:T][:, ::-1], Ad[:, S_LAT:T][:, ::-1], Bdd[:, S_LAT:T][:, ::-1], 0.0, ALU.mult, ALU.add),
                                     reads=allt, writes=[("Bd", d), ("A", d)])
                                S.op("dve", lambda e, Ad=Ad, Bdd=Bdd: e.tensor_tensor_scan(Bdd[:, 0:S_LAT][:, ::-1], Ad[:, 0:S_LAT][:, ::-1], Bdd[:, 0:S_LAT][:, ::-1], Bdd[:, S_LAT:S_LAT + 1], ALU.mult, ALU.add),
                                     reads=allt, writes=[("Bd", d), ("A", d)] + allt)
                    NI = len(items)
                    for k in range(NI + 4):
                        for stg_, off in ((s1, 0), (s2, 1), (s3, 2), (s4, 3), (s5, 4)):
                            j = k - off
                            if 0 <= j < NI:
                                stg_(j)
                    S.op("pool", lambda e: e.tensor_tensor(Bd[0][:, 0:S_LAT], Bd[0][:, 0:S_LAT], Bd[1][:, 0:S_LAT], ALU.add),
                         reads=[("Bd", 1)], writes=[("Bd", 0)])
                    S.op("dve", lambda e, zc=zc, gl=gl: e.tensor_tensor(zc[:, :], Bd[0][:, 0:S_LAT], gl[:, :], ALU.mult),
                         reads=[("Bd", 0), ("gl", cp_)], writes=[("zc", cp_)])
                def lru_wout(c):
                    wsl = 0
                    cp_ = c % 2
                    xc, xb, gl, zc, wo_c, Wbd = xcs[cp_], xbs[cp_], gls[cp_], zcs[cp_], wos[cp_], Wbds[cp_]
                    for t in range(4):
                        t0, n = TT[t]
                        for o in range(NCH):
                            pw = cw[0] % 2
                            cw[0] += 1
                            S.op("pe", lambda e, o=o, t0=t0, n=n, pw=pw, wo_c=wo_c, zc=zc: e.matmul(
                                psW[pw][:, 0:n], wo_c[:, o * 128:(o + 1) * 128], zc[:, t0:t0 + n], start=True, stop=True),
                                reads=[("lwo", cp_), ("zc", cp_)], writes=[("psW", pw)])
                            S.op("dve", lambda e, o=o, t0=t0, n=n, pw=pw: e.scalar_tensor_tensor(
                                self.h[:, o, t0:t0 + n], psW[pw][:, 0:n], self.modT[:, 16 + o, 0:1], self.h[:, o, t0:t0 + n], ALU.mult, ALU.add),
                                reads=[("psW", pw), ("h", o, t), self.mtok], writes=[("h", o, t)])
                lru_front(0)
                lru_front(1)
                for c in range(NCH):
                    lru_mid(c)
                    if c + 2 < NCH:
                        lru_front(c + 2)
                    lru_wout(c)
                S.flush()

def _fm(v):
    v = np.asarray(v, np.float32)
    lead = v.shape[:-1]
    r = v.reshape(lead + (NCH, 128))
    return np.ascontiguousarray(np.moveaxis(r, -1, 0))


def prep_shared(inp):
    sh = {}
    sh["ada_w"] = np.ascontiguousarray(inp["ada_w"], np.float32)
    sh["ada_bT"] = _fm(inp["ada_b"].reshape(DEPTH, 48, 128).reshape(DEPTH, 48 * 128)) if False else \
        np.ascontiguousarray(np.moveaxis(np.asarray(inp["ada_b"], np.float32).reshape(DEPTH, 48, 128), -1, 0))
    sh["nmT"] = _fm(inp["norm_mix"])
    sh["nfT"] = _fm(inp["norm_ffn"])
    sh["nfinT"] = _fm(inp["norm_final"])
    sh["ffn_w_in"] = np.ascontiguousarray(inp["ffn_w_in"], np.float32)
    sh["ffn_w_out"] = np.ascontiguousarray(inp["ffn_w_out"], np.float32)
    sh["na_w_qkv"] = np.ascontiguousarray(inp["na_w_qkv"][0], np.float32)
    sh["na_w_o"] = np.ascontiguousarray(inp["na_w_o"][0], np.float32)
    sh["nab"] = na_bias_table(inp["na_rpb"][0])
    d = np.arange(64)
    partner = np.where((d // 16) % 2 == 0, d + 16, d - 16)
    for nm_, nq, nkv in (("gqa", 16, 4), ("swa", 16, 2)):
        w = np.ascontiguousarray(inp[nm_ + "_w_qkv"][0], np.float32)
        perm = np.arange(w.shape[1])
        for hd in range(nq + nkv):
            perm[hd * 64:(hd + 1) * 64] = hd * 64 + partner
        sh[nm_ + "_w_qkv"] = w
        sh[nm_ + "_w_qkv_p"] = np.ascontiguousarray(w[:, perm])
        sh[nm_ + "_w_o"] = np.ascontiguousarray(inp[nm_ + "_w_o"][0], np.float32)
    qg_, kg_ = np.asarray(inp["gqa_q_gain"][0], np.float32), np.asarray(inp["gqa_k_gain"][0], np.float32)
    gg = np.stack([qg_, qg_[partner], kg_, kg_[partner]], axis=1)
    sh["gqa_gains"] = np.ascontiguousarray(np.concatenate([gg, gg], axis=0))
    sk = np.asarray(inp["swa_sinks"][0], np.float32)
    sh["swa_sinks"] = np.ascontiguousarray(np.concatenate([np.broadcast_to(sk[0::2][None, :], (64, 8)), np.broadcast_to(sk[1::2][None, :], (64, 8))], axis=0))
    t = np.arange(S_LAT)
    row = (t // GRID_W).astype(np.float32)
    col = (t % GRID_W).astype(np.float32)
    inv_freq = (1.0 / (np.float32(10000.0) ** (np.arange(0, 32, 2, dtype=np.float32) / np.float32(32)))).astype(np.float32)
    ang_row = row[:, None] * inv_freq
    ang_col = col[:, None] * inv_freq
    cs = np.zeros((64, 2, T), np.float32)
    cs[:, 0, :] = 1.0
    for dd in range(64):
        seg, f = dd // 16, dd % 16
        ang = ang_row[:, f] if seg < 2 else ang_col[:, f]
        cs[dd, 0, :S_LAT] = np.cos(ang)
        cs[dd, 1, :S_LAT] = np.sin(ang) * (-1.0 if seg % 2 == 0 else 1.0)
    sh["rope_cs"] = np.ascontiguousarray(np.concatenate([cs, cs], axis=0))
    kk = np.arange(128)[:, None]
    qq = np.arange(128)[None, :]
    vecs = np.zeros((12, D), np.float32)
    vecs[0:4] = inp["lru_conv_w"][0]
    vecs[4] = inp["lru_conv_b"][0]
    vecs[5:7] = inp["lru_b_a"][0]
    vecs[7:9] = inp["lru_b_x"][0]
    vecs[9:11] = inp["lru_lam"][0]
    sh["lru_vecs"] = _fm(vecs)
    sh["lru_w_in"] = np.ascontiguousarray(inp["lru_w_in"][0], np.float32)
    sh["lru_w_a"] = np.ascontiguousarray(inp["lru_w_a"][0], np.float32)
    sh["lru_w_x"] = np.ascontiguousarray(inp["lru_w_x"][0], np.float32)
    sh["lru_w_out"] = np.ascontiguousarray(inp["lru_w_out"][0], np.float32)
    sh["swa_masks"] = np.ascontiguousarray(np.stack([(qq <= kk), (kk <= qq)], axis=1).astype(np.float32))
    return sh


def prep_core(inp, b):
    c = {}
    x = np.asarray(inp["x"][b], np.float32)
    ctx = np.asarray(inp["ctx"][b], np.float32)
    c["hin"] = np.ascontiguousarray(np.concatenate([x, ctx], axis=0).T)
    cT = np.stack([_fm(inp["c"][b]), _fm(inp["c_ctx"])], axis=-1)
    c["cT"] = np.ascontiguousarray(cT)
    return c


def na_bias_table(rpb):
    rpb = np.asarray(rpb, np.float32)
    kc = np.arange(64)[:, None]
    qc = np.arange(64)[None, :]
    dcol = np.clip(kc - qc + 15, 0, 30)
    cstart = np.clip(qc - 8, 0, 48)
    col_in = (kc >= cstart) & (kc < cstart + 16)
    g = rpb[:, :, dcol]
    g = np.where(col_in[None, None], g, np.float32(MASKV))
    return np.ascontiguousarray(np.transpose(g, (0, 2, 1, 3)).astype(np.float32))


_CACHE = {}


def kernel(**inputs):
    inputs = {k: np.asarray(v) for k, v in inputs.items()}
    n_cores = 8
    if "nc" not in _CACHE:
        B = Builder([0, 1, 2, 3], final=True)
        _CACHE["nc"] = (B, B.build())
    B, nc = _CACHE["nc"]
    sh = prep_shared(inputs)
    in_maps = []
    for b in range(n_cores):
        core = prep_core(inputs, b)
        in_maps.append({name: (core[name] if name in core else sh[name]) for name in B.dr if name != "y"})
    res = run_bass_kernel_spmd(nc, in_maps, core_ids=list(range(n_cores)))
    out = np.stack([np.asarray(res.results[b]["y"]).T for b in range(n_cores)], axis=0)
    return np.ascontiguousarray(out.astype(np.float32))
```
